# Optimizing a Trainium2 kernel written in Bass

```python
import math
import jax
import jax.numpy as jnp
from jax import lax
import numpy as np

D_MODEL = 4096
BATCH = 1
SEQ = 8192
DEPTH = 2

N_A_LAYERS = (DEPTH + 1) // 2
N_B_LAYERS = DEPTH // 2

SSM_EXPAND = 2
SSM_D_INNER = SSM_EXPAND * D_MODEL
SSM_HEAD_DIM = 64
SSM_N_HEADS = SSM_D_INNER // SSM_HEAD_DIM
SSM_D_STATE = 128
SSM_N_GROUPS = 8
SSM_CONV_K = 4
SSM_CHUNK = 128
SSM_CONV_CH = SSM_D_INNER + 2 * SSM_N_GROUPS * SSM_D_STATE
SSM_IN_W = SSM_D_INNER + SSM_CONV_CH + SSM_N_HEADS
SSM_DT_MIN = 1e-3
SSM_DT_MAX = 1e-1

ATT_HEAD_DIM = 128
ATT_N_HEADS = D_MODEL // ATT_HEAD_DIM
ATT_N_KV_HEADS = 8
MOBA_BLOCK = 256
MOBA_TOPK = 3
MOBA_Q_BLOCK = 32
ROPE_THETA = 10000.0

FFN_DIM = ((8 * D_MODEL // 3 + 255) // 256) * 256
FFN_CONV_K = 3

NORM_EPS = 1e-5

kernel_name = "yoco_mamba2_moba_convffn"


def rmsnorm(x, g):
    xf = x.astype(jnp.float32)
    y = xf * lax.rsqrt(jnp.mean(xf * xf, axis=-1, keepdims=True) + NORM_EPS)
    return y.astype(x.dtype) * g


def causal_dwconv(x, w, b):
    k = w.shape[0]
    y = lax.conv_general_dilated(
        x, w[:, None, :].astype(x.dtype), window_strides=(1,), padding=[(k - 1, 0)],
        dimension_numbers=('NWC', 'WIO', 'NWC'), feature_group_count=x.shape[-1])
    return y + b


def rope(x, positions):
    half = x.shape[-1] // 2
    inv = ROPE_THETA ** (-jnp.arange(half, dtype=jnp.float32) / half)
    ang = positions.astype(jnp.float32)[:, None] * inv[None, :]
    cos = jnp.cos(ang)[None, :, None, :]
    sin = jnp.sin(ang)[None, :, None, :]
    x1 = x[..., :half].astype(jnp.float32)
    x2 = x[..., half:].astype(jnp.float32)
    return jnp.concatenate([x1 * cos - x2 * sin, x2 * cos + x1 * sin], axis=-1).astype(x.dtype)


def segsum(a):
    t = a.shape[-1]
    cs = jnp.cumsum(a, axis=-1)
    diff = cs[..., :, None] - cs[..., None, :]
    return jnp.where(jnp.tril(jnp.ones((t, t), dtype=bool)), diff, -jnp.inf)


def ssd_chunked(x, a, b, c):
    bsz, s, h, p = x.shape
    g, n = b.shape[2], b.shape[3]
    j = h // g
    l = SSM_CHUNK
    nc = s // l
    xf = x.astype(jnp.float32).reshape(bsz, nc, l, g, j, p)
    bf = b.astype(jnp.float32).reshape(bsz, nc, l, g, n)
    cf = c.astype(jnp.float32).reshape(bsz, nc, l, g, n)
    a = a.astype(jnp.float32).reshape(bsz, nc, l, g, j).transpose(0, 3, 4, 1, 2)
    a_cs = jnp.cumsum(a, axis=-1)
    decay_in = jnp.exp(segsum(a))
    cb = jnp.einsum('bclgn,bcsgn->bgcls', cf, bf)
    y_diag = jnp.einsum('bgcls,bgjcls,bcsgjp->bclgjp', cb, decay_in, xf)
    decay_to_end = jnp.exp(a_cs[..., -1:] - a_cs)
    states = jnp.einsum('bclgn,bgjcl,bclgjp->bcgjpn', bf, decay_to_end, xf)
    chunk_decay = jnp.exp(a_cs[..., -1]).transpose(3, 0, 1, 2)

    def step(hs, inp):
        d, st = inp
        return hs * d[..., None, None] + st, hs

    h0 = jnp.zeros((bsz, g, j, p, n), jnp.float32)
    _, prev = lax.scan(step, h0, (chunk_decay, states.transpose(1, 0, 2, 3, 4, 5)))
    prev = prev.transpose(1, 0, 2, 3, 4, 5)
    decay_from_start = jnp.exp(a_cs)
    y_off = jnp.einsum('bclgn,bcgjpn,bgjcl->bclgjp', cf, prev, decay_from_start)
    return (y_diag + y_off).reshape(bsz, s, h, p)


def mamba2_mixer(u, w_in, conv_w, conv_b, dt_bias, a_log, d_skip, norm_g, w_out):
    bsz, s, _ = u.shape
    di, gn = SSM_D_INNER, SSM_N_GROUPS * SSM_D_STATE
    zxbcdt = u @ w_in
    z = zxbcdt[..., :di]
    xbc = jax.nn.silu(causal_dwconv(zxbcdt[..., di:di + SSM_CONV_CH], conv_w, conv_b))
    dt = zxbcdt[..., di + SSM_CONV_CH:]
    xs = xbc[..., :di].reshape(bsz, s, SSM_N_HEADS, SSM_HEAD_DIM)
    bs = xbc[..., di:di + gn].reshape(bsz, s, SSM_N_GROUPS, SSM_D_STATE)
    cs = xbc[..., di + gn:].reshape(bsz, s, SSM_N_GROUPS, SSM_D_STATE)
    dt = jax.nn.softplus(dt.astype(jnp.float32) + dt_bias.astype(jnp.float32))
    a = -jnp.exp(a_log.astype(jnp.float32))
    y = ssd_chunked(xs * dt[..., None], dt * a, bs, cs)
    y = y + xs.astype(jnp.float32) * d_skip.astype(jnp.float32)[:, None]
    yg = (y.reshape(bsz, s, di) * jax.nn.silu(z.astype(jnp.float32))).reshape(bsz, s, SSM_N_GROUPS, di // SSM_N_GROUPS)
    yg = yg * lax.rsqrt(jnp.mean(yg * yg, axis=-1, keepdims=True) + NORM_EPS)
    y = yg.reshape(bsz, s, di).astype(u.dtype) * norm_g
    return y @ w_out


def conv_ffn(h, w_up, conv_w, conv_b, w_down):
    u = causal_dwconv(h @ w_up, conv_w, conv_b)
    gate, val = jnp.split(u, 2, axis=-1)
    return (jax.nn.silu(gate) * val) @ w_down


def shared_kv(hs, kv_norm, w_kv, positions):
    bsz, s, _ = hs.shape
    kvd = ATT_N_KV_HEADS * ATT_HEAD_DIM
    kv = rmsnorm(hs, kv_norm) @ w_kv
    k = rope(kv[..., :kvd].reshape(bsz, s, ATT_N_KV_HEADS, ATT_HEAD_DIM), positions)
    v = kv[..., kvd:].reshape(bsz, s, ATT_N_KV_HEADS, ATT_HEAD_DIM)
    nb = -(-s // MOBA_BLOCK)
    pad = nb * MOBA_BLOCK - s
    k = jnp.pad(k, ((0, 0), (0, pad), (0, 0), (0, 0)))
    v = jnp.pad(v, ((0, 0), (0, pad), (0, 0), (0, 0)))
    k_blk = k.reshape(bsz, nb, MOBA_BLOCK, ATT_N_KV_HEADS, ATT_HEAD_DIM).transpose(0, 3, 1, 2, 4)
    v_blk = v.reshape(bsz, nb, MOBA_BLOCK, ATT_N_KV_HEADS, ATT_HEAD_DIM).transpose(0, 3, 1, 2, 4)
    k_mean = jnp.mean(k_blk.astype(jnp.float32), axis=3).astype(k_blk.dtype)
    return k_blk, v_blk, k_mean


def moba_attention(q, k_blk, v_blk, k_mean):
    bsz, s, h, dh = q.shape
    kvh, nb = k_blk.shape[1], k_blk.shape[2]
    topk = min(MOBA_TOPK, nb)
    nq = s // MOBA_Q_BLOCK
    kv_of_h = jnp.arange(h) // (h // kvh)
    k_mean_h = k_mean[:, kv_of_h]
    scale = dh ** -0.5
    b_idx = jnp.arange(bsz)[:, None, None, None]
    h_idx = kv_of_h[None, None, :, None]

    def one_block(args):
        q_c, qi = args
        q_start = qi * MOBA_Q_BLOCK
        blk = q_start // MOBA_BLOCK
        q_pos = q_start + jnp.arange(MOBA_Q_BLOCK)
        gate = jnp.einsum('bqhd,bhnd->bqhn', q_c, k_mean_h).astype(jnp.float32)
        gate = jnp.where(jnp.arange(nb) < blk, gate, -jnp.inf)
        _, sel = lax.top_k(gate, topk)
        sel_valid = sel < blk
        k_sel = k_blk[b_idx, h_idx, sel]
        v_sel = v_blk[b_idx, h_idx, sel]
        s_sel = jnp.einsum('bqhd,bqhtkd->bqhtk', q_c, k_sel).astype(jnp.float32) * scale
        s_sel = jnp.where(sel_valid[..., None], s_sel, -jnp.inf)
        k_own = lax.dynamic_index_in_dim(k_blk, blk, axis=2, keepdims=False)[:, kv_of_h]
        v_own = lax.dynamic_index_in_dim(v_blk, blk, axis=2, keepdims=False)[:, kv_of_h]
        s_own = jnp.einsum('bqhd,bhkd->bqhk', q_c, k_own).astype(jnp.float32) * scale
        k_pos = blk * MOBA_BLOCK + jnp.arange(MOBA_BLOCK)
        causal = k_pos[None, :] <= q_pos[:, None]
        s_own = jnp.where(causal[None, :, None, :], s_own, -jnp.inf)
        sc = jnp.concatenate([s_sel.reshape(bsz, MOBA_Q_BLOCK, h, topk * MOBA_BLOCK), s_own], axis=-1)
        p = jax.nn.softmax(sc, axis=-1).astype(q_c.dtype)
        p_sel = p[..., :topk * MOBA_BLOCK].reshape(bsz, MOBA_Q_BLOCK, h, topk, MOBA_BLOCK)
        p_own = p[..., topk * MOBA_BLOCK:]
        return (jnp.einsum('bqhtk,bqhtkd->bqhd', p_sel, v_sel)
                + jnp.einsum('bqhk,bhkd->bqhd', p_own, v_own))

    q_blocks = q.reshape(bsz, nq, MOBA_Q_BLOCK, h, dh).transpose(1, 0, 2, 3, 4)
    out = lax.map(one_block, (q_blocks, jnp.arange(nq)))
    return out.transpose(1, 0, 2, 3, 4).reshape(bsz, s, h, dh)


def setup_inputs(seed: int = 0) -> dict:
    key = jax.random.key(seed)
    ks = jax.random.split(key, 24)
    f32 = jnp.float32

    def nrm(k, shape, scale):
        return jax.random.normal(k, shape, f32) * scale

    def gain(k, shape):
        return 1.0 + 0.02 * jax.random.normal(k, shape, f32)

    na, nbl = N_A_LAYERS, N_B_LAYERS
    dt = jnp.exp(jax.random.uniform(ks[6], (na, SSM_N_HEADS), f32)
                 * (math.log(SSM_DT_MAX) - math.log(SSM_DT_MIN)) + math.log(SSM_DT_MIN))
    dt_bias = dt + jnp.log(-jnp.expm1(-dt))
    a_log = jnp.log(jax.random.uniform(ks[7], (na, SSM_N_HEADS), f32, minval=1.0, maxval=16.0))
    return {
        'x': jax.random.normal(ks[0], (BATCH, SEQ, D_MODEL), f32),
        'norm_a': gain(ks[1], (na, D_MODEL)),
        'mamba_w_in': nrm(ks[2], (na, D_MODEL, SSM_IN_W), D_MODEL ** -0.5),
        'mamba_conv_w': nrm(ks[3], (na, SSM_CONV_K, SSM_CONV_CH), SSM_CONV_K ** -0.5),
        'mamba_conv_b': nrm(ks[4], (na, SSM_CONV_CH), 0.02),
        'mamba_dt_bias': dt_bias,
        'mamba_a_log': a_log,
        'mamba_d': 1.0 + 0.1 * jax.random.normal(ks[8], (na, SSM_N_HEADS), f32),
        'mamba_norm': gain(ks[9], (na, SSM_D_INNER)),
        'mamba_w_out': nrm(ks[10], (na, SSM_D_INNER, D_MODEL), SSM_D_INNER ** -0.5),
        'kv_norm': gain(ks[11], (D_MODEL,)),
        'w_kv': nrm(ks[12], (D_MODEL, 2 * ATT_N_KV_HEADS * ATT_HEAD_DIM), D_MODEL ** -0.5),
        'norm_b': gain(ks[13], (nbl, D_MODEL)),
        'w_q': nrm(ks[14], (nbl, D_MODEL, ATT_N_HEADS * ATT_HEAD_DIM), D_MODEL ** -0.5),
        'w_o': nrm(ks[15], (nbl, ATT_N_HEADS * ATT_HEAD_DIM, D_MODEL), (ATT_N_HEADS * ATT_HEAD_DIM) ** -0.5),
        'ffn_norm': gain(ks[16], (DEPTH, D_MODEL)),
        'ffn_w_up': nrm(ks[17], (DEPTH, D_MODEL, 2 * FFN_DIM), D_MODEL ** -0.5),
        'ffn_conv_w': nrm(ks[18], (DEPTH, FFN_CONV_K, 2 * FFN_DIM), FFN_CONV_K ** -0.5),
        'ffn_conv_b': nrm(ks[19], (DEPTH, 2 * FFN_DIM), 0.02),
        'ffn_w_down': nrm(ks[20], (DEPTH, FFN_DIM, D_MODEL), FFN_DIM ** -0.5),
        'final_norm': gain(ks[21], (D_MODEL,)),
    }


def reference(x, norm_a, mamba_w_in, mamba_conv_w, mamba_conv_b, mamba_dt_bias, mamba_a_log,
              mamba_d, mamba_norm, mamba_w_out, kv_norm, w_kv, norm_b, w_q, w_o,
              ffn_norm, ffn_w_up, ffn_conv_w, ffn_conv_b, ffn_w_down, final_norm):
    bsz, s, _ = x.shape
    positions = jnp.arange(s, dtype=jnp.int32)
    h = x
    for i in range(DEPTH):
        if i < N_A_LAYERS:
            h = h + mamba2_mixer(rmsnorm(h, norm_a[i]), mamba_w_in[i], mamba_conv_w[i], mamba_conv_b[i],
                                 mamba_dt_bias[i], mamba_a_log[i], mamba_d[i], mamba_norm[i], mamba_w_out[i])
        else:
            j = i - N_A_LAYERS
            if j == 0:
                k_blk, v_blk, k_mean = shared_kv(h, kv_norm, w_kv, positions)
            q = (rmsnorm(h, norm_b[j]) @ w_q[j]).reshape(bsz, s, ATT_N_HEADS, ATT_HEAD_DIM)
            q = rope(q, positions)
            o = moba_attention(q, k_blk, v_blk, k_mean).reshape(bsz, s, ATT_N_HEADS * ATT_HEAD_DIM)
            h = h + o @ w_o[j]
        h = h + conv_ffn(rmsnorm(h, ffn_norm[i]), ffn_w_up[i], ffn_conv_w[i], ffn_conv_b[i], ffn_w_down[i])
    return rmsnorm(h, final_norm)
```

```python
import numpy as np
import ml_dtypes
import concourse.bass as bass
import concourse.mybir as mybir
from concourse.bass_utils import run_bass_kernel_spmd

F32 = mybir.dt.float32
F32R = mybir.dt.float32r
BF16 = mybir.dt.bfloat16
AF = mybir.ActivationFunctionType
ALU = mybir.AluOpType
AX = mybir.AxisListType


class Buf:
    __slots__ = ("t", "name", "last_w", "readers", "dsem", "dcnt")

    def __init__(self, t, name):
        self.t = t
        self.name = name
        self.last_w = None
        self.readers = []
        self.dsem = None
        self.dcnt = 0

    def __getitem__(self, k):
        return self.t[k]


class _Rec:
    def __init__(self):
        self.call = None

    def __getattr__(self, name):
        def f(*a, **k):
            self.call = (name, a, k)
            return None
        return f


class Op:
    __slots__ = ("call", "eng", "fn", "reads", "writes", "is_dma", "owner", "deps", "need_inc", "semval", "idx", "is_final", "raw")


class Prog:
    ENGS = ("pe", "act", "dve", "pool", "sp")

    def __init__(self, nc, same_engine_sync=True):
        self.nc = nc
        self.eng = {"pe": nc.tensor, "act": nc.scalar, "dve": nc.vector, "pool": nc.gpsimd, "sp": nc.sync}
        self.ops = []
        self.same_engine_sync = same_engine_sync
        self.nbuf = 0

    def sbuf(self, shape, dtype, name=None):
        self.nbuf += 1
        name = "s_" + (name or f"sb{self.nbuf}")
        return Buf(self.nc.alloc_sbuf_tensor(name, list(shape), dtype), name)

    def psum(self, shape, dtype=F32, name=None):
        self.nbuf += 1
        name = "p_" + (name or f"ps{self.nbuf}")
        return Buf(self.nc.alloc_psum_tensor(name, list(shape), dtype), name)

    def dram(self, name, shape, dtype, kind="Internal"):
        return Buf(self.nc.dram_tensor(name, list(shape), dtype, kind=kind), name)

    def op(self, eng, fn, reads=(), writes=()):
        o = Op()
        o.eng = eng
        o.fn = None
        r = _Rec()
        fn(r)
        o.call = r.call
        o.reads = tuple(reads)
        o.writes = tuple(writes)
        o.is_dma = False
        o.owner = None
        o.need_inc = False
        o.is_final = False
        o.idx = len(self.ops)
        self.ops.append(o)
        return o

    def dma(self, queue, out_ap, in_ap, reads=(), writes=(), owner=None, **kw):
        o = self.op(queue, lambda e: e.dma_start(out=out_ap, in_=in_ap, **kw), reads, writes)
        o.is_dma = True
        o.owner = owner if owner is not None else (writes[0] if writes else reads[0])
        return o

    def emit(self):
        nc = self.nc
        ops = self.ops
        for o in ops:
            deps = set()
            raw = set()
            for b in o.reads:
                if b.last_w is not None:
                    deps.add(b.last_w)
                    raw.add(b.last_w)
            o.raw = raw
            for b in o.writes:
                if b.last_w is not None:
                    deps.add(b.last_w)
                deps.update(b.readers)
            for b in o.reads:
                if (not o.is_dma) and b.readers:
                    q = ops[b.readers[-1]]
                    if (not q.is_dma) and q.eng == o.eng:
                        b.readers[-1] = o.idx
                        continue
                b.readers.append(o.idx)
            for b in o.writes:
                b.last_w = o.idx
                b.readers = []
            deps.discard(o.idx)
            if o.is_final:
                last = {}
                for q in ops[:o.idx]:
                    if q.is_dma:
                        deps.add(q.idx)
                    else:
                        last[q.eng] = q.idx
                deps.update(last.values())
            o.deps = deps
            for d in deps:
                p = ops[d]
                if p.is_dma:
                    continue
                if p.eng != o.eng or (self.same_engine_sync and o.eng != "pe" and d in o.raw):
                    p.need_inc = True
                if p.eng == o.eng and o.is_dma:
                    p.need_inc = True
        esem = {e: nc.alloc_semaphore(f"es_{e}") for e in self.ENGS}
        ecnt = {e: 0 for e in self.ENGS}
        for o in ops:
            if o.is_dma:
                b = o.owner
                if b.dsem is None:
                    b.dsem = nc.alloc_semaphore(f"ds_{b.name}")
                b.dcnt += 16
                o.semval = (b.dsem, b.dcnt)
            elif o.need_inc:
                ecnt[o.eng] += 1
                o.semval = (esem[o.eng], ecnt[o.eng])
            else:
                o.semval = None
        waited = {e: {} for e in self.ENGS}
        nwaits = 0
        for o in ops:
            e = self.eng[o.eng]
            w = waited[o.eng]
            need = {}
            for d in o.deps:
                p = ops[d]
                if p.semval is None:
                    continue
                if (not p.is_dma) and p.eng == o.eng and not (o.is_dma or (self.same_engine_sync and o.eng != "pe" and d in o.raw)):
                    continue
                s, v = p.semval
                key = id(s)
                if w.get(key, 0) >= v:
                    continue
                if key not in need or need[key][1] < v:
                    need[key] = (s, v)
            for key, (s, v) in need.items():
                e.wait_ge(s, v)
                w[key] = v
                nwaits += 1
            ins = None
            if o.call is not None:
                nm, a_, k_ = o.call
                ins = getattr(e, nm)(*a_, **k_)
            if o.semval is not None and ins is not None:
                ins.then_inc(o.semval[0], 16 if o.is_dma else 1)
        self.final = {"esem": esem, "ecnt": ecnt, "nwaits": nwaits}
        return self.final

    def finish(self, eng="sp"):
        o = self.op(eng, lambda e: None)
        o.is_final = True


D = 4096
DT = 32
FF = 11008
FT = 86
TP = 512
EPS = 1e-5
WU = 43 * 128


class WStream:
    def __init__(self, P, nbuf, unit_cols, queue="sp", pf=3):
        self.P = P
        self.bufs = [P.sbuf([128, unit_cols], BF16, f"wp{i}") for i in range(nbuf)]
        self.reqs = []
        self.issued = 0
        self.taken = 0
        self.queue = queue
        self.pf = pf

    def add(self, dbuf, ap, ncols):
        self.reqs.append((dbuf, ap, ncols))

    def _issue(self, i):
        dbuf, ap, ncols = self.reqs[i]
        b = self.bufs[i % len(self.bufs)]
        self.P.dma(self.queue, b[:, 0:ncols], ap, reads=[dbuf], writes=[b])

    def next(self):
        i = self.taken
        lim = min(len(self.reqs), i + self.pf + 1)
        while self.issued < lim:
            self._issue(self.issued)
            self.issued += 1
        self.taken += 1
        return self.bufs[i % len(self.bufs)]


def build_tok(mode, KTIN):
    nc = bass.Bass("TRN2", target_bir_lowering=False)
    P = Prog(nc)
    NTOK = 1026
    inT = P.dram("inT", [128, KTIN, NTOK], BF16, kind="ExternalInput")
    hT = P.dram("hT", [128, DT, NTOK], F32, kind="ExternalInput")
    wp_d = P.dram("wp", [DT, 128, KTIN, 128], BF16, kind="ExternalInput")
    wu_d = P.dram("wu", [FT, 128, 2, DT, 128], BF16, kind="ExternalInput")
    wd_d = P.dram("wd", [DT, 128, FT, 128], BF16, kind="ExternalInput")
    par_d = P.dram("par", [128, 4 * DT + 172 * 4], F32, kind="ExternalInput")
    if mode == "A":
        wq_d = P.dram("wq", [12, 128, DT, 512], BF16, kind="ExternalInput")
        cs_d = P.dram("cs", [128, 8, 128], F32, kind="ExternalInput")
        q_o = P.dram("q_o", [1024, 4096], BF16, kind="ExternalOutput")
        kv_o = P.dram("kv_o", [1024, 2048], BF16, kind="ExternalOutput")
    hoT = P.dram("hoT", [128, DT, 1024], F32, kind="ExternalOutput")
    hmid_d = P.dram("hmid", [128, DT, 1024], F32)
    if mode == "B":
        hout_d = P.dram("hout", [128, DT, 1024], F32)
    else:
        hout_d = hoT
    hmid_v = [Buf(hmid_d.t, f"hmid{i}") for i in range(DT)]
    hout_v = [Buf(hout_d.t, f"hout{i}") for i in range(DT)]

    par = P.sbuf([128, 4 * DT + 172 * 4], F32, "par")
    G_FFN, G_2, G_3, CW, CB = 0, DT, 2 * DT, 4 * DT, 4 * DT + 172 * 3
    big = P.sbuf([128, FT * TP], BF16, "big")
    hn = P.sbuf([128, DT * TP], BF16, "hn")
    ones = P.sbuf([128, 128], BF16, "ones")
    rstd = P.sbuf([128, TP], F32, "rstd")
    tails = P.sbuf([128, 172 * 2], F32, "tails")
    ws = WStream(P, 4, WU, "sp", pf=3)
    stg = [P.sbuf([128, TP], F32, f"stg{i}") for i in range(3)]
    sqb = [P.sbuf([128, TP], BF16, f"sq{i}") for i in range(2)]
    ub = [P.sbuf([128, 2 * (TP + 2)], F32, f"ub{i}") for i in range(2)]
    cv = [P.sbuf([128, 2 * TP], F32, f"cv{i}") for i in range(2)]
    ps = [P.psum([128, 512], F32, f"psb{i}") for i in range(8)]
    PS_MM = (ps[0], ps[1])
    PS_UP = ((ps[2], ps[3]), (ps[4], ps[5]))
    PS_SS = ps[6]
    PS_TM = ps[7]
    if mode == "A":
        cs = P.sbuf([128, 8 * 128], F32, "cs")
        rs_tm = P.sbuf([128, 4], F32, "rs_tm")
        qst = cv
        qrt = ub
        qob = [P.sbuf([128, 512], BF16, f"qob{i}") for i in range(2)]
        onecol = P.sbuf([128, 1], F32, "onecol")

    P.dma("pool", par[:], par_d[:], reads=[par_d], writes=[par])
    P.op("pool", lambda e: e.memset(ones[:], 1.0), writes=[ones])
    P.op("pool", lambda e: e.memset(tails[:], 0.0), writes=[tails])
    if mode == "A":
        P.dma("pool", cs[:], cs_d[:].rearrange("p a b -> p (a b)"), reads=[cs_d], writes=[cs])
        P.op("pool", lambda e: e.memset(onecol[:], 1.0), writes=[onecol])

    passes = [(0, 2, True), (2, TP, False), (2 + TP, TP, False)]
    for (c0, n, halo) in passes:
        for dt in range(DT):
            for h0 in range(0, KTIN, 32):
                ws.add(wp_d, wp_d[dt, :, h0:h0 + 32, :].rearrange("p a b -> p (a b)"), 32 * 128)
        for j in range(FT):
            for half in range(2):
                ws.add(wu_d, wu_d[j, :, half].rearrange("p b c -> p (b c)"), DT * 128)
        if halo:
            continue
        for dt in range(DT):
            for h0 in range(0, FT, 43):
                ws.add(wd_d, wd_d[dt, :, h0:h0 + 43, :].rearrange("p a b -> p (a b)"), 43 * 128)
        if mode == "A":
            for ch in range(12):
                for k0 in range(0, DT, 8):
                    ws.add(wq_d, wq_d[ch, :, k0:k0 + 8, :].rearrange("p a b -> p (a b)"), 8 * 512)

    cnt = {"mm": 0, "up": 0, "stg": 0, "sq": 0, "q": 0}

    def rstd_from(ssq_ps, out_ap, n, eng_cols):
        P.op("act", lambda e: e.activation(out_ap, ssq_ps[:, 0:n], AF.Sqrt, bias=epsb[:, 0:1], scale=1.0 / D),
             reads=[ssq_ps, epsb], writes=[rstd])
        P.op("dve", lambda e: e.reciprocal(out_ap, out_ap), reads=[rstd], writes=[rstd])

    epsb = P.sbuf([128, 1], F32, "epsb")
    P.op("pool", lambda e: e.memset(epsb[:], EPS), writes=[epsb])

    for pi, (c0, n, halo) in enumerate(passes):
        o0 = c0 - 2
        P.dma("sp", big[:, 0:KTIN * TP].rearrange("p (k t) -> p k t", t=TP)[:, :, 0:n], inT[:, :, c0:c0 + n],
              reads=[inT], writes=[big])
        for dt in range(DT):
            pb = PS_MM[cnt["mm"] % 2]; cnt["mm"] += 1
            for h0 in range(0, KTIN, 32):
                w = ws.next()
                for k in range(32):
                    kk = h0 + k
                    P.op("pe", (lambda e, w=w, k=k, kk=kk, pb=pb: e.matmul(
                        pb[:, 0:n], w[:, k * 128:(k + 1) * 128], big[:, kk * TP:kk * TP + n],
                        start=(kk == 0), stop=(kk == KTIN - 1))), reads=[w, big], writes=[pb])
            sg = stg[cnt["stg"] % 3]; cnt["stg"] += 1
            P.dma("pool", sg[:, 0:n], hT[:, dt, c0:c0 + n], reads=[hT], writes=[sg])
            P.op("dve", (lambda e, sg=sg, pb=pb: e.tensor_tensor(sg[:, 0:n], pb[:, 0:n], sg[:, 0:n], ALU.add)),
                 reads=[pb, sg], writes=[sg])
            if not halo:
                P.dma("pool", hmid_d[:, dt, o0:o0 + n], sg[:, 0:n], reads=[sg], writes=[hmid_v[dt]], owner=sg)
            sq = sqb[cnt["sq"] % 2]; cnt["sq"] += 1
            P.op("act", (lambda e, sg=sg, sq=sq: e.activation(sq[:, 0:n], sg[:, 0:n], AF.Square)), reads=[sg], writes=[sq])
            P.op("pe", (lambda e, sq=sq, dt=dt: e.matmul(PS_SS[:, 0:n], ones[:], sq[:, 0:n], start=(dt == 0), stop=(dt == DT - 1))),
                 reads=[ones, sq], writes=[PS_SS])
            P.op("dve", (lambda e, sg=sg, dt=dt: e.tensor_scalar(hn[:, dt * TP:dt * TP + n], sg[:, 0:n],
                                                              par[:, G_FFN + dt:G_FFN + dt + 1], None, ALU.mult)),
                 reads=[sg, par], writes=[hn])
        rstd_from(PS_SS, rstd[:, 0:n], n, 128)
        for j in range(FT):
            pg, pv = PS_UP[cnt["up"] % 2]
            u = ub[cnt["up"] % 2]
            c = cv[cnt["up"] % 2]
            cnt["up"] += 1
            for half, pb in ((0, pg), (1, pv)):
                w = ws.next()
                for k in range(DT):
                    P.op("pe", (lambda e, w=w, k=k, half=half, pb=pb: e.matmul(
                        pb[:, 0:n], w[:, k * 128:(k + 1) * 128], hn[:, k * TP:k * TP + n],
                        start=(k == 0), stop=(k == DT - 1))), reads=[w, hn], writes=[pb])
            ug = u[:, 0:TP + 2]
            uv = u[:, TP + 2:2 * TP + 4]
            tg = j
            tv = FT + j
            P.op("dve", (lambda e, pg=pg, ug=ug: e.tensor_tensor(ug[:, 2:2 + n], pg[:, 0:n], rstd[:, 0:n], ALU.mult)),
                 reads=[pg, rstd], writes=[u])
            P.op("dve", (lambda e, pv=pv, uv=uv: e.tensor_tensor(uv[:, 2:2 + n], pv[:, 0:n], rstd[:, 0:n], ALU.mult)),
                 reads=[pv, rstd], writes=[u])
            if halo:
                P.op("pool", (lambda e, ug=ug, tg=tg: e.tensor_copy(tails[:, 2 * tg:2 * tg + 2], ug[:, 2:4])), reads=[u], writes=[tails])
                P.op("pool", (lambda e, uv=uv, tv=tv: e.tensor_copy(tails[:, 2 * tv:2 * tv + 2], uv[:, 2:4])), reads=[u], writes=[tails])
                continue
            P.op("pool", (lambda e, ug=ug, tg=tg: e.tensor_copy(ug[:, 0:2], tails[:, 2 * tg:2 * tg + 2])), reads=[tails], writes=[u])
            P.op("pool", (lambda e, uv=uv, tv=tv: e.tensor_copy(uv[:, 0:2], tails[:, 2 * tv:2 * tv + 2])), reads=[tails], writes=[u])
            P.op("pool", (lambda e, ug=ug, tg=tg: e.tensor_copy(tails[:, 2 * tg:2 * tg + 2], ug[:, n:n + 2])), reads=[u], writes=[tails])
            P.op("pool", (lambda e, uv=uv, tv=tv: e.tensor_copy(tails[:, 2 * tv:2 * tv + 2], uv[:, n:n + 2])), reads=[u], writes=[tails])
            cg = c[:, 0:TP]
            cvv = c[:, TP:2 * TP]
            for (uu, cc, t) in ((ug, cg, tg), (uv, cvv, tv)):
                P.op("dve", (lambda e, uu=uu, cc=cc, t=t: e.tensor_scalar(
                    cc[:, 0:n], uu[:, 0:n], par[:, CW + 3 * t:CW + 3 * t + 1], par[:, CB + t:CB + t + 1], ALU.mult, ALU.add)),
                    reads=[u, par], writes=[c])
                for kk in (1, 2):
                    P.op("dve", (lambda e, uu=uu, cc=cc, t=t, kk=kk: e.scalar_tensor_tensor(
                        cc[:, 0:n], uu[:, kk:kk + n], par[:, CW + 3 * t + kk:CW + 3 * t + kk + 1], cc[:, 0:n], ALU.mult, ALU.add)),
                        reads=[u, par, c], writes=[c])
            P.op("act", (lambda e, cg=cg: e.activation(cg[:, 0:n], cg[:, 0:n], AF.Silu)), reads=[c], writes=[c])
            P.op("dve", (lambda e, cg=cg, cvv=cvv, j=j: e.tensor_tensor(big[:, j * TP:j * TP + n], cg[:, 0:n], cvv[:, 0:n], ALU.mult)),
                 reads=[c], writes=[big])
        if halo:
            continue
        for dt in range(DT):
            pb = PS_MM[cnt["mm"] % 2]; cnt["mm"] += 1
            for h0 in range(0, FT, 43):
                w = ws.next()
                for k in range(43):
                    kk = h0 + k
                    P.op("pe", (lambda e, w=w, k=k, kk=kk, pb=pb: e.matmul(
                        pb[:, 0:n], w[:, k * 128:(k + 1) * 128], big[:, kk * TP:kk * TP + n],
                        start=(kk == 0), stop=(kk == FT - 1))), reads=[w, big], writes=[pb])
            sg = stg[cnt["stg"] % 3]; cnt["stg"] += 1
            P.dma("pool", sg[:, 0:n], hmid_d[:, dt, o0:o0 + n], reads=[hmid_v[dt]], writes=[sg])
            P.op("dve", (lambda e, sg=sg, pb=pb: e.tensor_tensor(sg[:, 0:n], pb[:, 0:n], sg[:, 0:n], ALU.add)),
                 reads=[pb, sg], writes=[sg])
            P.dma("pool", hout_d[:, dt, o0:o0 + n], sg[:, 0:n], reads=[sg], writes=[hout_v[dt]], owner=sg)
            sq = sqb[cnt["sq"] % 2]; cnt["sq"] += 1
            P.op("act", (lambda e, sg=sg, sq=sq: e.activation(sq[:, 0:n], sg[:, 0:n], AF.Square)), reads=[sg], writes=[sq])
            P.op("pe", (lambda e, sq=sq, dt=dt: e.matmul(PS_SS[:, 0:n], ones[:], sq[:, 0:n], start=(dt == 0), stop=(dt == DT - 1))),
                 reads=[ones, sq], writes=[PS_SS])
            if mode == "A":
                P.op("dve", (lambda e, sg=sg, dt=dt: e.tensor_scalar(hn[:, dt * TP:dt * TP + n], sg[:, 0:n],
                                                                  par[:, G_2 + dt:G_2 + dt + 1], None, ALU.mult)),
                     reads=[sg, par], writes=[hn])
        rstd_from(PS_SS, rstd[:, 0:n], n, 128)
        if mode == "B":
            for dt in range(DT):
                sg = stg[cnt["stg"] % 3]; cnt["stg"] += 1
                P.dma("pool", sg[:, 0:n], hout_d[:, dt, o0:o0 + n], reads=[hout_v[dt]], writes=[sg])
                P.op("dve", (lambda e, sg=sg, dt=dt: e.scalar_tensor_tensor(sg[:, 0:n], sg[:, 0:n], par[:, G_2 + dt:G_2 + dt + 1],
                                                                         rstd[:, 0:n], ALU.mult, ALU.mult)),
                     reads=[sg, par, rstd], writes=[sg])
                P.dma("pool", hoT[:, dt, o0:o0 + n], sg[:, 0:n], reads=[sg], writes=[], owner=sg)
        else:
            for tt in range(4):
                P.op("pe", (lambda e, tt=tt: e.matmul(PS_TM[:, tt:tt + 1], rstd[0:1, tt * 128:(tt + 1) * 128], onecol[0:1, 0:1],
                                                      start=True, stop=True)), reads=[rstd, onecol], writes=[PS_TM])
            P.op("act", lambda e: e.copy(rs_tm[:, 0:4], PS_TM[:, 0:4]), reads=[PS_TM], writes=[rs_tm])
            for dt in range(DT):
                sg = stg[cnt["stg"] % 3]; cnt["stg"] += 1
                P.dma("pool", sg[:, 0:n], hout_d[:, dt, o0:o0 + n], reads=[hout_v[dt]], writes=[sg])
                P.op("dve", (lambda e, sg=sg, dt=dt: e.tensor_scalar(big[:, dt * TP:dt * TP + n], sg[:, 0:n],
                                                                  par[:, G_3 + dt:G_3 + dt + 1], None, ALU.mult)),
                     reads=[sg, par], writes=[big])
            for ch in range(12):
                src = big if ch < 8 else hn
                wl = [ws.next() for _ in range(4)] if False else None
                for tt in range(4):
                    pass
                pbs = [ps[0], ps[1], ps[2], ps[3]]
                for piece in range(4):
                    w = ws.next()
                    for tt in range(4):
                        for k in range(8):
                            kk = piece * 8 + k
                            P.op("pe", (lambda e, w=w, k=k, kk=kk, tt=tt, src=src: e.matmul(
                                pbs[tt][:, 0:512], src[:, kk * TP + tt * 128:kk * TP + (tt + 1) * 128], w[:, k * 512:(k + 1) * 512],
                                start=(kk == 0), stop=(kk == DT - 1))), reads=[w, src], writes=[pbs[tt]])
                for tt in range(4):
                    tile_idx = (o0 // 128) + tt
                    qs = qst[cnt["q"] % 2]; qr = qrt[cnt["q"] % 2]; qo = qob[cnt["q"] % 2]; cnt["q"] += 1
                    rope = ch < 10
                    if rope:
                        P.op("act", (lambda e, qs=qs, tt=tt: e.activation(qs[:, 0:512], pbs[tt][:, :], AF.Copy, scale=rs_tm[:, tt:tt + 1])),
                             reads=[pbs[tt], rs_tm], writes=[qs])
                        q3 = qs[:, 0:512].rearrange("p (h two d) -> p h two d", two=2, d=64)
                        r3 = qr[:, 0:512].rearrange("p (h two d) -> p h two d", two=2, d=64)
                        o3 = qo[:, :].rearrange("p (h two d) -> p h two d", two=2, d=64)
                        cosb = cs[:, tile_idx * 128:tile_idx * 128 + 64]
                        sinb = cs[:, tile_idx * 128 + 64:tile_idx * 128 + 128]
                        for hh in range(4):
                            P.op("pool", (lambda e, hh=hh, q3=q3, r3=r3, sinb=sinb: e.tensor_tensor(r3[:, hh, 0, :], q3[:, hh, 1, :], sinb, ALU.mult)),
                                 reads=[qs, cs], writes=[qr])
                            P.op("pool", (lambda e, hh=hh, q3=q3, r3=r3, sinb=sinb: e.tensor_tensor(r3[:, hh, 1, :], q3[:, hh, 0, :], sinb, ALU.mult)),
                                 reads=[qs, cs], writes=[qr])
                            P.op("dve", (lambda e, hh=hh, q3=q3, cosb=cosb: e.tensor_tensor(q3[:, hh, 0, :], q3[:, hh, 0, :], cosb, ALU.mult)),
                                 reads=[qs, cs], writes=[qs])
                            P.op("dve", (lambda e, hh=hh, q3=q3, cosb=cosb: e.tensor_tensor(q3[:, hh, 1, :], q3[:, hh, 1, :], cosb, ALU.mult)),
                                 reads=[qs, cs], writes=[qs])
                            P.op("dve", (lambda e, hh=hh, q3=q3, r3=r3, o3=o3: e.tensor_tensor(o3[:, hh, 0, :], q3[:, hh, 0, :], r3[:, hh, 0, :], ALU.subtract)),
                                 reads=[qs, qr], writes=[qo])
                            P.op("dve", (lambda e, hh=hh, q3=q3, r3=r3, o3=o3: e.tensor_tensor(o3[:, hh, 1, :], q3[:, hh, 1, :], r3[:, hh, 1, :], ALU.add)),
                                 reads=[qs, qr], writes=[qo])
                    else:
                        P.op("act", (lambda e, qo=qo, tt=tt: e.activation(qo[:, :], pbs[tt][:, :], AF.Copy, scale=rs_tm[:, tt:tt + 1])),
                             reads=[pbs[tt], rs_tm], writes=[qo])
                    r0 = o0 + tt * 128
                    if ch < 8:
                        P.dma("pool", q_o[r0:r0 + 128, ch * 512:(ch + 1) * 512], qo[:, :], reads=[qo], writes=[], owner=qo)
                    else:
                        P.dma("pool", kv_o[r0:r0 + 128, (ch - 8) * 512:(ch - 7) * 512], qo[:, :], reads=[qo], writes=[], owner=qo)
    P.finish("sp")
    info = P.emit()
    return nc, info


D = 4096
KT = 32
ST = 512
NTOK = 8192
EPS = 1e-5
NH = 16
XQ = "sp"
PG, PCW, PCB, PDTB, PALOG, PDD, PNG = 0, 32, 32 + 40, 32 + 50, 32 + 66, 32 + 82, 32 + 98
NPAR = 32 + 98 + 1024


def build_mamba(n_super=NTOK // ST, debug=False):
    nc = bass.Bass("TRN2", target_bir_lowering=False)
    P = Prog(nc)
    xT = P.dram("xT", [128, KT, NTOK], F32, kind="ExternalInput")
    wx_d = P.dram("wx", [18, 128, KT, 128], BF16, kind="ExternalInput")
    wdt_d = P.dram("wdt", [128, KT * 16], BF16, kind="ExternalInput")
    par_d = P.dram("par", [128, NPAR], F32, kind="ExternalInput")
    cst_d = P.dram("cst", [128, 256], F32, kind="ExternalInput")
    esel_d = P.dram("esel", [16, 2048], F32, kind="ExternalInput")
    idb_d = P.dram("idb", [128, 128], BF16, kind="ExternalInput")
    y_o = P.dram("y_o", [NTOK, 1024], BF16, kind="ExternalOutput")

    par = P.sbuf([128, NPAR], F32, "par")
    cst = P.sbuf([128, 256], F32, "cst")
    esel = P.sbuf([16, 2048], F32, "esel")
    idb = P.sbuf([128, 128], BF16, "idb")
    wdt = P.sbuf([128, KT * 16], BF16, "wdt")
    ones = P.sbuf([128, 128], BF16, "ones")
    onecol = P.sbuf([128, 1], F32, "onecol")
    epsb = P.sbuf([128, 1], F32, "epsb")
    Arep = P.sbuf([128, NH], F32, "Arep")
    uT = P.sbuf([128, KT * ST], BF16, "uT")
    xbcT = P.sbuf([128, 10 * ST], BF16, "xbcT")
    szT = P.sbuf([128, 8 * ST], BF16, "szT")
    rstd = P.sbuf([128, ST], F32, "rstd")
    rs_tm = P.sbuf([128, 4], F32, "rs_tm")
    tails = P.sbuf([128, 30], F32, "tails")
    S = P.sbuf([128, 1024], F32, "S")
    S_bf = P.sbuf([128, 1024], BF16, "S_bf")
    ws = WStream(P, 4, KT * 128, "sp", pf=3)
    xs = [P.sbuf([128, ST], F32, f"xs{i}") for i in range(3)]
    sqb = [P.sbuf([128, ST], BF16, f"sq{i}") for i in range(2)]
    raw = [P.sbuf([128, ST + 3], F32, f"raw{i}") for i in range(2)]
    acc = [P.sbuf([128, ST], F32, f"acc{i}") for i in range(2)]
    Xtm = P.sbuf([128, 1024], BF16, "Xtm")
    Btm = P.sbuf([128, 128], BF16, "Btm")
    Xdt = P.sbuf([128, 1024], BF16, "Xdt")
    Xdte = P.sbuf([128, 1024], BF16, "Xdte")
    arg = [P.sbuf([128, 512], F32, f"arg{i}") for i in range(2)]
    Ldec = [P.sbuf([128, 512], F32, f"Ldec{i}") for i in range(2)]
    MT = P.sbuf([128, 2048], BF16, "MT")
    yd_sb = P.sbuf([128, 1024], F32, "yd_sb")
    yb = P.sbuf([128, 1024], F32, "yb")
    yg = P.sbuf([128, 1024], F32, "yg")
    junk = P.sbuf([128, 1024], BF16, "junk")
    yn = [P.sbuf([128, 1024], BF16, f"yn{i}") for i in range(2)]
    dtv = P.sbuf([128, NH], F32, "dtv")
    e1 = P.sbuf([128, NH], F32, "e1")
    av = P.sbuf([128, NH], F32, "av")
    acs_sb = P.sbuf([128, NH], F32, "acs_sb")
    acsT_sb = P.sbuf([16, 128], F32, "acsT_sb")
    ea = P.sbuf([128, NH], F32, "ea")
    cdb = P.sbuf([128, NH], F32, "cdb")
    dearg = P.sbuf([128, NH], F32, "dearg")
    de = P.sbuf([128, NH], F32, "de")
    ss = P.sbuf([128, 1], F32, "ss")
    rsn = P.sbuf([128, 1], F32, "rsn")

    b0 = P.psum([128, 1024], BF16, "b0")
    b1 = P.psum([128, 1024], BF16, "b1")
    b2 = P.psum([128, 512], F32, "b2")
    b3 = P.psum([128, 512], F32, "b3")
    pA = [P.psum([128, 512], F32, f"pA{i}") for i in range(2)]
    pB = [P.psum([128, 512], F32, f"pB{i}") for i in range(2)]

    T = cst[:, 0:128]
    maskb = cst[:, 128:256]

    P.dma("pool", par[:], par_d[:], reads=[par_d], writes=[par])
    P.dma("pool", cst[:], cst_d[:], reads=[cst_d], writes=[cst])
    P.dma("pool", esel[:], esel_d[:], reads=[esel_d], writes=[esel])
    P.dma("pool", idb[:], idb_d[:], reads=[idb_d], writes=[idb])
    P.dma("pool", wdt[:], wdt_d[:], reads=[wdt_d], writes=[wdt])
    P.op("pool", lambda e: e.memset(ones[:], 1.0), writes=[ones])
    P.op("pool", lambda e: e.memset(onecol[:], 1.0), writes=[onecol])
    P.op("pool", lambda e: e.memset(epsb[:], EPS), writes=[epsb])
    P.op("pool", lambda e: e.memset(tails[:], 0.0), writes=[tails])
    P.op("pool", lambda e: e.memset(S[:], 0.0), writes=[S])
    P.op("pool", lambda e: e.memset(S_bf[:], 0.0), writes=[S_bf])
    P.op("act", lambda e: e.activation(Arep[:], par[:, PALOG:PALOG + NH], AF.Exp), reads=[par], writes=[Arep])
    P.op("dve", lambda e: e.tensor_scalar(Arep[:], Arep[:], -1.0, None, ALU.mult), reads=[Arep], writes=[Arep])

    for st in range(n_super):
        for j in range(18):
            ws.add(wx_d, wx_d[j].rearrange("p a b -> p (a b)"), KT * 128)

    cnt = {"xs": 0, "sq": 0, "t": 0, "g": 0, "yn": 0}
    for st in range(n_super):
        t0 = st * ST
        for kt in range(KT):
            x_ = xs[cnt["xs"] % 3]; cnt["xs"] += 1
            P.dma(XQ, x_[:], xT[:, kt, t0:t0 + ST], reads=[xT], writes=[x_])
            sq = sqb[cnt["sq"] % 2]; cnt["sq"] += 1
            P.op("pool", (lambda e, x_=x_, sq=sq: e.tensor_tensor(sq[:], x_[:], x_[:], ALU.mult)), reads=[x_], writes=[sq])
            P.op("pe", (lambda e, sq=sq, kt=kt: e.matmul(b3[:], ones[:], sq[:], start=(kt == 0), stop=(kt == KT - 1))),
                 reads=[ones, sq], writes=[b3])
            P.op("dve", (lambda e, x_=x_, kt=kt: e.tensor_scalar(uT[:, kt * ST:(kt + 1) * ST], x_[:], par[:, PG + kt:PG + kt + 1], None, ALU.mult)),
                 reads=[x_, par], writes=[uT])
        P.op("act", lambda e: e.activation(rstd[:], b3[:], AF.Ln, bias=epsb[:, 0:1], scale=1.0 / D), reads=[b3, epsb], writes=[rstd])
        P.op("act", lambda e: e.activation(rstd[:], rstd[:], AF.Exp, scale=-0.5), reads=[rstd], writes=[rstd])
        for tt in range(4):
            P.op("pe", (lambda e, tt=tt: e.matmul(b2[:, 300 + tt:301 + tt], rstd[0:1, tt * 128:(tt + 1) * 128], onecol[0:1, 0:1],
                                                  start=True, stop=True)), reads=[rstd, onecol], writes=[b2])
        P.op("act", lambda e: e.copy(rs_tm[:, 0:4], b2[:, 300:304]), reads=[b2], writes=[rs_tm])
        for j in range(18):
            pb = pA[cnt["t"] % 2]
            rw = raw[cnt["t"] % 2]
            ac = acc[cnt["t"] % 2]
            cnt["t"] += 1
            w = ws.next()
            for k in range(KT):
                P.op("pe", (lambda e, w=w, k=k, pb=pb: e.matmul(pb[:], w[:, k * 128:(k + 1) * 128], uT[:, k * ST:(k + 1) * ST],
                                                                start=(k == 0), stop=(k == KT - 1))), reads=[w, uT], writes=[pb])
            if j < 8:
                P.op("dve", (lambda e, pb=pb, ac=ac: e.tensor_tensor(ac[:], pb[:], rstd[:], ALU.mult)), reads=[pb, rstd], writes=[ac])
                P.op("act", (lambda e, ac=ac, rw=rw: e.activation(rw[:, 0:ST], ac[:], AF.Exp, scale=-1.0)), reads=[ac], writes=[rw])
                P.op("dve", (lambda e, rw=rw: e.tensor_scalar(rw[:, 0:ST], rw[:, 0:ST], 1.0, None, ALU.add)), reads=[rw], writes=[rw])
                P.op("dve", (lambda e, rw=rw: e.reciprocal(rw[:, 0:ST], rw[:, 0:ST])), reads=[rw], writes=[rw])
                P.op("dve", (lambda e, ac=ac, rw=rw, j=j: e.tensor_tensor(szT[:, j * ST:(j + 1) * ST], ac[:], rw[:, 0:ST], ALU.mult)),
                     reads=[ac, rw], writes=[szT])
                continue
            c = j - 8
            P.op("dve", (lambda e, pb=pb, rw=rw: e.tensor_tensor(rw[:, 3:3 + ST], pb[:], rstd[:], ALU.mult)), reads=[pb, rstd], writes=[rw])
            P.op("pool", (lambda e, rw=rw, c=c: e.tensor_copy(rw[:, 0:3], tails[:, 3 * c:3 * c + 3])), reads=[tails], writes=[rw])
            P.op("pool", (lambda e, rw=rw, c=c: e.tensor_copy(tails[:, 3 * c:3 * c + 3], rw[:, ST:ST + 3])), reads=[rw], writes=[tails])
            P.op("dve", (lambda e, rw=rw, ac=ac, c=c: e.tensor_scalar(ac[:], rw[:, 0:ST], par[:, PCW + 4 * c:PCW + 4 * c + 1],
                                                                    par[:, PCB + c:PCB + c + 1], ALU.mult, ALU.add)),
                 reads=[rw, par], writes=[ac])
            for kk in (1, 2, 3):
                P.op("dve", (lambda e, rw=rw, ac=ac, c=c, kk=kk: e.scalar_tensor_tensor(
                    ac[:], rw[:, kk:kk + ST], par[:, PCW + 4 * c + kk:PCW + 4 * c + kk + 1], ac[:], ALU.mult, ALU.add)),
                    reads=[rw, par, ac], writes=[ac])
            P.op("act", (lambda e, ac=ac, rw=rw: e.activation(rw[:, 0:ST], ac[:], AF.Exp, scale=-1.0)), reads=[ac], writes=[rw])
            P.op("dve", (lambda e, rw=rw: e.tensor_scalar(rw[:, 0:ST], rw[:, 0:ST], 1.0, None, ALU.add)), reads=[rw], writes=[rw])
            P.op("dve", (lambda e, rw=rw: e.reciprocal(rw[:, 0:ST], rw[:, 0:ST])), reads=[rw], writes=[rw])
            P.op("dve", (lambda e, ac=ac, rw=rw, c=c: e.tensor_tensor(xbcT[:, c * ST:(c + 1) * ST], ac[:], rw[:, 0:ST], ALU.mult)),
                 reads=[ac, rw], writes=[xbcT])
        for tt in range(4):
            c0 = tt * 128
            BT = xbcT[:, 8 * ST + c0:8 * ST + c0 + 128]
            CT = xbcT[:, 9 * ST + c0:9 * ST + c0 + 128]
            for k in range(KT):
                P.op("pe", (lambda e, k=k, c0=c0: e.matmul(b2[:, 272:288], uT[:, k * ST + c0:k * ST + c0 + 128], wdt[:, k * 16:(k + 1) * 16],
                                                           start=(k == 0), stop=(k == KT - 1))), reads=[uT, wdt], writes=[b2])
            P.op("dve", (lambda e, tt=tt: e.scalar_tensor_tensor(dtv[:], b2[:, 272:288], rs_tm[:, tt:tt + 1], par[:, PDTB:PDTB + NH], ALU.mult, ALU.add)),
                 reads=[b2, rs_tm, par], writes=[dtv])
            P.op("act", lambda e: e.activation(e1[:], dtv[:], AF.Exp), reads=[dtv], writes=[e1])
            P.op("act", lambda e: e.activation(dtv[:], e1[:], AF.Ln, bias=onecol[:, 0:1]), reads=[e1, onecol], writes=[dtv])
            P.op("dve", lambda e: e.tensor_tensor(av[:], dtv[:], Arep[:], ALU.mult), reads=[dtv, Arep], writes=[av])
            P.op("pe", lambda e: e.matmul(b2[:, 128:144], T, av[:], start=True, stop=True), reads=[cst, av], writes=[b2])
            P.op("pe", lambda e: e.matmul(b2[0:16, 144:272], av[:], T, start=True, stop=True), reads=[cst, av], writes=[b2])
            P.op("act", lambda e: e.copy(acs_sb[:], b2[:, 128:144]), reads=[b2], writes=[acs_sb])
            P.op("act", lambda e: e.copy(acsT_sb[:], b2[0:16, 144:272]), reads=[b2], writes=[acsT_sb])
            P.op("act", lambda e: e.activation(ea[:], acs_sb[:], AF.Exp), reads=[acs_sb], writes=[ea])
            for j in range(8):
                P.op("pe", (lambda e, j=j, c0=c0: e.transpose(b0[:, j * 128:(j + 1) * 128], xbcT[:, j * ST + c0:j * ST + c0 + 128], idb[:])),
                     reads=[xbcT, idb], writes=[b0])
            P.op("act", lambda e: e.copy(Xtm[:], b0[:]), reads=[b0], writes=[Xtm])
            P.op("pe", (lambda e, BT=BT: e.transpose(b0[:, 0:128], BT, idb[:])), reads=[xbcT, idb], writes=[b0])
            P.op("act", lambda e: e.copy(Btm[:], b0[:, 0:128]), reads=[b0], writes=[Btm])
            P.op("pe", (lambda e, BT=BT, CT=CT: e.matmul(b2[:, 0:128], BT, CT, start=True, stop=True)), reads=[xbcT], writes=[b2])
            for g in range(4):
                ar = arg[cnt["g"] % 2]; Ld = Ldec[cnt["g"] % 2]; cnt["g"] += 1
                for hh in range(4):
                    h = 4 * g + hh
                    P.op("pe", (lambda e, h=h, hh=hh: e.matmul(b3[:, hh * 128:(hh + 1) * 128], esel[0:16, h * 128:(h + 1) * 128], acsT_sb[:],
                                                              start=True, stop=True)), reads=[esel, acsT_sb], writes=[b3])
                b3v = b3[:, :].rearrange("p (h l) -> p h l", l=128)[:, :, 127]
                P.op("act", (lambda e, g=g, b3v=b3v: e.activation(cdb[:, 4 * g:4 * g + 4], b3v, AF.Exp)), reads=[b3], writes=[cdb])
                P.op("dve", (lambda e, g=g, b3v=b3v: e.tensor_tensor(dearg[:, 4 * g:4 * g + 4], b3v, acs_sb[:, 4 * g:4 * g + 4], ALU.subtract)),
                     reads=[b3, acs_sb], writes=[dearg])
                for hh in range(4):
                    h = 4 * g + hh
                    P.op("dve", (lambda e, h=h, hh=hh, ar=ar: e.scalar_tensor_tensor(
                        ar[:, hh * 128:(hh + 1) * 128], b3[:, hh * 128:(hh + 1) * 128], acs_sb[:, h:h + 1], maskb, ALU.subtract, ALU.add)),
                        reads=[b3, acs_sb, cst], writes=[ar])
                P.op("act", (lambda e, ar=ar, Ld=Ld: e.activation(Ld[:], ar[:], AF.Exp)), reads=[ar], writes=[Ld])
                for hh in range(4):
                    h = 4 * g + hh
                    P.op("dve", (lambda e, h=h, hh=hh, Ld=Ld: e.tensor_tensor(MT[:, h * 128:(h + 1) * 128], Ld[:, hh * 128:(hh + 1) * 128],
                                                                          b2[:, 0:128], ALU.mult)), reads=[Ld, b2], writes=[MT])
            P.op("act", lambda e: e.activation(de[:], dearg[:], AF.Exp), reads=[dearg], writes=[de])
            for h in range(NH):
                hs = slice(h * 64, (h + 1) * 64)
                P.op("pool", (lambda e, h=h, hs=hs: e.tensor_scalar(Xdt[:, hs], Xtm[:, hs], dtv[:, h:h + 1], None, ALU.mult)),
                     reads=[Xtm, dtv], writes=[Xdt])
            for h in range(NH):
                hs = slice(h * 64, (h + 1) * 64)
                P.op("pool", (lambda e, h=h, hs=hs: e.tensor_scalar(Xdte[:, hs], Xdt[:, hs], de[:, h:h + 1], None, ALU.mult)),
                     reads=[Xdt, de], writes=[Xdte])
            for cc in range(2):
                P.op("pe", (lambda e, cc=cc, CT=CT: e.matmul(pB[cc][:], CT, S_bf[:, cc * 512:(cc + 1) * 512], start=True, stop=True)),
                     reads=[xbcT, S_bf], writes=[pB[cc]])
            for h in range(NH):
                P.op("pe", (lambda e, h=h: e.matmul(pA[h // 8][:, (h % 8) * 64:(h % 8 + 1) * 64], MT[:, h * 128:(h + 1) * 128],
                                                    Xdt[:, h * 64:(h + 1) * 64], start=True, stop=True)),
                     reads=[MT, Xdt], writes=[pA[h // 8]])
            for cc in range(2):
                P.op("act", (lambda e, cc=cc: e.copy(yd_sb[:, cc * 512:(cc + 1) * 512], pA[cc][:])), reads=[pA[cc]], writes=[yd_sb])
            for h in range(NH):
                hs = slice(h * 64, (h + 1) * 64)
                ps_ = slice((h % 8) * 64, (h % 8 + 1) * 64)
                P.op("dve", (lambda e, h=h, hs=hs, ps_=ps_: e.scalar_tensor_tensor(yb[:, hs], pB[h // 8][:, ps_], ea[:, h:h + 1], yd_sb[:, hs],
                                                                               ALU.mult, ALU.add)), reads=[pB[h // 8], ea, yd_sb], writes=[yb])
            for h in range(NH):
                hs = slice(h * 64, (h + 1) * 64)
                P.op("dve", (lambda e, h=h, hs=hs: e.scalar_tensor_tensor(yb[:, hs], Xtm[:, hs], par[:, PDD + h:PDD + h + 1], yb[:, hs],
                                                                      ALU.mult, ALU.add)), reads=[Xtm, par, yb], writes=[yb])
            for cc in range(2):
                P.op("pe", (lambda e, cc=cc: e.matmul(pB[cc][:], Btm[:], Xdte[:, cc * 512:(cc + 1) * 512], start=True, stop=True)),
                     reads=[Btm, Xdte], writes=[pB[cc]])
            for h in range(NH):
                hs = slice(h * 64, (h + 1) * 64)
                ps_ = slice((h % 8) * 64, (h % 8 + 1) * 64)
                P.op("dve", (lambda e, h=h, hs=hs, ps_=ps_: e.scalar_tensor_tensor(S[:, hs], S[:, hs], cdb[:, h:h + 1], pB[h // 8][:, ps_],
                                                                               ALU.mult, ALU.add)), reads=[S, cdb, pB[h // 8]], writes=[S])
            P.op("act", lambda e: e.copy(S_bf[:], S[:]), reads=[S], writes=[S_bf])
            for j in range(8):
                P.op("pe", (lambda e, j=j, c0=c0: e.transpose(b1[:, j * 128:(j + 1) * 128], szT[:, j * ST + c0:j * ST + c0 + 128], idb[:])),
                     reads=[szT, idb], writes=[b1])
            P.op("dve", lambda e: e.tensor_tensor(yg[:], yb[:], b1[:], ALU.mult), reads=[yb, b1], writes=[yg])
            P.op("dve", lambda e: e.scalar_tensor_tensor(junk[:], yg[:], 1.0, yg[:], ALU.mult, ALU.mult, accum_out=ss[:, 0:1]),
                 reads=[yg], writes=[junk, ss])
            P.op("act", lambda e: e.activation(rsn[:], ss[:], AF.Ln, bias=epsb[:, 0:1], scale=1.0 / 1024), reads=[ss, epsb], writes=[rsn])
            P.op("act", lambda e: e.activation(rsn[:], rsn[:], AF.Exp, scale=-0.5), reads=[rsn], writes=[rsn])
            yo_ = yn[cnt["yn"] % 2]; cnt["yn"] += 1
            P.op("dve", (lambda e, yo_=yo_: e.scalar_tensor_tensor(yo_[:], yg[:], rsn[:, 0:1], par[:, PNG:PNG + 1024], ALU.mult, ALU.mult)),
                 reads=[yg, rsn, par], writes=[yo_])
            P.dma("pool", y_o[t0 + c0:t0 + c0 + 128, :], yo_[:], reads=[yo_], writes=[], owner=yo_)
            if debug and st == 0 and tt == 0:
                dbg = P.sbuf([128, 6400], F32, "dbg")
                dbg_o = P.dram("dbg_o", [128, 6400], F32, kind="ExternalOutput")
                items = [(Xtm, 0, 1024), (Btm, 0, 128), (dtv, 0, 16), (acs_sb, 0, 16), (ea, 0, 16), (cdb, 0, 16), (de, 0, 16),
                         (MT, 0, 128), (yd_sb, 0, 1024), (yb, 0, 1024), (S, 0, 1024), (yg, 0, 1024),
                         (xbcT, 0, 128), (xbcT, 8 * ST, 128), (xbcT, 9 * ST, 128), (szT, 0, 128), (MT, 15 * 128, 128),
                         (rs_tm, 0, 4), (e1, 0, 16), (av, 0, 16), (b2, 272, 16), (b2, 128, 16), (rstd, 0, 128), (Arep, 0, 16)]
                off = 0
                for (bb, o_, n_) in items:
                    P.op("dve", (lambda e, bb=bb, o_=o_, n_=n_, off=off: e.tensor_copy(dbg[:, off:off + n_], bb[:, o_:o_ + n_])),
                         reads=[bb], writes=[dbg])
                    off += n_
                P.dma("pool", dbg_o[:], dbg[:], reads=[dbg], writes=[], owner=dbg)
    P.finish("sp")
    info = P.emit()
    return nc, info


NQT = 64
SCALE = 128 ** -0.5
NEG = -1.0e30


def build_attn(n_qt=NQT):
    nc = bass.Bass("TRN2", target_bir_lowering=False)
    P = Prog(nc)
    qT_d = P.dram("qT", [128, NQT, 512], BF16, kind="ExternalInput")
    kT_d = P.dram("kT", [128, 8192], BF16, kind="ExternalInput")
    v_d = P.dram("v", [128, 64, 128], BF16, kind="ExternalInput")
    tri_d = P.dram("tri", [128, 128], BF16, kind="ExternalInput")
    o_o = P.dram("o_o", [8192, 512], BF16, kind="ExternalOutput")

    qT = P.sbuf([128, NQT * 512], BF16, "qT")
    kT = P.sbuf([128, 8192], BF16, "kT")
    vp = P.sbuf([128, 64 * 129], BF16, "vp")
    tri = P.sbuf([128, 128], BF16, "tri")
    km = P.sbuf([128, 32], F32, "km")
    kmh = P.sbuf([128, 32], BF16, "kmh")
    kmhf = P.sbuf([128, 32], F32, "kmhf")
    kml = P.sbuf([128, 32], BF16, "kml")
    g_sb = P.sbuf([128, 128], F32, "g_sb")
    t8 = P.sbuf([128, 32], F32, "t8")
    sel = P.sbuf([128, 128], F32, "sel")
    pT = [P.sbuf([128, 512], BF16, f"pT{i}") for i in range(4)]
    acc = [P.sbuf([128, 4 * 129], F32, f"acc{i}") for i in range(2)]
    rec = P.sbuf([128, 4], F32, "rec")
    ob = [P.sbuf([128, 512], BF16, f"ob{i}") for i in range(2)]
    sTb = [P.psum([128, 512], F32, f"sT{i}") for i in range(4)]
    pvb = [P.psum([128, 512], F32, f"pv{i}") for i in range(4)]

    vp3 = vp[:, :].rearrange("p (t c) -> p t c", c=129)
    P.dma("sp", qT[:], qT_d[:].rearrange("p a b -> p (a b)"), reads=[qT_d], writes=[qT])
    P.dma("pool", kT[:], kT_d[:], reads=[kT_d], writes=[kT])
    P.dma("pool", vp3[:, :, 0:128], v_d[:], reads=[v_d], writes=[vp])
    P.dma("pool", tri[:], tri_d[:], reads=[tri_d], writes=[tri])
    P.op("pool", lambda e: e.memset(vp3[:, :, 128:129], 1.0), writes=[vp])
    P.op("dve", lambda e: e.tensor_reduce(km[:], kT[:, :].rearrange("p (n k) -> p n k", k=256), AX.X, ALU.add), reads=[kT], writes=[km])
    P.op("dve", lambda e: e.tensor_scalar(km[:], km[:], 1.0 / 256, None, ALU.mult), reads=[km], writes=[km])
    P.op("dve", lambda e: e.tensor_copy(kmh[:], km[:]), reads=[km], writes=[kmh])
    P.op("dve", lambda e: e.tensor_copy(kmhf[:], kmh[:]), reads=[kmh], writes=[kmhf])
    P.op("dve", lambda e: e.tensor_tensor(kml[:], km[:], kmhf[:], ALU.subtract), reads=[km, kmhf], writes=[kml])

    cnt = {"s": 0, "p": 0, "pv": 0}
    for qt in range(n_qt):
        blk = qt // 2
        half = qt % 2
        ac = acc[qt % 2]
        q_rhs = qT[:, qt * 512:(qt + 1) * 512]
        P.op("pool", (lambda e, ac=ac: e.memset(ac[:], 0.0)), writes=[ac])
        if blk > 0:
            gp = sTb[0]
            for h in range(4):
                P.op("pe", (lambda e, h=h, gp=gp: e.matmul(gp[:, h * 32:h * 32 + blk], qT[:, qt * 512 + h * 128:qt * 512 + (h + 1) * 128],
                                                          kmh[:, 0:blk], start=True, stop=False)), reads=[qT, kmh], writes=[gp])
                P.op("pe", (lambda e, h=h, gp=gp: e.matmul(gp[:, h * 32:h * 32 + blk], qT[:, qt * 512 + h * 128:qt * 512 + (h + 1) * 128],
                                                          kml[:, 0:blk], start=False, stop=True)), reads=[qT, kml], writes=[gp])
            P.op("pool", lambda e: e.memset(g_sb[:], NEG), writes=[g_sb])
            g3 = g_sb[:, :].rearrange("p (h n) -> p h n", n=32)[:, :, 0:blk]
            p3 = gp[:, 0:128].rearrange("p (h n) -> p h n", n=32)[:, :, 0:blk]
            P.op("act", (lambda e, g3=g3, p3=p3: e.copy(g3, p3)), reads=[gp], writes=[g_sb])
            for h in range(4):
                P.op("dve", (lambda e, h=h: e.max(t8[:, h * 8:(h + 1) * 8], g_sb[:, h * 32:(h + 1) * 32])), reads=[g_sb], writes=[t8])
            for h in range(4):
                P.op("dve", (lambda e, h=h: e.tensor_scalar(sel[:, h * 32:h * 32 + blk], g_sb[:, h * 32:h * 32 + blk],
                                                         t8[:, h * 8 + 2:h * 8 + 3], None, ALU.is_ge)), reads=[g_sb, t8], writes=[sel])
        groups = [([(2 * n, False), (2 * n + 1, False)], n) for n in range(blk)]
        if half == 0:
            groups.append(([(2 * blk, True)], None))
        else:
            groups.append(([(2 * blk, False), (2 * blk + 1, True)], None))
        for tiles, n in groups:
            pts = []
            for (kti, diag) in tiles:
                sT = sTb[cnt["s"] % 4]; cnt["s"] += 1
                pt = pT[cnt["p"] % 4]; cnt["p"] += 1
                P.op("pe", (lambda e, sT=sT, kti=kti: e.matmul(sT[:], kT[:, kti * 128:(kti + 1) * 128], q_rhs, start=True, stop=True)),
                     reads=[kT, qT], writes=[sT])
                P.op("act", (lambda e, sT=sT, pt=pt: e.activation(pt[:], sT[:], AF.Exp, scale=SCALE)), reads=[sT], writes=[pt])
                if diag:
                    for h in range(4):
                        P.op("pool", (lambda e, pt=pt, h=h: e.tensor_tensor(pt[:, h * 128:(h + 1) * 128], pt[:, h * 128:(h + 1) * 128], tri[:], ALU.mult)),
                             reads=[pt, tri], writes=[pt])
                pts.append((pt, kti))
            pvA = pvb[(cnt["pv"] % 2) * 2]; pvB = pvb[(cnt["pv"] % 2) * 2 + 1]; cnt["pv"] += 1
            for h in range(4):
                pv = pvA if h < 2 else pvB
                for i, (pt, kti) in enumerate(pts):
                    P.op("pe", (lambda e, pv=pv, pt=pt, kti=kti, h=h, i=i: e.matmul(
                        pv[:, (h % 2) * 129:(h % 2) * 129 + 129], pt[:, h * 128:(h + 1) * 128], vp[:, kti * 129:(kti + 1) * 129],
                        start=(i == 0), stop=(i == len(pts) - 1))), reads=[pt, vp], writes=[pv])
            for h in range(4):
                pv = pvA if h < 2 else pvB
                src = pv[:, (h % 2) * 129:(h % 2) * 129 + 129]
                dst = ac[:, h * 129:(h + 1) * 129]
                if n is None:
                    P.op("dve", (lambda e, src=src, dst=dst: e.tensor_tensor(dst, src, dst, ALU.add)), reads=[pv, ac], writes=[ac])
                else:
                    P.op("dve", (lambda e, src=src, dst=dst, h=h, n=n: e.scalar_tensor_tensor(
                        dst, src, sel[:, h * 32 + n:h * 32 + n + 1], dst, ALU.mult, ALU.add)), reads=[pv, ac, sel], writes=[ac])
        a3 = ac[:, :].rearrange("p (h c) -> p h c", c=129)[:, :, 128]
        P.op("dve", (lambda e, a3=a3: e.reciprocal(rec[:], a3)), reads=[ac], writes=[rec])
        o_ = ob[qt % 2]
        for h in range(4):
            P.op("dve", (lambda e, h=h, o_=o_, ac=ac: e.tensor_scalar(o_[:, h * 128:(h + 1) * 128], ac[:, h * 129:h * 129 + 128],
                                                                  rec[:, h:h + 1], None, ALU.mult)), reads=[ac, rec], writes=[o_])
        P.dma("pool", o_o[qt * 128:(qt + 1) * 128, :], o_[:], reads=[o_], writes=[], owner=o_)
    P.finish("sp")
    info = P.emit()
    return nc, info


CCH = 8192


def build_cast(M):
    nc = bass.Bass("TRN2", target_bir_lowering=False)
    P = Prog(nc)
    w_d = P.dram("w", [128, M], F32, kind="ExternalInput")
    wb_d = P.dram("wb", [128, M], BF16, kind="ExternalOutput")
    NB = 3
    fb = [P.sbuf([128, CCH], F32, f"fb{i}") for i in range(NB)]
    bb = [P.sbuf([128, CCH], BF16, f"bb{i}") for i in range(NB)]
    n = M // CCH
    engs = ("dve", "act", "pool")
    for i in range(n):
        f = fb[i % NB]
        b = bb[i % NB]
        P.dma("sp", f[:], w_d[:, i * CCH:(i + 1) * CCH], reads=[w_d], writes=[f])
        en = engs[i % 3]
        if en == "act":
            P.op("act", lambda e: e.copy(b[:], f[:]), reads=[f], writes=[b])
        else:
            P.op(en, lambda e: e.tensor_copy(b[:], f[:]), reads=[f], writes=[b])
        P.dma("pool", wb_d[:, i * CCH:(i + 1) * CCH], b[:], reads=[b], writes=[], owner=b)
    P.finish("sp")
    P.emit()
    return nc


_bf = ml_dtypes.bfloat16
NCORES = 8


def _run(nc, in_maps):
    res = run_bass_kernel_spmd(nc, in_maps, core_ids=list(range(NCORES)))
    return res.results


def _fm(a, kt):
    n = a.shape[0]
    return np.ascontiguousarray(a.T.reshape(kt, 128, n).transpose(1, 0, 2))


def _tok_par(g_ffn, g2, g3, cw, cb):
    par = np.zeros((128, 4 * 32 + 172 * 4), np.float32)
    par[:, 0:32] = g_ffn.reshape(32, 128).T
    par[:, 32:64] = g2.reshape(32, 128).T
    if g3 is not None:
        par[:, 64:96] = g3.reshape(32, 128).T
    par[:, 128:128 + 516] = cw.reshape(3, 172, 128).transpose(2, 1, 0).reshape(128, 516)
    par[:, 128 + 516:] = cb.reshape(172, 128).T
    return par


def _cs_table(base):
    pos = base + np.arange(1024, dtype=np.float32)
    inv = (10000.0 ** (-np.arange(64, dtype=np.float32) / 64)).astype(np.float32)
    ang = pos[:, None] * inv[None, :]
    t = np.concatenate([np.cos(ang), np.sin(ang)], 1).astype(np.float32)
    return np.ascontiguousarray(t.reshape(8, 128, 128).transpose(1, 0, 2))


def kernel(x, norm_a, mamba_w_in, mamba_conv_w, mamba_conv_b, mamba_dt_bias, mamba_a_log,
           mamba_d, mamba_norm, mamba_w_out, kv_norm, w_kv, norm_b, w_q, w_o,
           ffn_norm, ffn_w_up, ffn_conv_w, ffn_conv_b, ffn_w_down, final_norm):
    f32 = np.float32
    x = np.asarray(x, f32)[0]
    w_in = np.asarray(mamba_w_in, f32)[0]

    shapes = []
    for g in range(8):
        shapes.append((f"wx{g}", (18, 128, 32, 128)))
        shapes.append((f"wdt{g}", (128, 512)))
    shapes += [("wpA", (32, 128, 64, 128)), ("wuA", (86, 128, 2, 32, 128)), ("wdA", (32, 128, 86, 128)), ("wqA", (12, 128, 32, 512)),
               ("wpB", (32, 128, 32, 128)), ("wuB", (86, 128, 2, 32, 128)), ("wdB", (32, 128, 86, 128))]
    offs = {}
    tot = 0
    for nm, sh in shapes:
        offs[nm] = (tot, sh)
        tot += int(np.prod(sh))
    per = -(-tot // (NCORES * 128 * CCH)) * CCH
    flat = np.zeros(NCORES * 128 * per, f32)

    def put(nm, arr):
        o, sh = offs[nm]
        flat[o:o + arr.size].reshape(sh)[...] = arr

    for g in range(8):
        wst = np.concatenate([w_in[:, g * 1024:(g + 1) * 1024], w_in[:, 8192 + g * 1024:8192 + (g + 1) * 1024],
                              w_in[:, 16384 + g * 128:16384 + (g + 1) * 128], w_in[:, 17408 + g * 128:17408 + (g + 1) * 128]], 1)
        put(f"wx{g}", wst.reshape(32, 128, 18, 128).transpose(2, 1, 0, 3))
        put(f"wdt{g}", w_in[:, 18432 + g * 16:18432 + (g + 1) * 16].reshape(32, 128, 16).transpose(1, 0, 2).reshape(128, 512))
    put("wpA", np.asarray(mamba_w_out, f32)[0].reshape(64, 128, 32, 128).transpose(2, 1, 0, 3))
    put("wuA", np.asarray(ffn_w_up, f32)[0].reshape(32, 128, 2, 86, 128).transpose(3, 1, 2, 0, 4))
    put("wdA", np.asarray(ffn_w_down, f32)[0].reshape(86, 128, 32, 128).transpose(2, 1, 0, 3))
    wqkv = np.concatenate([np.asarray(w_q, f32)[0], np.asarray(w_kv, f32)], 1)
    put("wqA", wqkv.reshape(32, 128, 12, 512).transpose(2, 1, 0, 3))
    put("wpB", np.asarray(w_o, f32)[0].reshape(32, 128, 32, 128).transpose(2, 1, 0, 3))
    put("wuB", np.asarray(ffn_w_up, f32)[1].reshape(32, 128, 2, 86, 128).transpose(3, 1, 2, 0, 4))
    put("wdB", np.asarray(ffn_w_down, f32)[1].reshape(86, 128, 32, 128).transpose(2, 1, 0, 3))
    del wqkv
    fl = flat.reshape(NCORES, 128, per)
    res = _run(build_cast(per), [{"w": fl[c]} for c in range(NCORES)])
    wbf = np.concatenate([r["wb"].reshape(-1) for r in res])
    del flat, fl, res

    def getw(nm):
        o, sh = offs[nm]
        return wbf[o:o + int(np.prod(sh))].reshape(sh)

    xT_l = _fm(x, 32)
    cst = np.zeros((128, 256), f32)
    kk = np.arange(128)
    cst[:, 0:128] = (kk[:, None] <= kk[None, :]).astype(f32)
    cst[:, 128:256] = np.where(kk[:, None] <= kk[None, :], 0.0, -30000.0)
    esel = np.zeros((16, 2048), f32)
    for h in range(16):
        esel[h, h * 128:(h + 1) * 128] = 1.0
    idb = np.eye(128, dtype=f32).astype(_bf)
    tri = cst[:, 0:128].astype(_bf)
    cwm = np.asarray(mamba_conv_w, f32)[0]
    cbm = np.asarray(mamba_conv_b, f32)[0]
    in_maps = []
    for g in range(8):
        par = np.zeros((128, NPAR), f32)
        par[:, PG:PG + 32] = np.asarray(norm_a, f32)[0].reshape(32, 128).T
        idx = np.concatenate([np.arange(g * 1024, (g + 1) * 1024), 8192 + np.arange(g * 128, (g + 1) * 128),
                              9216 + np.arange(g * 128, (g + 1) * 128)])
        par[:, PCW:PCW + 40] = cwm[:, idx].reshape(4, 10, 128).transpose(2, 1, 0).reshape(128, 40)
        par[:, PCB:PCB + 10] = cbm[idx].reshape(10, 128).T
        par[:, PDTB:PDTB + 16] = np.asarray(mamba_dt_bias, f32)[0][None, g * 16:(g + 1) * 16]
        par[:, PALOG:PALOG + 16] = np.asarray(mamba_a_log, f32)[0][None, g * 16:(g + 1) * 16]
        par[:, PDD:PDD + 16] = np.asarray(mamba_d, f32)[0][None, g * 16:(g + 1) * 16]
        par[:, PNG:PNG + 1024] = np.asarray(mamba_norm, f32)[0][None, g * 1024:(g + 1) * 1024]
        in_maps.append({"xT": xT_l, "wx": getw(f"wx{g}"), "wdt": getw(f"wdt{g}"), "par": par, "cst": cst, "esel": esel, "idb": idb})
    nc1, _ = build_mamba()
    res = _run(nc1, in_maps)
    Y = np.concatenate([r["y_o"] for r in res], 1)
    del in_maps, res, xT_l

    def tok_inputs(act_bf, h_f32, ktin, c):
        a = np.zeros((1026, act_bf.shape[1]), act_bf.dtype)
        hh = np.zeros((1026, 4096), f32)
        lo = 1024 * c - 2
        s0 = max(lo, 0)
        a[s0 - lo:] = act_bf[s0:1024 * c + 1024]
        hh[s0 - lo:] = h_f32[s0:1024 * c + 1024]
        return _fm(a, ktin), _fm(hh, 32)

    parA = _tok_par(np.asarray(ffn_norm, f32)[0], np.asarray(kv_norm, f32), np.asarray(norm_b, f32)[0],
                    np.asarray(ffn_conv_w, f32)[0], np.asarray(ffn_conv_b, f32)[0])
    in_maps = []
    for c in range(8):
        inT, hT = tok_inputs(Y, x, 64, c)
        in_maps.append({"inT": inT, "hT": hT, "wp": getw("wpA"), "wu": getw("wuA"), "wd": getw("wdA"), "wq": getw("wqA"),
                        "par": parA, "cs": _cs_table(1024 * c)})
    nc2, _ = build_tok("A", 64)
    res = _run(nc2, in_maps)
    h1 = np.concatenate([r["hoT"].transpose(2, 1, 0).reshape(1024, 4096) for r in res], 0)
    Q = np.concatenate([r["q_o"] for r in res], 0)
    KV = np.concatenate([r["kv_o"] for r in res], 0)
    del in_maps, res, Y

    in_maps = []
    for kv in range(8):
        q = Q[:, kv * 512:(kv + 1) * 512]
        qT = np.ascontiguousarray(q.reshape(64, 128, 4, 128).transpose(3, 0, 2, 1).reshape(128, 64, 512))
        kT = np.ascontiguousarray(KV[:, kv * 128:(kv + 1) * 128].T)
        vl = np.ascontiguousarray(KV[:, 1024 + kv * 128:1024 + (kv + 1) * 128].reshape(64, 128, 128).transpose(1, 0, 2))
        in_maps.append({"qT": qT, "kT": kT, "v": vl, "tri": tri})
    nc3, _ = build_attn()
    res = _run(nc3, in_maps)
    O = np.concatenate([r["o_o"] for r in res], 1)
    del in_maps, res, Q, KV

    parB = _tok_par(np.asarray(ffn_norm, f32)[1], np.asarray(final_norm, f32), None,
                    np.asarray(ffn_conv_w, f32)[1], np.asarray(ffn_conv_b, f32)[1])
    in_maps = []
    for c in range(8):
        inT, hT = tok_inputs(O, h1, 32, c)
        in_maps.append({"inT": inT, "hT": hT, "wp": getw("wpB"), "wu": getw("wuB"), "wd": getw("wdB"), "par": parB})
    nc4, _ = build_tok("B", 32)
    res = _run(nc4, in_maps)
    out = np.concatenate([r["hoT"].transpose(2, 1, 0).reshape(1024, 4096) for r in res], 0)
    return np.ascontiguousarray(out.astype(f32))[None]
```

```python
import numpy as np
import ml_dtypes
import concourse.bass as bass
import concourse.mybir as mybir
from concourse.bass_utils import run_bass_kernel_spmd

F32 = mybir.dt.float32
F32R = mybir.dt.float32r
BF16 = mybir.dt.bfloat16
AF = mybir.ActivationFunctionType
ALU = mybir.AluOpType
AX = mybir.AxisListType


class Buf:
    __slots__ = ("t", "name", "last_w", "readers", "dsem", "dcnt")

    def __init__(self, t, name):
        self.t = t
        self.name = name
        self.last_w = None
        self.readers = []
        self.dsem = None
        self.dcnt = 0

    def __getitem__(self, k):
        return self.t[k]


class _Rec:
    def __init__(self):
        self.call = None

    def __getattr__(self, name):
        def f(*a, **k):
            self.call = (name, a, k)
            return None
        return f


class Op:
    __slots__ = ("call", "eng", "fn", "reads", "writes", "is_dma", "owner", "deps", "need_inc", "semval", "idx", "is_final", "raw")


class Prog:
    ENGS = ("pe", "act", "dve", "pool", "sp")

    def __init__(self, nc, same_engine_sync=True):
        self.nc = nc
        self.eng = {"pe": nc.tensor, "act": nc.scalar, "dve": nc.vector, "pool": nc.gpsimd, "sp": nc.sync}
        self.ops = []
        self.same_engine_sync = same_engine_sync
        self.nbuf = 0

    def sbuf(self, shape, dtype, name=None):
        self.nbuf += 1
        name = "s_" + (name or f"sb{self.nbuf}")
        return Buf(self.nc.alloc_sbuf_tensor(name, list(shape), dtype), name)

    def psum(self, shape, dtype=F32, name=None):
        self.nbuf += 1
        name = "p_" + (name or f"ps{self.nbuf}")
        return Buf(self.nc.alloc_psum_tensor(name, list(shape), dtype), name)

    def dram(self, name, shape, dtype, kind="Internal"):
        return Buf(self.nc.dram_tensor(name, list(shape), dtype, kind=kind), name)

    def op(self, eng, fn, reads=(), writes=()):
        o = Op()
        o.eng = eng
        o.fn = None
        r = _Rec()
        fn(r)
        o.call = r.call
        o.reads = tuple(reads)
        o.writes = tuple(writes)
        o.is_dma = False
        o.owner = None
        o.need_inc = False
        o.is_final = False
        o.idx = len(self.ops)
        self.ops.append(o)
        return o

    def dma(self, queue, out_ap, in_ap, reads=(), writes=(), owner=None, **kw):
        o = self.op(queue, lambda e: e.dma_start(out=out_ap, in_=in_ap, **kw), reads, writes)
        o.is_dma = True
        o.owner = owner if owner is not None else (writes[0] if writes else reads[0])
        return o

    def emit(self):
        nc = self.nc
        ops = self.ops
        for i_, o_ in enumerate(ops):
            o_.idx = i_
        for o in ops:
            deps = set()
            raw = set()
            for b in o.reads:
                if b.last_w is not None:
                    deps.add(b.last_w)
                    raw.add(b.last_w)
            o.raw = raw
            for b in o.writes:
                if b.last_w is not None:
                    deps.add(b.last_w)
                deps.update(b.readers)
            for b in o.reads:
                if (not o.is_dma) and b.readers:
                    q = ops[b.readers[-1]]
                    if (not q.is_dma) and q.eng == o.eng:
                        b.readers[-1] = o.idx
                        continue
                b.readers.append(o.idx)
            for b in o.writes:
                b.last_w = o.idx
                b.readers = []
            deps.discard(o.idx)
            if o.is_final:
                last = {}
                for q in ops[:o.idx]:
                    if q.is_dma:
                        deps.add(q.idx)
                    else:
                        last[q.eng] = q.idx
                deps.update(last.values())
            o.deps = deps
            for d in deps:
                p = ops[d]
                if p.is_dma:
                    continue
                if p.eng != o.eng or (self.same_engine_sync and o.eng != "pe" and d in o.raw):
                    p.need_inc = True
                if p.eng == o.eng and o.is_dma:
                    p.need_inc = True
        esem = {e: nc.alloc_semaphore(f"es_{e}") for e in self.ENGS}
        ecnt = {e: 0 for e in self.ENGS}
        for o in ops:
            if o.is_dma:
                b = o.owner
                if b.dsem is None:
                    b.dsem = nc.alloc_semaphore(f"ds_{b.name}")
                b.dcnt += 16
                o.semval = (b.dsem, b.dcnt)
            elif o.need_inc:
                ecnt[o.eng] += 1
                o.semval = (esem[o.eng], ecnt[o.eng])
            else:
                o.semval = None
        waited = {e: {} for e in self.ENGS}
        nwaits = 0
        for o in ops:
            e = self.eng[o.eng]
            w = waited[o.eng]
            need = {}
            for d in o.deps:
                p = ops[d]
                if p.semval is None:
                    continue
                if (not p.is_dma) and p.eng == o.eng and not (o.is_dma or (self.same_engine_sync and o.eng != "pe" and d in o.raw)):
                    continue
                s, v = p.semval
                key = id(s)
                if w.get(key, 0) >= v:
                    continue
                if key not in need or need[key][1] < v:
                    need[key] = (s, v)
            for key, (s, v) in need.items():
                e.wait_ge(s, v)
                w[key] = v
                nwaits += 1
            ins = None
            if o.call is not None:
                nm, a_, k_ = o.call
                ins = getattr(e, nm)(*a_, **k_)
            if o.semval is not None and ins is not None:
                ins.then_inc(o.semval[0], 16 if o.is_dma else 1)
        self.final = {"esem": esem, "ecnt": ecnt, "nwaits": nwaits}
        return self.final

    def finish(self, eng="sp"):
        o = self.op(eng, lambda e: None)
        o.is_final = True


D = 4096
DT = 32
FF = 11008
FT = 86
TP = 512
EPS = 1e-5
WU = 43 * 128


class WStream:
    def __init__(self, P, nbuf, unit_cols, queue="sp", pf=3):
        self.P = P
        self.bufs = [P.sbuf([128, unit_cols], BF16, f"wp{i}") for i in range(nbuf)]
        self.reqs = []
        self.issued = 0
        self.taken = 0
        self.queue = queue
        self.pf = pf

    def add(self, dbuf, ap, ncols):
        self.reqs.append((dbuf, ap, ncols))

    def _issue(self, i):
        dbuf, ap, ncols = self.reqs[i]
        b = self.bufs[i % len(self.bufs)]
        self.P.dma(self.queue, b[:, 0:ncols], ap, reads=[dbuf], writes=[b])

    def next(self):
        i = self.taken
        lim = min(len(self.reqs), i + self.pf + 1)
        while self.issued < lim:
            self._issue(self.issued)
            self.issued += 1
        self.taken += 1
        return self.bufs[i % len(self.bufs)]


def build_tok(mode, KTIN):
    nc = bass.Bass("TRN2", target_bir_lowering=False)
    P = Prog(nc)
    NTOK = 1026
    inT = P.dram("inT", [128, KTIN, NTOK], BF16, kind="ExternalInput")
    hT = P.dram("hT", [128, DT, NTOK], F32, kind="ExternalInput")
    wp_d = P.dram("wp", [DT, 128, KTIN, 128], BF16, kind="ExternalInput")
    wu_d = P.dram("wu", [FT, 128, 2, DT, 128], BF16, kind="ExternalInput")
    wd_d = P.dram("wd", [DT, 128, FT, 128], BF16, kind="ExternalInput")
    par_d = P.dram("par", [128, 4 * DT + 172 * 4], F32, kind="ExternalInput")
    if mode == "A":
        wq_d = P.dram("wq", [12, 128, DT, 512], BF16, kind="ExternalInput")
        cs_d = P.dram("cs", [128, 8, 128], F32, kind="ExternalInput")
        q_o = P.dram("q_o", [1024, 4096], BF16, kind="ExternalOutput")
        kv_o = P.dram("kv_o", [1024, 2048], BF16, kind="ExternalOutput")
    hoT = P.dram("hoT", [128, DT, 1024], F32, kind="ExternalOutput")
    hmid_d = P.dram("hmid", [128, DT, 1024], F32)
    if mode == "B":
        hout_d = P.dram("hout", [128, DT, 1024], F32)
    else:
        hout_d = hoT
    hmid_v = [Buf(hmid_d.t, f"hmid{i}") for i in range(DT)]
    hout_v = [Buf(hout_d.t, f"hout{i}") for i in range(DT)]

    par = P.sbuf([128, 4 * DT + 172 * 4], F32, "par")
    G_FFN, G_2, G_3, CW, CB = 0, DT, 2 * DT, 4 * DT, 4 * DT + 172 * 3
    big = P.sbuf([128, FT * TP], BF16, "big")
    hn = P.sbuf([128, DT * TP], BF16, "hn")
    ones = P.sbuf([128, 128], BF16, "ones")
    rstd = P.sbuf([128, TP], F32, "rstd")
    tails = P.sbuf([128, 172 * 2], F32, "tails")
    ws = WStream(P, 4, WU, "sp", pf=3)
    stg = [P.sbuf([128, TP], F32, f"stg{i}") for i in range(3)]
    sqb = [P.sbuf([128, TP], BF16, f"sq{i}") for i in range(2)]
    ub = [P.sbuf([128, 2 * (TP + 2)], F32, f"ub{i}") for i in range(2)]
    cv = [P.sbuf([128, 2 * TP], F32, f"cv{i}") for i in range(2)]
    ps = [P.psum([128, 512], F32, f"psb{i}") for i in range(8)]
    PS_MM = (ps[0], ps[1])
    PS_UP = ((ps[2], ps[3]), (ps[4], ps[5]))
    PS_SS = ps[6]
    PS_TM = ps[7]
    if mode == "A":
        cs = P.sbuf([128, 8 * 128], F32, "cs")
        rs_tm = P.sbuf([128, 4], F32, "rs_tm")
        qst = cv
        qrt = ub
        qob = [P.sbuf([128, 512], BF16, f"qob{i}") for i in range(2)]
        onecol = P.sbuf([128, 1], F32, "onecol")

    P.dma("pool", par[:], par_d[:], reads=[par_d], writes=[par])
    P.op("pool", lambda e: e.memset(ones[:], 1.0), writes=[ones])
    P.op("pool", lambda e: e.memset(tails[:], 0.0), writes=[tails])
    if mode == "A":
        P.dma("pool", cs[:], cs_d[:].rearrange("p a b -> p (a b)"), reads=[cs_d], writes=[cs])
        P.op("pool", lambda e: e.memset(onecol[:], 1.0), writes=[onecol])

    passes = [(0, 2, True), (2, TP, False), (2 + TP, TP, False)]
    for (c0, n, halo) in passes:
        for dt in range(DT):
            for h0 in range(0, KTIN, 32):
                ws.add(wp_d, wp_d[dt, :, h0:h0 + 32, :].rearrange("p a b -> p (a b)"), 32 * 128)
        for j in range(FT):
            for half in range(2):
                ws.add(wu_d, wu_d[j, :, half].rearrange("p b c -> p (b c)"), DT * 128)
        if halo:
            continue
        for dt in range(DT):
            for h0 in range(0, FT, 43):
                ws.add(wd_d, wd_d[dt, :, h0:h0 + 43, :].rearrange("p a b -> p (a b)"), 43 * 128)
        if mode == "A":
            for ch in range(12):
                for k0 in range(0, DT, 8):
                    ws.add(wq_d, wq_d[ch, :, k0:k0 + 8, :].rearrange("p a b -> p (a b)"), 8 * 512)

    cnt = {"mm": 0, "up": 0, "stg": 0, "sq": 0, "q": 0}

    def rstd_from(ssq_ps, out_ap, n, eng_cols):
        P.op("act", lambda e: e.activation(out_ap, ssq_ps[:, 0:n], AF.Sqrt, bias=epsb[:, 0:1], scale=1.0 / D),
             reads=[ssq_ps, epsb], writes=[rstd])
        P.op("dve", lambda e: e.reciprocal(out_ap, out_ap), reads=[rstd], writes=[rstd])

    epsb = P.sbuf([128, 1], F32, "epsb")
    P.op("pool", lambda e: e.memset(epsb[:], EPS), writes=[epsb])

    for pi, (c0, n, halo) in enumerate(passes):
        o0 = c0 - 2
        P.dma("sp", big[:, 0:KTIN * TP].rearrange("p (k t) -> p k t", t=TP)[:, :, 0:n], inT[:, :, c0:c0 + n],
              reads=[inT], writes=[big])
        for dt in range(DT):
            pb = PS_MM[cnt["mm"] % 2]; cnt["mm"] += 1
            for h0 in range(0, KTIN, 32):
                w = ws.next()
                for k in range(32):
                    kk = h0 + k
                    P.op("pe", (lambda e, w=w, k=k, kk=kk, pb=pb: e.matmul(
                        pb[:, 0:n], w[:, k * 128:(k + 1) * 128], big[:, kk * TP:kk * TP + n],
                        start=(kk == 0), stop=(kk == KTIN - 1))), reads=[w, big], writes=[pb])
            sg = stg[cnt["stg"] % 3]; cnt["stg"] += 1
            P.dma("pool", sg[:, 0:n], hT[:, dt, c0:c0 + n], reads=[hT], writes=[sg])
            P.op("dve", (lambda e, sg=sg, pb=pb: e.tensor_tensor(sg[:, 0:n], pb[:, 0:n], sg[:, 0:n], ALU.add)),
                 reads=[pb, sg], writes=[sg])
            if not halo:
                P.dma("pool", hmid_d[:, dt, o0:o0 + n], sg[:, 0:n], reads=[sg], writes=[hmid_v[dt]], owner=sg)
            sq = sqb[cnt["sq"] % 2]; cnt["sq"] += 1
            P.op("act", (lambda e, sg=sg, sq=sq: e.activation(sq[:, 0:n], sg[:, 0:n], AF.Square)), reads=[sg], writes=[sq])
            P.op("pe", (lambda e, sq=sq, dt=dt: e.matmul(PS_SS[:, 0:n], ones[:], sq[:, 0:n], start=(dt == 0), stop=(dt == DT - 1))),
                 reads=[ones, sq], writes=[PS_SS])
            P.op("dve", (lambda e, sg=sg, dt=dt: e.tensor_scalar(hn[:, dt * TP:dt * TP + n], sg[:, 0:n],
                                                              par[:, G_FFN + dt:G_FFN + dt + 1], None, ALU.mult)),
                 reads=[sg, par], writes=[hn])
        rstd_from(PS_SS, rstd[:, 0:n], n, 128)
        for j in range(FT):
            pg, pv = PS_UP[cnt["up"] % 2]
            u = ub[cnt["up"] % 2]
            c = cv[cnt["up"] % 2]
            cnt["up"] += 1
            for half, pb in ((0, pg), (1, pv)):
                w = ws.next()
                for k in range(DT):
                    P.op("pe", (lambda e, w=w, k=k, half=half, pb=pb: e.matmul(
                        pb[:, 0:n], w[:, k * 128:(k + 1) * 128], hn[:, k * TP:k * TP + n],
                        start=(k == 0), stop=(k == DT - 1))), reads=[w, hn], writes=[pb])
            ug = u[:, 0:TP + 2]
            uv = u[:, TP + 2:2 * TP + 4]
            tg = j
            tv = FT + j
            P.op("dve", (lambda e, pg=pg, ug=ug: e.tensor_tensor(ug[:, 2:2 + n], pg[:, 0:n], rstd[:, 0:n], ALU.mult)),
                 reads=[pg, rstd], writes=[u])
            P.op("dve", (lambda e, pv=pv, uv=uv: e.tensor_tensor(uv[:, 2:2 + n], pv[:, 0:n], rstd[:, 0:n], ALU.mult)),
                 reads=[pv, rstd], writes=[u])
            if halo:
                P.op("pool", (lambda e, ug=ug, tg=tg: e.tensor_copy(tails[:, 2 * tg:2 * tg + 2], ug[:, 2:4])), reads=[u], writes=[tails])
                P.op("pool", (lambda e, uv=uv, tv=tv: e.tensor_copy(tails[:, 2 * tv:2 * tv + 2], uv[:, 2:4])), reads=[u], writes=[tails])
                continue
            P.op("pool", (lambda e, ug=ug, tg=tg: e.tensor_copy(ug[:, 0:2], tails[:, 2 * tg:2 * tg + 2])), reads=[tails], writes=[u])
            P.op("pool", (lambda e, uv=uv, tv=tv: e.tensor_copy(uv[:, 0:2], tails[:, 2 * tv:2 * tv + 2])), reads=[tails], writes=[u])
            P.op("pool", (lambda e, ug=ug, tg=tg: e.tensor_copy(tails[:, 2 * tg:2 * tg + 2], ug[:, n:n + 2])), reads=[u], writes=[tails])
            P.op("pool", (lambda e, uv=uv, tv=tv: e.tensor_copy(tails[:, 2 * tv:2 * tv + 2], uv[:, n:n + 2])), reads=[u], writes=[tails])
            cg = c[:, 0:TP]
            cvv = c[:, TP:2 * TP]
            for (uu, cc, t) in ((ug, cg, tg), (uv, cvv, tv)):
                P.op("dve", (lambda e, uu=uu, cc=cc, t=t: e.tensor_scalar(
                    cc[:, 0:n], uu[:, 0:n], par[:, CW + 3 * t:CW + 3 * t + 1], par[:, CB + t:CB + t + 1], ALU.mult, ALU.add)),
                    reads=[u, par], writes=[c])
                for kk in (1, 2):
                    P.op("dve", (lambda e, uu=uu, cc=cc, t=t, kk=kk: e.scalar_tensor_tensor(
                        cc[:, 0:n], uu[:, kk:kk + n], par[:, CW + 3 * t + kk:CW + 3 * t + kk + 1], cc[:, 0:n], ALU.mult, ALU.add)),
                        reads=[u, par, c], writes=[c])
            P.op("act", (lambda e, cg=cg: e.activation(cg[:, 0:n], cg[:, 0:n], AF.Silu)), reads=[c], writes=[c])
            P.op("dve", (lambda e, cg=cg, cvv=cvv, j=j: e.tensor_tensor(big[:, j * TP:j * TP + n], cg[:, 0:n], cvv[:, 0:n], ALU.mult)),
                 reads=[c], writes=[big])
        if halo:
            continue
        for dt in range(DT):
            pb = PS_MM[cnt["mm"] % 2]; cnt["mm"] += 1
            for h0 in range(0, FT, 43):
                w = ws.next()
                for k in range(43):
                    kk = h0 + k
                    P.op("pe", (lambda e, w=w, k=k, kk=kk, pb=pb: e.matmul(
                        pb[:, 0:n], w[:, k * 128:(k + 1) * 128], big[:, kk * TP:kk * TP + n],
                        start=(kk == 0), stop=(kk == FT - 1))), reads=[w, big], writes=[pb])
            sg = stg[cnt["stg"] % 3]; cnt["stg"] += 1
            P.dma("pool", sg[:, 0:n], hmid_d[:, dt, o0:o0 + n], reads=[hmid_v[dt]], writes=[sg])
            P.op("dve", (lambda e, sg=sg, pb=pb: e.tensor_tensor(sg[:, 0:n], pb[:, 0:n], sg[:, 0:n], ALU.add)),
                 reads=[pb, sg], writes=[sg])
            P.dma("pool", hout_d[:, dt, o0:o0 + n], sg[:, 0:n], reads=[sg], writes=[hout_v[dt]], owner=sg)
            sq = sqb[cnt["sq"] % 2]; cnt["sq"] += 1
            P.op("act", (lambda e, sg=sg, sq=sq: e.activation(sq[:, 0:n], sg[:, 0:n], AF.Square)), reads=[sg], writes=[sq])
            P.op("pe", (lambda e, sq=sq, dt=dt: e.matmul(PS_SS[:, 0:n], ones[:], sq[:, 0:n], start=(dt == 0), stop=(dt == DT - 1))),
                 reads=[ones, sq], writes=[PS_SS])
            if mode == "A":
                P.op("dve", (lambda e, sg=sg, dt=dt: e.tensor_scalar(hn[:, dt * TP:dt * TP + n], sg[:, 0:n],
                                                                  par[:, G_2 + dt:G_2 + dt + 1], None, ALU.mult)),
                     reads=[sg, par], writes=[hn])
        rstd_from(PS_SS, rstd[:, 0:n], n, 128)
        if mode == "B":
            for dt in range(DT):
                sg = stg[cnt["stg"] % 3]; cnt["stg"] += 1
                P.dma("pool", sg[:, 0:n], hout_d[:, dt, o0:o0 + n], reads=[hout_v[dt]], writes=[sg])
                P.op("dve", (lambda e, sg=sg, dt=dt: e.scalar_tensor_tensor(sg[:, 0:n], sg[:, 0:n], par[:, G_2 + dt:G_2 + dt + 1],
                                                                         rstd[:, 0:n], ALU.mult, ALU.mult)),
                     reads=[sg, par, rstd], writes=[sg])
                P.dma("pool", hoT[:, dt, o0:o0 + n], sg[:, 0:n], reads=[sg], writes=[], owner=sg)
        else:
            for tt in range(4):
                P.op("pe", (lambda e, tt=tt: e.matmul(PS_TM[:, tt:tt + 1], rstd[0:1, tt * 128:(tt + 1) * 128], onecol[0:1, 0:1],
                                                      start=True, stop=True)), reads=[rstd, onecol], writes=[PS_TM])
            P.op("act", lambda e: e.copy(rs_tm[:, 0:4], PS_TM[:, 0:4]), reads=[PS_TM], writes=[rs_tm])
            for dt in range(DT):
                sg = stg[cnt["stg"] % 3]; cnt["stg"] += 1
                P.dma("pool", sg[:, 0:n], hout_d[:, dt, o0:o0 + n], reads=[hout_v[dt]], writes=[sg])
                P.op("dve", (lambda e, sg=sg, dt=dt: e.tensor_scalar(big[:, dt * TP:dt * TP + n], sg[:, 0:n],
                                                                  par[:, G_3 + dt:G_3 + dt + 1], None, ALU.mult)),
                     reads=[sg, par], writes=[big])
            for ch in range(12):
                src = big if ch < 8 else hn
                wl = [ws.next() for _ in range(4)] if False else None
                for tt in range(4):
                    pass
                pbs = [ps[0], ps[1], ps[2], ps[3]]
                for piece in range(4):
                    w = ws.next()
                    for tt in range(4):
                        for k in range(8):
                            kk = piece * 8 + k
                            P.op("pe", (lambda e, w=w, k=k, kk=kk, tt=tt, src=src: e.matmul(
                                pbs[tt][:, 0:512], src[:, kk * TP + tt * 128:kk * TP + (tt + 1) * 128], w[:, k * 512:(k + 1) * 512],
                                start=(kk == 0), stop=(kk == DT - 1))), reads=[w, src], writes=[pbs[tt]])
                for tt in range(4):
                    tile_idx = (o0 // 128) + tt
                    qs = qst[cnt["q"] % 2]; qr = qrt[cnt["q"] % 2]; qo = qob[cnt["q"] % 2]; cnt["q"] += 1
                    rope = ch < 10
                    if rope:
                        P.op("act", (lambda e, qs=qs, tt=tt: e.activation(qs[:, 0:512], pbs[tt][:, :], AF.Copy, scale=rs_tm[:, tt:tt + 1])),
                             reads=[pbs[tt], rs_tm], writes=[qs])
                        q3 = qs[:, 0:512].rearrange("p (h two d) -> p h two d", two=2, d=64)
                        r3 = qr[:, 0:512].rearrange("p (h two d) -> p h two d", two=2, d=64)
                        o3 = qo[:, :].rearrange("p (h two d) -> p h two d", two=2, d=64)
                        cosb = cs[:, tile_idx * 128:tile_idx * 128 + 64]
                        sinb = cs[:, tile_idx * 128 + 64:tile_idx * 128 + 128]
                        for hh in range(4):
                            P.op("pool", (lambda e, hh=hh, q3=q3, r3=r3, sinb=sinb: e.tensor_tensor(r3[:, hh, 0, :], q3[:, hh, 1, :], sinb, ALU.mult)),
                                 reads=[qs, cs], writes=[qr])
                            P.op("pool", (lambda e, hh=hh, q3=q3, r3=r3, sinb=sinb: e.tensor_tensor(r3[:, hh, 1, :], q3[:, hh, 0, :], sinb, ALU.mult)),
                                 reads=[qs, cs], writes=[qr])
                            P.op("dve", (lambda e, hh=hh, q3=q3, cosb=cosb: e.tensor_tensor(q3[:, hh, 0, :], q3[:, hh, 0, :], cosb, ALU.mult)),
                                 reads=[qs, cs], writes=[qs])
                            P.op("dve", (lambda e, hh=hh, q3=q3, cosb=cosb: e.tensor_tensor(q3[:, hh, 1, :], q3[:, hh, 1, :], cosb, ALU.mult)),
                                 reads=[qs, cs], writes=[qs])
                            P.op("dve", (lambda e, hh=hh, q3=q3, r3=r3, o3=o3: e.tensor_tensor(o3[:, hh, 0, :], q3[:, hh, 0, :], r3[:, hh, 0, :], ALU.subtract)),
                                 reads=[qs, qr], writes=[qo])
                            P.op("dve", (lambda e, hh=hh, q3=q3, r3=r3, o3=o3: e.tensor_tensor(o3[:, hh, 1, :], q3[:, hh, 1, :], r3[:, hh, 1, :], ALU.add)),
                                 reads=[qs, qr], writes=[qo])
                    else:
                        P.op("act", (lambda e, qo=qo, tt=tt: e.activation(qo[:, :], pbs[tt][:, :], AF.Copy, scale=rs_tm[:, tt:tt + 1])),
                             reads=[pbs[tt], rs_tm], writes=[qo])
                    r0 = o0 + tt * 128
                    if ch < 8:
                        P.dma("pool", q_o[r0:r0 + 128, ch * 512:(ch + 1) * 512], qo[:, :], reads=[qo], writes=[], owner=qo)
                    else:
                        P.dma("pool", kv_o[r0:r0 + 128, (ch - 8) * 512:(ch - 7) * 512], qo[:, :], reads=[qo], writes=[], owner=qo)
    P.finish("sp")
    info = P.emit()
    return nc, info


D = 4096
KT = 32
ST = 512
NTOK = 8192
EPS = 1e-5
NH = 16
XQ = "sp"
BCE = "pool"
PG, PCW, PCB, PDTB, PALOG, PDD, PNG = 0, 32, 32 + 40, 32 + 50, 32 + 66, 32 + 82, 32 + 98
NPAR = 32 + 98 + 1024


def build_mamba(n_super=NTOK // ST, debug=False):
    nc = bass.Bass("TRN2", target_bir_lowering=False)
    P = Prog(nc)
    xT = P.dram("xT", [128, KT, NTOK], F32, kind="ExternalInput")
    wx_d = P.dram("wx", [18, 128, KT, 128], BF16, kind="ExternalInput")
    wdt_d = P.dram("wdt", [128, KT * 16], BF16, kind="ExternalInput")
    par_d = P.dram("par", [128, NPAR], F32, kind="ExternalInput")
    cst_d = P.dram("cst", [128, 256], F32, kind="ExternalInput")
    esel_d = P.dram("esel", [16, 2048], F32, kind="ExternalInput")
    idb_d = P.dram("idb", [128, 128], BF16, kind="ExternalInput")
    y_o = P.dram("y_o", [NTOK, 1024], BF16, kind="ExternalOutput")

    par = P.sbuf([128, NPAR], F32, "par")
    cst = P.sbuf([128, 256], F32, "cst")
    esel = P.sbuf([16, 2048], F32, "esel")
    idb = P.sbuf([128, 128], BF16, "idb")
    wdt = P.sbuf([128, KT * 16], BF16, "wdt")
    ones = P.sbuf([128, 128], BF16, "ones")
    onecol = P.sbuf([128, 1], F32, "onecol")
    epsb = P.sbuf([128, 1], F32, "epsb")
    Arep = P.sbuf([128, NH], F32, "Arep")
    uT = P.sbuf([128, KT * ST], BF16, "uT")
    xbcT2 = [P.sbuf([128, 10 * ST], BF16, f"xbcT{i}") for i in range(2)]
    szT2 = [P.sbuf([128, 8 * ST], BF16, f"szT{i}") for i in range(2)]
    rstd = P.sbuf([128, ST], F32, "rstd")
    rs_tm = P.sbuf([128, 4], F32, "rs_tm")
    tails = P.sbuf([128, 30], F32, "tails")
    S = P.sbuf([128, 1024], F32, "S")
    S_bf = P.sbuf([128, 1024], BF16, "S_bf")
    ws = WStream(P, 4, KT * 128, "sp", pf=3)
    xs = [P.sbuf([128, ST], F32, f"xs{i}") for i in range(3)]
    sqb = [P.sbuf([128, ST], BF16, f"sq{i}") for i in range(2)]
    raw = [P.sbuf([128, ST + 3], F32, f"raw{i}") for i in range(2)]
    acc = [P.sbuf([128, ST], F32, f"acc{i}") for i in range(2)]
    Xtm2 = [P.sbuf([128, 1024], BF16, f"Xtm{i}") for i in range(2)]
    Btm2 = [P.sbuf([128, 128], BF16, f"Btm{i}") for i in range(2)]
    Xdt2 = [P.sbuf([128, 1024], BF16, f"Xdt{i}") for i in range(2)]
    Xdte2 = [P.sbuf([128, 1024], BF16, f"Xdte{i}") for i in range(2)]
    arg = [P.sbuf([128, 512], F32, f"arg{i}") for i in range(2)]
    Ldec = [P.sbuf([128, 512], F32, f"Ldec{i}") for i in range(2)]
    MT2 = [P.sbuf([128, 2048], BF16, f"MT{i}") for i in range(2)]
    yd_sb2 = [P.sbuf([128, 1024], F32, f"yd_sb{i}") for i in range(2)]
    yb = P.sbuf([128, 1024], F32, "yb")
    yg = P.sbuf([128, 1024], F32, "yg")
    junk = P.sbuf([128, 1024], BF16, "junk")
    yn = [P.sbuf([128, 1024], BF16, f"yn{i}") for i in range(2)]
    dtv2 = [P.sbuf([128, NH], F32, f"dtv{i}") for i in range(2)]
    e1 = P.sbuf([128, NH], F32, "e1")
    av = P.sbuf([128, NH], F32, "av")
    acs_sb2 = [P.sbuf([128, NH], F32, f"acs_sb{i}") for i in range(2)]
    acsT_sb = P.sbuf([16, 128], F32, "acsT_sb")
    ea2 = [P.sbuf([128, NH], F32, f"ea{i}") for i in range(2)]
    cdb2 = [P.sbuf([128, NH], F32, f"cdb{i}") for i in range(2)]
    dearg = P.sbuf([128, NH], F32, "dearg")
    de2 = [P.sbuf([128, NH], F32, f"de{i}") for i in range(2)]
    ss = P.sbuf([128, 1], F32, "ss")
    rsn = P.sbuf([128, 1], F32, "rsn")

    b0 = P.psum([128, 1024], BF16, "b0")
    b1 = P.psum([128, 1024], BF16, "b1")
    b2 = P.psum([128, 512], F32, "b2")
    b3 = P.psum([128, 512], F32, "b3")
    pA = [P.psum([128, 512], F32, f"pA{i}") for i in range(2)]
    pB = [P.psum([128, 512], F32, f"pB{i}") for i in range(2)]

    T = cst[:, 0:128]
    maskb = cst[:, 128:256]

    P.dma("pool", par[:], par_d[:], reads=[par_d], writes=[par])
    P.dma("pool", cst[:], cst_d[:], reads=[cst_d], writes=[cst])
    P.dma("pool", esel[:], esel_d[:], reads=[esel_d], writes=[esel])
    P.dma("pool", idb[:], idb_d[:], reads=[idb_d], writes=[idb])
    P.dma("pool", wdt[:], wdt_d[:], reads=[wdt_d], writes=[wdt])
    P.op("pool", lambda e: e.memset(ones[:], 1.0), writes=[ones])
    P.op("pool", lambda e: e.memset(onecol[:], 1.0), writes=[onecol])
    P.op("pool", lambda e: e.memset(epsb[:], EPS), writes=[epsb])
    P.op("pool", lambda e: e.memset(tails[:], 0.0), writes=[tails])
    P.op("pool", lambda e: e.memset(S[:], 0.0), writes=[S])
    P.op("pool", lambda e: e.memset(S_bf[:], 0.0), writes=[S_bf])
    P.op("act", lambda e: e.activation(Arep[:], par[:, PALOG:PALOG + NH], AF.Exp), reads=[par], writes=[Arep])
    P.op("dve", lambda e: e.tensor_scalar(Arep[:], Arep[:], -1.0, None, ALU.mult), reads=[Arep], writes=[Arep])

    for st in range(n_super):
        for j in range(18):
            ws.add(wx_d, wx_d[j].rearrange("p a b -> p (a b)"), KT * 128)

    cnt = {"xs": 0, "sq": 0, "t": 0, "g": 0, "yn": 0}
    pending = []
    for st in range(n_super):
        xbcT = xbcT2[st % 2]
        szT = szT2[st % 2]
        t0 = st * ST
        for kt in range(KT):
            x_ = xs[cnt["xs"] % 3]; cnt["xs"] += 1
            P.dma(XQ, x_[:], xT[:, kt, t0:t0 + ST], reads=[xT], writes=[x_])
            sq = sqb[cnt["sq"] % 2]; cnt["sq"] += 1
            P.op("act", (lambda e, x_=x_, sq=sq: e.activation(sq[:], x_[:], AF.Square)), reads=[x_], writes=[sq])
            P.op("pe", (lambda e, sq=sq, kt=kt: e.matmul(b3[:], ones[:], sq[:], start=(kt == 0), stop=(kt == KT - 1))),
                 reads=[ones, sq], writes=[b3])
            P.op("dve", (lambda e, x_=x_, kt=kt: e.tensor_scalar(uT[:, kt * ST:(kt + 1) * ST], x_[:], par[:, PG + kt:PG + kt + 1], None, ALU.mult)),
                 reads=[x_, par], writes=[uT])
        P.op("act", lambda e: e.activation(rstd[:], b3[:], AF.Ln, bias=epsb[:, 0:1], scale=1.0 / D), reads=[b3, epsb], writes=[rstd])
        P.op("act", lambda e: e.activation(rstd[:], rstd[:], AF.Exp, scale=-0.5), reads=[rstd], writes=[rstd])
        for tt in range(4):
            P.op("pe", (lambda e, tt=tt: e.matmul(b2[:, 300 + tt:301 + tt], rstd[0:1, tt * 128:(tt + 1) * 128], onecol[0:1, 0:1],
                                                  start=True, stop=True)), reads=[rstd, onecol], writes=[b2])
        P.op("act", lambda e: e.copy(rs_tm[:, 0:4], b2[:, 300:304]), reads=[b2], writes=[rs_tm])
        for j in range(18):
            pb = pA[cnt["t"] % 2]
            rw = raw[cnt["t"] % 2]
            ac = acc[cnt["t"] % 2]
            cnt["t"] += 1
            w = ws.next()
            for k in range(KT):
                P.op("pe", (lambda e, w=w, k=k, pb=pb: e.matmul(pb[:], w[:, k * 128:(k + 1) * 128], uT[:, k * ST:(k + 1) * ST],
                                                                start=(k == 0), stop=(k == KT - 1))), reads=[w, uT], writes=[pb])
            if j < 8:
                P.op("dve", (lambda e, pb=pb, ac=ac: e.tensor_tensor(ac[:], pb[:], rstd[:], ALU.mult)), reads=[pb, rstd], writes=[ac])
                P.op("act", (lambda e, ac=ac, j=j: e.activation(szT[:, j * ST:(j + 1) * ST], ac[:], AF.Silu)), reads=[ac], writes=[szT])
                continue
            c = j - 8
            P.op("dve", (lambda e, pb=pb, rw=rw: e.tensor_tensor(rw[:, 3:3 + ST], pb[:], rstd[:], ALU.mult)), reads=[pb, rstd], writes=[rw])
            P.op("pool", (lambda e, rw=rw, c=c: e.tensor_copy(rw[:, 0:3], tails[:, 3 * c:3 * c + 3])), reads=[tails], writes=[rw])
            P.op("pool", (lambda e, rw=rw, c=c: e.tensor_copy(tails[:, 3 * c:3 * c + 3], rw[:, ST:ST + 3])), reads=[rw], writes=[tails])
            P.op("dve", (lambda e, rw=rw, ac=ac, c=c: e.tensor_scalar(ac[:], rw[:, 0:ST], par[:, PCW + 4 * c:PCW + 4 * c + 1],
                                                                    par[:, PCB + c:PCB + c + 1], ALU.mult, ALU.add)),
                 reads=[rw, par], writes=[ac])
            for kk in (1, 2, 3):
                P.op("dve", (lambda e, rw=rw, ac=ac, c=c, kk=kk: e.scalar_tensor_tensor(
                    ac[:], rw[:, kk:kk + ST], par[:, PCW + 4 * c + kk:PCW + 4 * c + kk + 1], ac[:], ALU.mult, ALU.add)),
                    reads=[rw, par, ac], writes=[ac])
            P.op("act", (lambda e, ac=ac, c=c: e.activation(xbcT[:, c * ST:(c + 1) * ST], ac[:], AF.Silu)), reads=[ac], writes=[xbcT])
        def stage1(st, tt, par_i, xbcT, szT):
            Xtm, Btm, Xdt, Xdte, MT, yd_sb = Xtm2[par_i], Btm2[par_i], Xdt2[par_i], Xdte2[par_i], MT2[par_i], yd_sb2[par_i]
            dtv, acs_sb, ea, cdb, de = dtv2[par_i], acs_sb2[par_i], ea2[par_i], cdb2[par_i], de2[par_i]
            c0 = tt * 128
            BT = xbcT[:, 8 * ST + c0:8 * ST + c0 + 128]
            CT = xbcT[:, 9 * ST + c0:9 * ST + c0 + 128]
            for k in range(KT):
                P.op("pe", (lambda e, k=k, c0=c0: e.matmul(b2[:, 272:288], uT[:, k * ST + c0:k * ST + c0 + 128], wdt[:, k * 16:(k + 1) * 16],
                                                           start=(k == 0), stop=(k == KT - 1))), reads=[uT, wdt], writes=[b2])
            P.op("dve", (lambda e, tt=tt: e.scalar_tensor_tensor(dtv[:], b2[:, 272:288], rs_tm[:, tt:tt + 1], par[:, PDTB:PDTB + NH], ALU.mult, ALU.add)),
                 reads=[b2, rs_tm, par], writes=[dtv])
            P.op("act", lambda e: e.activation(e1[:], dtv[:], AF.Exp), reads=[dtv], writes=[e1])
            P.op("act", lambda e: e.activation(dtv[:], e1[:], AF.Ln, bias=onecol[:, 0:1]), reads=[e1, onecol], writes=[dtv])
            P.op("dve", lambda e: e.tensor_tensor(av[:], dtv[:], Arep[:], ALU.mult), reads=[dtv, Arep], writes=[av])
            P.op("pe", lambda e: e.matmul(b2[:, 128:144], T, av[:], start=True, stop=True), reads=[cst, av], writes=[b2])
            P.op("pe", lambda e: e.matmul(b2[0:16, 144:272], av[:], T, start=True, stop=True), reads=[cst, av], writes=[b2])
            P.op("act", lambda e: e.copy(acs_sb[:], b2[:, 128:144]), reads=[b2], writes=[acs_sb])
            P.op("act", lambda e: e.copy(acsT_sb[:], b2[0:16, 144:272]), reads=[b2], writes=[acsT_sb])
            P.op("act", lambda e: e.activation(ea[:], acs_sb[:], AF.Exp), reads=[acs_sb], writes=[ea])
            for j in range(8):
                P.op("pe", (lambda e, j=j, c0=c0: e.transpose(b0[:, j * 128:(j + 1) * 128], xbcT[:, j * ST + c0:j * ST + c0 + 128], idb[:])),
                     reads=[xbcT, idb], writes=[b0])
            P.op("act", lambda e: e.copy(Xtm[:], b0[:]), reads=[b0], writes=[Xtm])
            P.op("pe", (lambda e, BT=BT: e.transpose(b0[:, 0:128], BT, idb[:])), reads=[xbcT, idb], writes=[b0])
            P.op("act", lambda e: e.copy(Btm[:], b0[:, 0:128]), reads=[b0], writes=[Btm])
            P.op("pe", (lambda e, BT=BT, CT=CT: e.matmul(b2[:, 0:128], BT, CT, start=True, stop=True)), reads=[xbcT], writes=[b2])
            for g in range(4):
                ar = arg[cnt["g"] % 2]; Ld = Ldec[cnt["g"] % 2]; cnt["g"] += 1
                for hh in range(4):
                    h = 4 * g + hh
                    P.op("pe", (lambda e, h=h, hh=hh: e.matmul(b3[:, hh * 128:(hh + 1) * 128], esel[0:16, h * 128:(h + 1) * 128], acsT_sb[:],
                                                              start=True, stop=True)), reads=[esel, acsT_sb], writes=[b3])
                b3v = b3[:, :].rearrange("p (h l) -> p h l", l=128)[:, :, 127]
                P.op("act", (lambda e, g=g, b3v=b3v: e.activation(cdb[:, 4 * g:4 * g + 4], b3v, AF.Exp)), reads=[b3], writes=[cdb])
                P.op("dve", (lambda e, g=g, b3v=b3v: e.tensor_tensor(dearg[:, 4 * g:4 * g + 4], b3v, acs_sb[:, 4 * g:4 * g + 4], ALU.subtract)),
                     reads=[b3, acs_sb], writes=[dearg])
                for hh in range(4):
                    h = 4 * g + hh
                    P.op("dve", (lambda e, h=h, hh=hh, ar=ar: e.scalar_tensor_tensor(
                        ar[:, hh * 128:(hh + 1) * 128], b3[:, hh * 128:(hh + 1) * 128], acs_sb[:, h:h + 1], maskb, ALU.subtract, ALU.add)),
                        reads=[b3, acs_sb, cst], writes=[ar])
                P.op("act", (lambda e, ar=ar, Ld=Ld: e.activation(Ld[:], ar[:], AF.Exp)), reads=[ar], writes=[Ld])
                for hh in range(4):
                    h = 4 * g + hh
                    P.op("dve", (lambda e, h=h, hh=hh, Ld=Ld: e.tensor_tensor(MT[:, h * 128:(h + 1) * 128], Ld[:, hh * 128:(hh + 1) * 128],
                                                                          b2[:, 0:128], ALU.mult)), reads=[Ld, b2], writes=[MT])
            P.op("act", lambda e: e.activation(de[:], dearg[:], AF.Exp), reads=[dearg], writes=[de])
            X3 = Xtm[:, :].rearrange("p (h d) -> p h d", d=64)
            Xd3 = Xdt[:, :].rearrange("p (h d) -> p h d", d=64)
            Xe3 = Xdte[:, :].rearrange("p (h d) -> p h d", d=64)
            dt_b = dtv[:, 0:NH].unsqueeze(2).broadcast_to([128, NH, 64])
            de_b = de[:, 0:NH].unsqueeze(2).broadcast_to([128, NH, 64])
            P.op(BCE, (lambda e: e.tensor_tensor(Xd3, X3, dt_b, ALU.mult)), reads=[Xtm, dtv], writes=[Xdt])
            P.op(BCE, (lambda e: e.tensor_tensor(Xe3, Xd3, de_b, ALU.mult)), reads=[Xdt, de], writes=[Xdte])
            for h in range(NH):
                P.op("pe", (lambda e, h=h: e.matmul(pA[h // 8][:, (h % 8) * 64:(h % 8 + 1) * 64], MT[:, h * 128:(h + 1) * 128],
                                                    Xdt[:, h * 64:(h + 1) * 64], start=True, stop=True)),
                     reads=[MT, Xdt], writes=[pA[h // 8]])
            for cc in range(2):
                P.op("act", (lambda e, cc=cc: e.copy(yd_sb[:, cc * 512:(cc + 1) * 512], pA[cc][:])), reads=[pA[cc]], writes=[yd_sb])

        def stage2(st, tt, par_i, xbcT, szT):
            Xtm, Btm, Xdt, Xdte, MT, yd_sb = Xtm2[par_i], Btm2[par_i], Xdt2[par_i], Xdte2[par_i], MT2[par_i], yd_sb2[par_i]
            dtv, acs_sb, ea, cdb, de = dtv2[par_i], acs_sb2[par_i], ea2[par_i], cdb2[par_i], de2[par_i]
            c0 = tt * 128
            t0 = st * ST
            CT = xbcT[:, 9 * ST + c0:9 * ST + c0 + 128]
            for cc in range(2):
                P.op("pe", (lambda e, cc=cc, CT=CT: e.matmul(pB[cc][:], CT, S_bf[:, cc * 512:(cc + 1) * 512], start=True, stop=True)),
                     reads=[xbcT, S_bf], writes=[pB[cc]])
            for h in range(NH):
                hs = slice(h * 64, (h + 1) * 64)
                ps_ = slice((h % 8) * 64, (h % 8 + 1) * 64)
                P.op("dve", (lambda e, h=h, hs=hs, ps_=ps_: e.scalar_tensor_tensor(yb[:, hs], pB[h // 8][:, ps_], ea[:, h:h + 1], yd_sb[:, hs],
                                                                               ALU.mult, ALU.add)), reads=[pB[h // 8], ea, yd_sb], writes=[yb])
            for h in range(NH):
                hs = slice(h * 64, (h + 1) * 64)
                P.op("dve", (lambda e, h=h, hs=hs: e.scalar_tensor_tensor(yb[:, hs], Xtm[:, hs], par[:, PDD + h:PDD + h + 1], yb[:, hs],
                                                                      ALU.mult, ALU.add)), reads=[Xtm, par, yb], writes=[yb])
            for cc in range(2):
                P.op("pe", (lambda e, cc=cc: e.matmul(pB[cc][:], Btm[:], Xdte[:, cc * 512:(cc + 1) * 512], start=True, stop=True)),
                     reads=[Btm, Xdte], writes=[pB[cc]])
            for h in range(NH):
                hs = slice(h * 64, (h + 1) * 64)
                ps_ = slice((h % 8) * 64, (h % 8 + 1) * 64)
                P.op("dve", (lambda e, h=h, hs=hs, ps_=ps_: e.scalar_tensor_tensor(S[:, hs], S[:, hs], cdb[:, h:h + 1], pB[h // 8][:, ps_],
                                                                               ALU.mult, ALU.add)), reads=[S, cdb, pB[h // 8]], writes=[S])
            P.op("act", lambda e: e.copy(S_bf[:], S[:]), reads=[S], writes=[S_bf])
            for j in range(8):
                P.op("pe", (lambda e, j=j, c0=c0: e.transpose(b1[:, j * 128:(j + 1) * 128], szT[:, j * ST + c0:j * ST + c0 + 128], idb[:])),
                     reads=[szT, idb], writes=[b1])
            P.op("dve", lambda e: e.tensor_tensor(yg[:], yb[:], b1[:], ALU.mult), reads=[yb, b1], writes=[yg])
            P.op("dve", lambda e: e.scalar_tensor_tensor(junk[:], yg[:], 1.0, yg[:], ALU.mult, ALU.mult, accum_out=ss[:, 0:1]),
                 reads=[yg], writes=[junk, ss])
            P.op("act", lambda e: e.activation(rsn[:], ss[:], AF.Ln, bias=epsb[:, 0:1], scale=1.0 / 1024), reads=[ss, epsb], writes=[rsn])
            P.op("act", lambda e: e.activation(rsn[:], rsn[:], AF.Exp, scale=-0.5), reads=[rsn], writes=[rsn])
            yo_ = yn[cnt["yn"] % 2]; cnt["yn"] += 1
            P.op("dve", (lambda e, yo_=yo_: e.scalar_tensor_tensor(yo_[:], yg[:], rsn[:, 0:1], par[:, PNG:PNG + 1024], ALU.mult, ALU.mult)),
                 reads=[yg, rsn, par], writes=[yo_])
            P.dma("pool", y_o[t0 + c0:t0 + c0 + 128, :], yo_[:], reads=[yo_], writes=[], owner=yo_)

        def capture(fn, *a):
            old = P.ops
            P.ops = []
            fn(*a)
            got = P.ops
            P.ops = old
            return got

        def merge(A, B):
            out = []
            ia = ib = 0
            while ia < len(A) or ib < len(B):
                if ib >= len(B) or (ia < len(A) and ia * len(B) <= ib * len(A)):
                    out.append(A[ia]); ia += 1
                else:
                    out.append(B[ib]); ib += 1
            return out

        for tt in range(4):
            ci = st * 4 + tt
            A = capture(stage1, st, tt, ci % 2, xbcT, szT)
            if pending:
                pst, ptt, ppar, pxb, psz = pending.pop()
                B = capture(stage2, pst, ptt, ppar, pxb, psz)
                P.ops.extend(merge(A, B))
            else:
                P.ops.extend(A)
            pending.append((st, tt, ci % 2, xbcT, szT))
    pst, ptt, ppar, pxb, psz = pending.pop()
    stage2(pst, ptt, ppar, pxb, psz)
    P.finish("sp")
    info = P.emit()
    return nc, info


NQT = 64
SCALE = 128 ** -0.5
NEG = -1.0e30
BIG = 30000.0


def build_attn(n_qt=NQT):
    nc = bass.Bass("TRN2", target_bir_lowering=False)
    P = Prog(nc)
    qT_d = P.dram("qT", [128, NQT, 512], BF16, kind="ExternalInput")
    kT_d = P.dram("kT", [128, 8192], BF16, kind="ExternalInput")
    v_d = P.dram("v", [128, 64, 128], BF16, kind="ExternalInput")
    tb_d = P.dram("tb", [128, 512], BF16, kind="ExternalInput")
    idb_d = P.dram("idb", [128, 128], BF16, kind="ExternalInput")
    en_d = P.dram("en", [32, 32 * 128], BF16, kind="ExternalInput")
    oT_o = P.dram("oT_o", [128, NQT, 512], BF16, kind="ExternalOutput")

    qT = P.sbuf([128, NQT * 512], BF16, "qT")
    kT = P.sbuf([128, 8192], BF16, "kT")
    vv = P.sbuf([128, 64 * 128], BF16, "vv")
    tb = P.sbuf([128, 512], BF16, "tb")
    idb = P.sbuf([128, 128], BF16, "idb")
    en = P.sbuf([32, 32 * 128], BF16, "en")
    ones = P.sbuf([128, 128], BF16, "ones")
    km = P.sbuf([128, 32], F32, "km")
    kmh = P.sbuf([128, 32], BF16, "kmh")
    kmhf = P.sbuf([128, 32], F32, "kmhf")
    kml = P.sbuf([128, 32], BF16, "kml")
    g_sb = P.sbuf([128, 128], F32, "g_sb")
    t8 = P.sbuf([128, 32], F32, "t8")
    selb = P.sbuf([128, 128], BF16, "selb")
    bias2 = [P.sbuf([32, 512], BF16, f"bias2_{i}") for i in range(2)]
    pT = [P.sbuf([128, 512], BF16, f"pT{i}") for i in range(4)]
    rec = P.sbuf([128, 512], F32, "rec")
    ob = [P.sbuf([128, 512], BF16, f"ob{i}") for i in range(2)]
    sTb = [P.psum([128, 512], F32, f"sT{i}") for i in range(3)]
    oTb = [P.psum([128, 512], F32, f"oT{i}") for i in range(2)]
    smb = [P.psum([128, 512], F32, f"sm{i}") for i in range(2)]
    gpb = P.psum([128, 1024], BF16, "gpb")
    gp = sTb[0]

    P.dma("sp", qT[:], qT_d[:].rearrange("p a b -> p (a b)"), reads=[qT_d], writes=[qT])
    P.dma("pool", kT[:], kT_d[:], reads=[kT_d], writes=[kT])
    P.dma("pool", vv[:], v_d[:].rearrange("p a b -> p (a b)"), reads=[v_d], writes=[vv])
    P.dma("pool", tb[:], tb_d[:], reads=[tb_d], writes=[tb])
    P.dma("pool", idb[:], idb_d[:], reads=[idb_d], writes=[idb])
    P.dma("pool", en[:], en_d[:], reads=[en_d], writes=[en])
    P.op("pool", lambda e: e.memset(ones[:], 1.0), writes=[ones])
    P.op("dve", lambda e: e.tensor_reduce(km[:], kT[:, :].rearrange("p (n k) -> p n k", k=256), AX.X, ALU.add), reads=[kT], writes=[km])
    P.op("dve", lambda e: e.tensor_scalar(km[:], km[:], 1.0 / 256, None, ALU.mult), reads=[km], writes=[km])
    P.op("dve", lambda e: e.tensor_copy(kmh[:], km[:]), reads=[km], writes=[kmh])
    P.op("dve", lambda e: e.tensor_copy(kmhf[:], kmh[:]), reads=[kmh], writes=[kmhf])
    P.op("dve", lambda e: e.tensor_tensor(kml[:], km[:], kmhf[:], ALU.subtract), reads=[km, kmhf], writes=[kml])

    LA = 2
    steps = []
    for qt in range(n_qt):
        blk = qt // 2
        half = qt % 2
        tiles = [(2 * n + i, n, False) for n in range(blk) for i in range(2)]
        if half == 0:
            tiles.append((2 * blk, None, True))
        else:
            tiles.append((2 * blk, None, False))
            tiles.append((2 * blk + 1, None, True))
        for ti, (kti, n, diag) in enumerate(tiles):
            steps.append((qt, ti, len(tiles), kti, n, diag))

    def prologue(qt):
        blk = qt // 2
        b2_ = bias2[qt % 2]
        if blk == 0:
            return
        for h in range(4):
            P.op("pe", (lambda e, h=h: e.matmul(gp[:, h * 32:h * 32 + blk], qT[:, qt * 512 + h * 128:qt * 512 + (h + 1) * 128],
                                                kmh[:, 0:blk], start=True, stop=False)), reads=[qT, kmh], writes=[gp])
            P.op("pe", (lambda e, h=h: e.matmul(gp[:, h * 32:h * 32 + blk], qT[:, qt * 512 + h * 128:qt * 512 + (h + 1) * 128],
                                                kml[:, 0:blk], start=False, stop=True)), reads=[qT, kml], writes=[gp])
        P.op("pool", lambda e: e.memset(g_sb[:], NEG), writes=[g_sb])
        g3 = g_sb[:, :].rearrange("p (h n) -> p h n", n=32)[:, :, 0:blk]
        p3 = gp[:, 0:128].rearrange("p (h n) -> p h n", n=32)[:, :, 0:blk]
        P.op("act", (lambda e: e.copy(g3, p3)), reads=[gp], writes=[g_sb])
        for h in range(4):
            P.op("dve", (lambda e, h=h: e.max(t8[:, h * 8:(h + 1) * 8], g_sb[:, h * 32:(h + 1) * 32])), reads=[g_sb], writes=[t8])
        for h in range(4):
            P.op("dve", (lambda e, h=h: e.tensor_scalar(selb[:, h * 32:(h + 1) * 32], g_sb[:, h * 32:(h + 1) * 32],
                                                     t8[:, h * 8 + 2:h * 8 + 3], 1.0, ALU.is_ge, ALU.subtract)), reads=[g_sb, t8], writes=[selb])
        for h in range(4):
            P.op("pe", (lambda e, h=h: e.transpose(gpb[0:32, h * 128:(h + 1) * 128], selb[:, h * 32:(h + 1) * 32], idb[:])),
                 reads=[selb, idb], writes=[gpb])
        P.op("act", (lambda e: e.copy(b2_[:], gpb[0:32, 0:512])), reads=[gpb], writes=[b2_])

    def front(i):
        qt, ti, nt, kti, n, diag = steps[i]
        if ti == 0:
            prologue(qt)
        sT = sTb[1 + i % 2]
        pt = pT[i % 4]
        q_rhs = qT[:, qt * 512:(qt + 1) * 512]
        b2_ = bias2[qt % 2]
        extra = (n is not None) or diag
        P.op("pe", (lambda e: e.matmul(sT[:], kT[:, kti * 128:(kti + 1) * 128], q_rhs, start=True, stop=not extra)),
             reads=[kT, qT], writes=[sT])
        if n is not None:
            P.op("pe", (lambda e: e.matmul(sT[:], en[0:32, n * 128:(n + 1) * 128], b2_[:], start=False, stop=True)),
                 reads=[en, b2_], writes=[sT])
        elif diag:
            P.op("pe", (lambda e: e.matmul(sT[:], idb[:], tb[:], start=False, stop=True)), reads=[idb, tb], writes=[sT])
        P.op("act", (lambda e: e.activation(pt[:], sT[:], AF.Exp, scale=SCALE)), reads=[sT], writes=[pt])

    def back(i):
        qt, ti, nt, kti, n, diag = steps[i]
        pt = pT[i % 4]
        oT = oTb[qt % 2]
        sm = smb[qt % 2]
        first = ti == 0
        last = ti == nt - 1
        P.op("pe", (lambda e: e.matmul(oT[:], vv[:, kti * 128:(kti + 1) * 128], pt[:], start=first, stop=last)), reads=[vv, pt], writes=[oT])
        P.op("pe", (lambda e: e.matmul(sm[:], ones[:], pt[:], start=first, stop=last)), reads=[ones, pt], writes=[sm])
        if last:
            P.op("dve", (lambda e: e.reciprocal(rec[:], sm[:])), reads=[sm], writes=[rec])
            o_ = ob[qt % 2]
            P.op("dve", (lambda e: e.tensor_tensor(o_[:], oT[:], rec[:], ALU.mult)), reads=[oT, rec], writes=[o_])
            P.dma("pool", oT_o[:, qt, :], o_[:], reads=[o_], writes=[], owner=o_)

    for idx in range(len(steps) + LA):
        if idx < len(steps):
            front(idx)
        if idx - LA >= 0:
            back(idx - LA)
    P.finish("sp")
    info = P.emit()
    return nc, info


CCH = 8192


def build_cast(M):
    nc = bass.Bass("TRN2", target_bir_lowering=False)
    P = Prog(nc)
    w_d = P.dram("w", [128, M], F32, kind="ExternalInput")
    wb_d = P.dram("wb", [128, M], BF16, kind="ExternalOutput")
    NB = 3
    fb = [P.sbuf([128, CCH], F32, f"fb{i}") for i in range(NB)]
    bb = [P.sbuf([128, CCH], BF16, f"bb{i}") for i in range(NB)]
    n = M // CCH
    engs = ("dve", "act", "pool")
    for i in range(n):
        f = fb[i % NB]
        b = bb[i % NB]
        P.dma("sp", f[:], w_d[:, i * CCH:(i + 1) * CCH], reads=[w_d], writes=[f])
        en = engs[i % 3]
        if en == "act":
            P.op("act", lambda e: e.copy(b[:], f[:]), reads=[f], writes=[b])
        else:
            P.op(en, lambda e: e.tensor_copy(b[:], f[:]), reads=[f], writes=[b])
        P.dma("pool", wb_d[:, i * CCH:(i + 1) * CCH], b[:], reads=[b], writes=[], owner=b)
    P.finish("sp")
    P.emit()
    return nc


_bf = ml_dtypes.bfloat16
NCORES = 8


def _run(nc, in_maps):
    res = run_bass_kernel_spmd(nc, in_maps, core_ids=list(range(NCORES)))
    return res.results


def _fm(a, kt):
    n = a.shape[0]
    return np.ascontiguousarray(a.T.reshape(kt, 128, n).transpose(1, 0, 2))


def _tok_par(g_ffn, g2, g3, cw, cb):
    par = np.zeros((128, 4 * 32 + 172 * 4), np.float32)
    par[:, 0:32] = g_ffn.reshape(32, 128).T
    par[:, 32:64] = g2.reshape(32, 128).T
    if g3 is not None:
        par[:, 64:96] = g3.reshape(32, 128).T
    par[:, 128:128 + 516] = cw.reshape(3, 172, 128).transpose(2, 1, 0).reshape(128, 516)
    par[:, 128 + 516:] = cb.reshape(172, 128).T
    return par


def _cs_table(base):
    pos = base + np.arange(1024, dtype=np.float32)
    inv = (10000.0 ** (-np.arange(64, dtype=np.float32) / 64)).astype(np.float32)
    ang = pos[:, None] * inv[None, :]
    t = np.concatenate([np.cos(ang), np.sin(ang)], 1).astype(np.float32)
    return np.ascontiguousarray(t.reshape(8, 128, 128).transpose(1, 0, 2))


def kernel(x, norm_a, mamba_w_in, mamba_conv_w, mamba_conv_b, mamba_dt_bias, mamba_a_log,
           mamba_d, mamba_norm, mamba_w_out, kv_norm, w_kv, norm_b, w_q, w_o,
           ffn_norm, ffn_w_up, ffn_conv_w, ffn_conv_b, ffn_w_down, final_norm):
    f32 = np.float32
    x = np.asarray(x, f32)[0]
    w_in = np.asarray(mamba_w_in, f32)[0]

    shapes = []
    for g in range(8):
        shapes.append((f"wx{g}", (18, 128, 32, 128)))
        shapes.append((f"wdt{g}", (128, 512)))
    shapes += [("wpA", (32, 128, 64, 128)), ("wuA", (86, 128, 2, 32, 128)), ("wdA", (32, 128, 86, 128)), ("wqA", (12, 128, 32, 512)),
               ("wpB", (32, 128, 32, 128)), ("wuB", (86, 128, 2, 32, 128)), ("wdB", (32, 128, 86, 128))]
    offs = {}
    tot = 0
    for nm, sh in shapes:
        offs[nm] = (tot, sh)
        tot += int(np.prod(sh))
    per = -(-tot // (NCORES * 128 * CCH)) * CCH
    flat = np.zeros(NCORES * 128 * per, f32)

    def put(nm, arr):
        o, sh = offs[nm]
        flat[o:o + arr.size].reshape(sh)[...] = arr

    for g in range(8):
        wst = np.concatenate([w_in[:, g * 1024:(g + 1) * 1024], w_in[:, 8192 + g * 1024:8192 + (g + 1) * 1024],
                              w_in[:, 16384 + g * 128:16384 + (g + 1) * 128], w_in[:, 17408 + g * 128:17408 + (g + 1) * 128]], 1)
        put(f"wx{g}", wst.reshape(32, 128, 18, 128).transpose(2, 1, 0, 3))
        put(f"wdt{g}", w_in[:, 18432 + g * 16:18432 + (g + 1) * 16].reshape(32, 128, 16).transpose(1, 0, 2).reshape(128, 512))
    put("wpA", np.asarray(mamba_w_out, f32)[0].reshape(64, 128, 32, 128).transpose(2, 1, 0, 3))
    put("wuA", np.asarray(ffn_w_up, f32)[0].reshape(32, 128, 2, 86, 128).transpose(3, 1, 2, 0, 4))
    put("wdA", np.asarray(ffn_w_down, f32)[0].reshape(86, 128, 32, 128).transpose(2, 1, 0, 3))
    wqkv = np.concatenate([np.asarray(w_q, f32)[0], np.asarray(w_kv, f32)], 1)
    put("wqA", wqkv.reshape(32, 128, 12, 512).transpose(2, 1, 0, 3))
    put("wpB", np.asarray(w_o, f32)[0].reshape(32, 128, 32, 128).transpose(2, 1, 0, 3))
    put("wuB", np.asarray(ffn_w_up, f32)[1].reshape(32, 128, 2, 86, 128).transpose(3, 1, 2, 0, 4))
    put("wdB", np.asarray(ffn_w_down, f32)[1].reshape(86, 128, 32, 128).transpose(2, 1, 0, 3))
    del wqkv
    fl = flat.reshape(NCORES, 128, per)
    res = _run(build_cast(per), [{"w": fl[c]} for c in range(NCORES)])
    wbf = np.concatenate([r["wb"].reshape(-1) for r in res])
    del flat, fl, res

    def getw(nm):
        o, sh = offs[nm]
        return wbf[o:o + int(np.prod(sh))].reshape(sh)

    xT_l = _fm(x, 32)
    cst = np.zeros((128, 256), f32)
    kk = np.arange(128)
    cst[:, 0:128] = (kk[:, None] <= kk[None, :]).astype(f32)
    cst[:, 128:256] = np.where(kk[:, None] <= kk[None, :], 0.0, -30000.0)
    esel = np.zeros((16, 2048), f32)
    for h in range(16):
        esel[h, h * 128:(h + 1) * 128] = 1.0
    idb = np.eye(128, dtype=f32).astype(_bf)
    tri = cst[:, 0:128].astype(_bf)
    cwm = np.asarray(mamba_conv_w, f32)[0]
    cbm = np.asarray(mamba_conv_b, f32)[0]
    in_maps = []
    for g in range(8):
        par = np.zeros((128, NPAR), f32)
        par[:, PG:PG + 32] = np.asarray(norm_a, f32)[0].reshape(32, 128).T
        idx = np.concatenate([np.arange(g * 1024, (g + 1) * 1024), 8192 + np.arange(g * 128, (g + 1) * 128),
                              9216 + np.arange(g * 128, (g + 1) * 128)])
        par[:, PCW:PCW + 40] = cwm[:, idx].reshape(4, 10, 128).transpose(2, 1, 0).reshape(128, 40)
        par[:, PCB:PCB + 10] = cbm[idx].reshape(10, 128).T
        par[:, PDTB:PDTB + 16] = np.asarray(mamba_dt_bias, f32)[0][None, g * 16:(g + 1) * 16]
        par[:, PALOG:PALOG + 16] = np.asarray(mamba_a_log, f32)[0][None, g * 16:(g + 1) * 16]
        par[:, PDD:PDD + 16] = np.asarray(mamba_d, f32)[0][None, g * 16:(g + 1) * 16]
        par[:, PNG:PNG + 1024] = np.asarray(mamba_norm, f32)[0][None, g * 1024:(g + 1) * 1024]
        in_maps.append({"xT": xT_l, "wx": getw(f"wx{g}"), "wdt": getw(f"wdt{g}"), "par": par, "cst": cst, "esel": esel, "idb": idb})
    nc1, _ = build_mamba()
    res = _run(nc1, in_maps)
    Y = np.concatenate([r["y_o"] for r in res], 1)
    del in_maps, res, xT_l

    def tok_inputs(act_bf, h_f32, ktin, c):
        a = np.zeros((1026, act_bf.shape[1]), act_bf.dtype)
        hh = np.zeros((1026, 4096), f32)
        lo = 1024 * c - 2
        s0 = max(lo, 0)
        a[s0 - lo:] = act_bf[s0:1024 * c + 1024]
        hh[s0 - lo:] = h_f32[s0:1024 * c + 1024]
        return _fm(a, ktin), _fm(hh, 32)

    parA = _tok_par(np.asarray(ffn_norm, f32)[0], np.asarray(kv_norm, f32), np.asarray(norm_b, f32)[0],
                    np.asarray(ffn_conv_w, f32)[0], np.asarray(ffn_conv_b, f32)[0])
    in_maps = []
    for c in range(8):
        inT, hT = tok_inputs(Y, x, 64, c)
        in_maps.append({"inT": inT, "hT": hT, "wp": getw("wpA"), "wu": getw("wuA"), "wd": getw("wdA"), "wq": getw("wqA"),
                        "par": parA, "cs": _cs_table(1024 * c)})
    nc2, _ = build_tok("A", 64)
    res = _run(nc2, in_maps)
    h1 = np.concatenate([r["hoT"].transpose(2, 1, 0).reshape(1024, 4096) for r in res], 0)
    Q = np.concatenate([r["q_o"] for r in res], 0)
    KV = np.concatenate([r["kv_o"] for r in res], 0)
    del in_maps, res, Y

    tb = np.tile(np.where(kk[:, None] <= kk[None, :], 0.0, -BIG).astype(f32), (1, 4)).astype(_bf)
    en = np.zeros((32, 32 * 128), f32)
    for n_ in range(32):
        en[n_, n_ * 128:(n_ + 1) * 128] = BIG
    en = en.astype(_bf)
    in_maps = []
    for kv in range(8):
        q = Q[:, kv * 512:(kv + 1) * 512]
        qT = np.ascontiguousarray(q.reshape(64, 128, 4, 128).transpose(3, 0, 2, 1).reshape(128, 64, 512))
        kT = np.ascontiguousarray(KV[:, kv * 128:(kv + 1) * 128].T)
        vl = np.ascontiguousarray(KV[:, 1024 + kv * 128:1024 + (kv + 1) * 128].reshape(64, 128, 128).transpose(1, 0, 2))
        in_maps.append({"qT": qT, "kT": kT, "v": vl, "tb": tb, "idb": idb, "en": en})
    nc3, _ = build_attn()
    res = _run(nc3, in_maps)
    OT = np.stack([r["oT_o"].reshape(128, 64, 4, 128) for r in res], 0)
    OT = np.ascontiguousarray(OT.transpose(1, 0, 3, 2, 4).reshape(128, 32, 8192))
    del in_maps, res, Q, KV

    parB = _tok_par(np.asarray(ffn_norm, f32)[1], np.asarray(final_norm, f32), None,
                    np.asarray(ffn_conv_w, f32)[1], np.asarray(ffn_conv_b, f32)[1])
    in_maps = []
    for c in range(8):
        inT = np.zeros((128, 32, 1026), _bf)
        lo = 1024 * c - 2
        s0 = max(lo, 0)
        inT[:, :, s0 - lo:] = OT[:, :, s0:1024 * c + 1024]
        hh = np.zeros((1026, 4096), f32)
        hh[s0 - lo:] = h1[s0:1024 * c + 1024]
        hT = _fm(hh, 32)
        in_maps.append({"inT": inT, "hT": hT, "wp": getw("wpB"), "wu": getw("wuB"), "wd": getw("wdB"), "par": parB})
    nc4, _ = build_tok("B", 32)
    res = _run(nc4, in_maps)
    out = np.concatenate([r["hoT"].transpose(2, 1, 0).reshape(1024, 4096) for r in res], 0)
    return np.ascontiguousarray(out.astype(f32))[None]
```

```python
import numpy as np
import ml_dtypes
import concourse.bass as bass
import concourse.mybir as mybir
from concourse.bass_utils import run_bass_kernel_spmd

F32 = mybir.dt.float32
F32R = mybir.dt.float32r
BF16 = mybir.dt.bfloat16
AF = mybir.ActivationFunctionType
ALU = mybir.AluOpType
AX = mybir.AxisListType


class Buf:
    __slots__ = ("t", "name", "last_w", "readers", "dsem", "dcnt")

    def __init__(self, t, name):
        self.t = t
        self.name = name
        self.last_w = None
        self.readers = []
        self.dsem = None
        self.dcnt = 0

    def __getitem__(self, k):
        return self.t[k]


class _Rec:
    def __init__(self):
        self.call = None

    def __getattr__(self, name):
        def f(*a, **k):
            self.call = (name, a, k)
            return None
        return f


class Op:
    __slots__ = ("call", "eng", "fn", "reads", "writes", "is_dma", "owner", "deps", "need_inc", "semval", "idx", "is_final", "raw")


class Prog:
    ENGS = ("pe", "act", "dve", "pool", "sp")

    def __init__(self, nc, same_engine_sync=True):
        self.nc = nc
        self.eng = {"pe": nc.tensor, "act": nc.scalar, "dve": nc.vector, "pool": nc.gpsimd, "sp": nc.sync}
        self.ops = []
        self.same_engine_sync = same_engine_sync
        self.nbuf = 0

    def sbuf(self, shape, dtype, name=None):
        self.nbuf += 1
        name = "s_" + (name or f"sb{self.nbuf}")
        return Buf(self.nc.alloc_sbuf_tensor(name, list(shape), dtype), name)

    def psum(self, shape, dtype=F32, name=None):
        self.nbuf += 1
        name = "p_" + (name or f"ps{self.nbuf}")
        return Buf(self.nc.alloc_psum_tensor(name, list(shape), dtype), name)

    def dram(self, name, shape, dtype, kind="Internal"):
        return Buf(self.nc.dram_tensor(name, list(shape), dtype, kind=kind), name)

    def op(self, eng, fn, reads=(), writes=()):
        o = Op()
        o.eng = eng
        o.fn = None
        r = _Rec()
        fn(r)
        o.call = r.call
        o.reads = tuple(reads)
        o.writes = tuple(writes)
        o.is_dma = False
        o.owner = None
        o.need_inc = False
        o.is_final = False
        o.idx = len(self.ops)
        self.ops.append(o)
        return o

    def dma(self, queue, out_ap, in_ap, reads=(), writes=(), owner=None, **kw):
        o = self.op(queue, lambda e: e.dma_start(out=out_ap, in_=in_ap, **kw), reads, writes)
        o.is_dma = True
        o.owner = owner if owner is not None else (writes[0] if writes else reads[0])
        return o

    def emit(self):
        nc = self.nc
        ops = self.ops
        for i_, o_ in enumerate(ops):
            o_.idx = i_
        for o in ops:
            deps = set()
            raw = set()
            for b in o.reads:
                if b.last_w is not None:
                    deps.add(b.last_w)
                    raw.add(b.last_w)
            o.raw = raw
            for b in o.writes:
                if b.last_w is not None:
                    deps.add(b.last_w)
                deps.update(b.readers)
            for b in o.reads:
                if (not o.is_dma) and b.readers:
                    q = ops[b.readers[-1]]
                    if (not q.is_dma) and q.eng == o.eng:
                        b.readers[-1] = o.idx
                        continue
                b.readers.append(o.idx)
            for b in o.writes:
                b.last_w = o.idx
                b.readers = []
            deps.discard(o.idx)
            if o.is_final:
                last = {}
                for q in ops[:o.idx]:
                    if q.is_dma:
                        deps.add(q.idx)
                    else:
                        last[q.eng] = q.idx
                deps.update(last.values())
            o.deps = deps
            for d in deps:
                p = ops[d]
                if p.is_dma:
                    continue
                if p.eng != o.eng or (self.same_engine_sync and o.eng != "pe" and d in o.raw):
                    p.need_inc = True
                if p.eng == o.eng and o.is_dma:
                    p.need_inc = True
        esem = {e: nc.alloc_semaphore(f"es_{e}") for e in self.ENGS}
        ecnt = {e: 0 for e in self.ENGS}
        for o in ops:
            if o.is_dma:
                b = o.owner
                if b.dsem is None:
                    b.dsem = nc.alloc_semaphore(f"ds_{b.name}")
                b.dcnt += 16
                o.semval = (b.dsem, b.dcnt)
            elif o.need_inc:
                ecnt[o.eng] += 1
                o.semval = (esem[o.eng], ecnt[o.eng])
            else:
                o.semval = None
        waited = {e: {} for e in self.ENGS}
        nwaits = 0
        for o in ops:
            e = self.eng[o.eng]
            w = waited[o.eng]
            need = {}
            for d in o.deps:
                p = ops[d]
                if p.semval is None:
                    continue
                if (not p.is_dma) and p.eng == o.eng and not (o.is_dma or (self.same_engine_sync and o.eng != "pe" and d in o.raw)):
                    continue
                s, v = p.semval
                key = id(s)
                if w.get(key, 0) >= v:
                    continue
                if key not in need or need[key][1] < v:
                    need[key] = (s, v)
            for key, (s, v) in need.items():
                e.wait_ge(s, v)
                w[key] = v
                nwaits += 1
            ins = None
            if o.call is not None:
                nm, a_, k_ = o.call
                ins = getattr(e, nm)(*a_, **k_)
            if o.semval is not None and ins is not None:
                ins.then_inc(o.semval[0], 16 if o.is_dma else 1)
        self.final = {"esem": esem, "ecnt": ecnt, "nwaits": nwaits}
        return self.final

    def finish(self, eng="sp"):
        o = self.op(eng, lambda e: None)
        o.is_final = True


D = 4096
DT = 32
FF = 11008
FT = 86
TP = 342
EPS = 1e-5
WU = 43 * 128


class WStream:
    def __init__(self, P, nbuf, unit_cols, queue="sp", pf=3):
        self.P = P
        self.bufs = [P.sbuf([128, unit_cols], BF16, f"wp{i}") for i in range(nbuf)]
        self.reqs = []
        self.issued = 0
        self.taken = 0
        self.queue = queue
        self.pf = pf

    def add(self, dbuf, ap, ncols):
        self.reqs.append((dbuf, ap, ncols))

    def _issue(self, i):
        dbuf, ap, ncols = self.reqs[i]
        b = self.bufs[i % len(self.bufs)]
        self.P.dma(self.queue, b[:, 0:ncols], ap, reads=[dbuf], writes=[b])

    def next(self):
        i = self.taken
        lim = min(len(self.reqs), i + self.pf + 1)
        while self.issued < lim:
            self._issue(self.issued)
            self.issued += 1
        self.taken += 1
        return self.bufs[i % len(self.bufs)]


def build_tok(mode, KTIN):
    nc = bass.Bass("TRN2", target_bir_lowering=False)
    P = Prog(nc)
    NTOK = 1026
    inT = P.dram("inT", [128, KTIN, NTOK], BF16, kind="ExternalInput")
    hT = P.dram("hT", [128, DT, NTOK], F32, kind="ExternalInput")
    wp_d = P.dram("wp", [DT, 128, KTIN, 128], BF16, kind="ExternalInput")
    wu_d = P.dram("wu", [FT, 128, 2, DT, 128], BF16, kind="ExternalInput")
    wd_d = P.dram("wd", [DT, 128, FT, 128], BF16, kind="ExternalInput")
    par_d = P.dram("par", [128, 4 * DT + 172 * 4], F32, kind="ExternalInput")
    if mode == "A":
        wq_d = P.dram("wq", [12, 128, DT, 512], BF16, kind="ExternalInput")
        cs_d = P.dram("cs", [128, 9, 128], F32, kind="ExternalInput")
        q_o = P.dram("q_o", [1024, 4096], BF16, kind="ExternalOutput")
        kv_o = P.dram("kv_o", [1024, 2048], BF16, kind="ExternalOutput")
    hoT = P.dram("hoT", [128, DT, 1024], F32, kind="ExternalOutput")
    hmid_d = P.dram("hmid", [128, DT, 1024], F32)
    if mode == "B":
        hout_d = P.dram("hout", [128, DT, 1024], F32)
    else:
        hout_d = hoT
    hmid_v = [Buf(hmid_d.t, f"hmid{i}") for i in range(DT)]
    hout_v = [Buf(hout_d.t, f"hout{i}") for i in range(DT)]

    par = P.sbuf([128, 4 * DT + 172 * 4], F32, "par")
    G_FFN, G_2, G_3, CW, CB = 0, DT, 2 * DT, 4 * DT, 4 * DT + 172 * 3
    big = P.sbuf([128, FT * TP], BF16, "big")
    hn = P.sbuf([128, DT * TP], BF16, "hn")
    ones = P.sbuf([128, 128], BF16, "ones")
    rstd = P.sbuf([128, TP], F32, "rstd")
    tails = P.sbuf([128, 172 * 2], F32, "tails")
    ws = WStream(P, 6, WU, "sp", pf=5)
    stg = [P.sbuf([128, TP], F32, f"stg{i}") for i in range(3)]
    sqb = [P.sbuf([128, TP], BF16, f"sq{i}") for i in range(2)]
    ub = [P.sbuf([128, 2 * (TP + 2)], F32, f"ub{i}") for i in range(2)]
    cv = [P.sbuf([128, 2 * TP], F32, f"cv{i}") for i in range(2)]
    ps = [P.psum([128, 512], F32, f"psb{i}") for i in range(8)]
    PS_MM = (ps[0], ps[1])
    PS_UP = ((ps[2], ps[3]), (ps[4], ps[5]))
    PS_SS = ps[6]
    PS_TM = ps[7]
    if mode == "A":
        cs = P.sbuf([128, 9 * 128], F32, "cs")
        rs_tm = P.sbuf([128, 4], F32, "rs_tm")
        qst = cv
        qrt = ub
        qob = [P.sbuf([128, 512], BF16, f"qob{i}") for i in range(2)]
        onecol = P.sbuf([128, 1], F32, "onecol")

    P.dma("pool", par[:], par_d[:], reads=[par_d], writes=[par])
    P.op("pool", lambda e: e.memset(ones[:], 1.0), writes=[ones])
    P.op("pool", lambda e: e.memset(tails[:], 0.0), writes=[tails])
    for sg_ in stg:
        P.op("pool", (lambda e, sg_=sg_: e.memset(sg_[:], 0.0)), writes=[sg_])
    if mode == "A":
        P.dma("pool", cs[:], cs_d[:].rearrange("p a b -> p (a b)"), reads=[cs_d], writes=[cs])
        P.op("pool", lambda e: e.memset(onecol[:], 1.0), writes=[onecol])

    passes = [(0, TP, False), (TP, TP, False), (2 * TP, TP, False)]
    for (c0, n, halo) in passes:
        for dt in range(DT):
            for h0 in range(0, KTIN, 32):
                ws.add(wp_d, wp_d[dt, :, h0:h0 + 32, :].rearrange("p a b -> p (a b)"), 32 * 128)
        for j in range(FT):
            for half in range(2):
                ws.add(wu_d, wu_d[j, :, half].rearrange("p b c -> p (b c)"), DT * 128)
        if halo:
            continue
        for dt in range(DT):
            for h0 in range(0, FT, 43):
                ws.add(wd_d, wd_d[dt, :, h0:h0 + 43, :].rearrange("p a b -> p (a b)"), 43 * 128)
        if mode == "A":
            for ch in range(12):
                for k0 in range(0, DT, 8):
                    ws.add(wq_d, wq_d[ch, :, k0:k0 + 8, :].rearrange("p a b -> p (a b)"), 8 * 512)

    cnt = {"mm": 0, "up": 0, "stg": 0, "sq": 0, "q": 0}

    def rstd_from(ssq_ps, out_ap, n, eng_cols):
        P.op("act", lambda e: e.activation(out_ap, ssq_ps[:, 0:n], AF.Sqrt, bias=epsb[:, 0:1], scale=1.0 / D),
             reads=[ssq_ps, epsb], writes=[rstd])
        P.op("dve", lambda e: e.reciprocal(out_ap, out_ap), reads=[rstd], writes=[rstd])

    epsb = P.sbuf([128, 1], F32, "epsb")
    P.op("pool", lambda e: e.memset(epsb[:], EPS), writes=[epsb])

    for pi, (c0, n, halo) in enumerate(passes):
        o0 = c0 - 2
        lo = 2 if c0 == 0 else 0
        P.dma("sp", big[:, 0:KTIN * TP].rearrange("p (k t) -> p k t", t=TP)[:, :, 0:n], inT[:, :, c0:c0 + n],
              reads=[inT], writes=[big])
        for dt in range(DT):
            pb = PS_MM[cnt["mm"] % 2]; cnt["mm"] += 1
            for h0 in range(0, KTIN, 32):
                w = ws.next()
                for k in range(32):
                    kk = h0 + k
                    P.op("pe", (lambda e, w=w, k=k, kk=kk, pb=pb: e.matmul(
                        pb[:, 0:n], w[:, k * 128:(k + 1) * 128], big[:, kk * TP:kk * TP + n],
                        start=(kk == 0), stop=(kk == KTIN - 1))), reads=[w, big], writes=[pb])
            sg = stg[cnt["stg"] % 3]; cnt["stg"] += 1
            P.dma("pool", sg[:, 0:n], hT[:, dt, c0:c0 + n], reads=[hT], writes=[sg])
            P.op("dve", (lambda e, sg=sg, pb=pb: e.tensor_tensor(sg[:, 0:n], pb[:, 0:n], sg[:, 0:n], ALU.add)),
                 reads=[pb, sg], writes=[sg])
            if not halo:
                P.dma("pool", hmid_d[:, dt, o0 + lo:o0 + n], sg[:, lo:n], reads=[sg], writes=[hmid_v[dt]], owner=sg)
            sq = sqb[cnt["sq"] % 2]; cnt["sq"] += 1
            P.op("act", (lambda e, sg=sg, sq=sq: e.activation(sq[:, 0:n], sg[:, 0:n], AF.Square)), reads=[sg], writes=[sq])
            P.op("pe", (lambda e, sq=sq, dt=dt: e.matmul(PS_SS[:, 0:n], ones[:], sq[:, 0:n], start=(dt == 0), stop=(dt == DT - 1))),
                 reads=[ones, sq], writes=[PS_SS])
            P.op("dve", (lambda e, sg=sg, dt=dt: e.tensor_scalar(hn[:, dt * TP:dt * TP + n], sg[:, 0:n],
                                                              par[:, G_FFN + dt:G_FFN + dt + 1], None, ALU.mult)),
                 reads=[sg, par], writes=[hn])
        rstd_from(PS_SS, rstd[:, 0:n], n, 128)
        for j in range(FT):
            pg, pv = PS_UP[cnt["up"] % 2]
            u = ub[cnt["up"] % 2]
            c = cv[cnt["up"] % 2]
            cnt["up"] += 1
            for half, pb in ((0, pg), (1, pv)):
                w = ws.next()
                for k in range(DT):
                    P.op("pe", (lambda e, w=w, k=k, half=half, pb=pb: e.matmul(
                        pb[:, 0:n], w[:, k * 128:(k + 1) * 128], hn[:, k * TP:k * TP + n],
                        start=(k == 0), stop=(k == DT - 1))), reads=[w, hn], writes=[pb])
            ug = u[:, 0:TP + 2]
            uv = u[:, TP + 2:2 * TP + 4]
            tg = j
            tv = FT + j
            P.op("dve", (lambda e, pg=pg, ug=ug: e.tensor_tensor(ug[:, 2:2 + n], pg[:, 0:n], rstd[:, 0:n], ALU.mult)),
                 reads=[pg, rstd], writes=[u])
            P.op("dve", (lambda e, pv=pv, uv=uv: e.tensor_tensor(uv[:, 2:2 + n], pv[:, 0:n], rstd[:, 0:n], ALU.mult)),
                 reads=[pv, rstd], writes=[u])
            if halo:
                P.op("pool", (lambda e, ug=ug, tg=tg: e.tensor_copy(tails[:, 2 * tg:2 * tg + 2], ug[:, 2:4])), reads=[u], writes=[tails])
                P.op("pool", (lambda e, uv=uv, tv=tv: e.tensor_copy(tails[:, 2 * tv:2 * tv + 2], uv[:, 2:4])), reads=[u], writes=[tails])
                continue
            P.op("pool", (lambda e, ug=ug, tg=tg: e.tensor_copy(ug[:, 0:2], tails[:, 2 * tg:2 * tg + 2])), reads=[tails], writes=[u])
            P.op("pool", (lambda e, uv=uv, tv=tv: e.tensor_copy(uv[:, 0:2], tails[:, 2 * tv:2 * tv + 2])), reads=[tails], writes=[u])
            P.op("pool", (lambda e, ug=ug, tg=tg: e.tensor_copy(tails[:, 2 * tg:2 * tg + 2], ug[:, n:n + 2])), reads=[u], writes=[tails])
            P.op("pool", (lambda e, uv=uv, tv=tv: e.tensor_copy(tails[:, 2 * tv:2 * tv + 2], uv[:, n:n + 2])), reads=[u], writes=[tails])
            cg = c[:, 0:TP]
            cvv = c[:, TP:2 * TP]
            for (uu, cc, t) in ((ug, cg, tg), (uv, cvv, tv)):
                P.op("dve", (lambda e, uu=uu, cc=cc, t=t: e.tensor_scalar(
                    cc[:, 0:n], uu[:, 0:n], par[:, CW + 3 * t:CW + 3 * t + 1], par[:, CB + t:CB + t + 1], ALU.mult, ALU.add)),
                    reads=[u, par], writes=[c])
                for kk in (1, 2):
                    P.op("dve", (lambda e, uu=uu, cc=cc, t=t, kk=kk: e.scalar_tensor_tensor(
                        cc[:, 0:n], uu[:, kk:kk + n], par[:, CW + 3 * t + kk:CW + 3 * t + kk + 1], cc[:, 0:n], ALU.mult, ALU.add)),
                        reads=[u, par, c], writes=[c])
            P.op("act", (lambda e, cg=cg: e.activation(cg[:, 0:n], cg[:, 0:n], AF.Silu)), reads=[c], writes=[c])
            P.op("dve", (lambda e, cg=cg, cvv=cvv, j=j: e.tensor_tensor(big[:, j * TP:j * TP + n], cg[:, 0:n], cvv[:, 0:n], ALU.mult)),
                 reads=[c], writes=[big])
        if halo:
            continue
        for dt in range(DT):
            pb = PS_MM[cnt["mm"] % 2]; cnt["mm"] += 1
            for h0 in range(0, FT, 43):
                w = ws.next()
                for k in range(43):
                    kk = h0 + k
                    P.op("pe", (lambda e, w=w, k=k, kk=kk, pb=pb: e.matmul(
                        pb[:, 0:n], w[:, k * 128:(k + 1) * 128], big[:, kk * TP:kk * TP + n],
                        start=(kk == 0), stop=(kk == FT - 1))), reads=[w, big], writes=[pb])
            sg = stg[cnt["stg"] % 3]; cnt["stg"] += 1
            P.dma("pool", sg[:, lo:n], hmid_d[:, dt, o0 + lo:o0 + n], reads=[hmid_v[dt]], writes=[sg])
            P.op("dve", (lambda e, sg=sg, pb=pb: e.tensor_tensor(sg[:, 0:n], pb[:, 0:n], sg[:, 0:n], ALU.add)),
                 reads=[pb, sg], writes=[sg])
            P.dma("pool", hout_d[:, dt, o0 + lo:o0 + n], sg[:, lo:n], reads=[sg], writes=[hout_v[dt]], owner=sg)
            sq = sqb[cnt["sq"] % 2]; cnt["sq"] += 1
            P.op("act", (lambda e, sg=sg, sq=sq: e.activation(sq[:, 0:n], sg[:, 0:n], AF.Square)), reads=[sg], writes=[sq])
            P.op("pe", (lambda e, sq=sq, dt=dt: e.matmul(PS_SS[:, 0:n], ones[:], sq[:, 0:n], start=(dt == 0), stop=(dt == DT - 1))),
                 reads=[ones, sq], writes=[PS_SS])
            if mode == "A":
                P.op("dve", (lambda e, sg=sg, dt=dt: e.tensor_scalar(hn[:, dt * TP:dt * TP + n], sg[:, 0:n],
                                                                  par[:, G_2 + dt:G_2 + dt + 1], None, ALU.mult)),
                     reads=[sg, par], writes=[hn])
        rstd_from(PS_SS, rstd[:, 0:n], n, 128)
        if mode == "B":
            for dt in range(DT):
                sg = stg[cnt["stg"] % 3]; cnt["stg"] += 1
                P.dma("pool", sg[:, lo:n], hout_d[:, dt, o0 + lo:o0 + n], reads=[hout_v[dt]], writes=[sg])
                P.op("dve", (lambda e, sg=sg, dt=dt: e.scalar_tensor_tensor(sg[:, 0:n], sg[:, 0:n], par[:, G_2 + dt:G_2 + dt + 1],
                                                                         rstd[:, 0:n], ALU.mult, ALU.mult)),
                     reads=[sg, par, rstd], writes=[sg])
                P.dma("pool", hoT[:, dt, o0 + lo:o0 + n], sg[:, lo:n], reads=[sg], writes=[], owner=sg)
        else:
            ttiles = [(ts, min(128, n - ts)) for ts in range(0, n, 128)]
            for tt, (ts, m) in enumerate(ttiles):
                P.op("pe", (lambda e, tt=tt, ts=ts, m=m: e.matmul(PS_TM[0:m, tt:tt + 1], rstd[0:1, ts:ts + m], onecol[0:1, 0:1],
                                                                  start=True, stop=True)), reads=[rstd, onecol], writes=[PS_TM])
            P.op("act", lambda e: e.copy(rs_tm[:, 0:4], PS_TM[:, 0:4]), reads=[PS_TM], writes=[rs_tm])
            for dt in range(DT):
                sg = stg[cnt["stg"] % 3]; cnt["stg"] += 1
                P.dma("pool", sg[:, lo:n], hout_d[:, dt, o0 + lo:o0 + n], reads=[hout_v[dt]], writes=[sg])
                P.op("dve", (lambda e, sg=sg, dt=dt: e.tensor_scalar(big[:, dt * TP:dt * TP + n], sg[:, 0:n],
                                                                  par[:, G_3 + dt:G_3 + dt + 1], None, ALU.mult)),
                     reads=[sg, par], writes=[big])
            for ch in range(12):
                src = big if ch < 8 else hn
                pbs = [ps[0], ps[1], ps[2], ps[3]]
                for piece in range(4):
                    w = ws.next()
                    for tt, (ts, m) in enumerate(ttiles):
                        for k in range(8):
                            kk = piece * 8 + k
                            P.op("pe", (lambda e, w=w, k=k, kk=kk, tt=tt, ts=ts, m=m, src=src: e.matmul(
                                pbs[tt][0:m, 0:512], src[:, kk * TP + ts:kk * TP + ts + m], w[:, k * 512:(k + 1) * 512],
                                start=(kk == 0), stop=(kk == DT - 1))), reads=[w, src], writes=[pbs[tt]])
                for tt, (ts, m) in enumerate(ttiles):
                    tile_idx = pi * 3 + tt
                    qs = qst[cnt["q"] % 2]; qr = qrt[cnt["q"] % 2]; qo = qob[cnt["q"] % 2]; cnt["q"] += 1
                    rope = ch < 10
                    if rope:
                        P.op("act", (lambda e, qs=qs, tt=tt, m=m: e.activation(qs[0:m, 0:512], pbs[tt][0:m, :], AF.Copy, scale=rs_tm[0:m, tt:tt + 1])),
                             reads=[pbs[tt], rs_tm], writes=[qs])
                        q3 = qs[0:m, 0:512].rearrange("p (h two d) -> p h two d", two=2, d=64)
                        r3 = qr[0:m, 0:512].rearrange("p (h two d) -> p h two d", two=2, d=64)
                        o3 = qo[0:m, :].rearrange("p (h two d) -> p h two d", two=2, d=64)
                        cosb = cs[0:m, tile_idx * 128:tile_idx * 128 + 64].unsqueeze(1).broadcast_to([m, 4, 64])
                        sinb = cs[0:m, tile_idx * 128 + 64:tile_idx * 128 + 128].unsqueeze(1).broadcast_to([m, 4, 64])
                        P.op("pool", (lambda e: e.tensor_tensor(r3[:, :, 0, :], q3[:, :, 1, :], sinb, ALU.mult)), reads=[qs, cs], writes=[qr])
                        P.op("pool", (lambda e: e.tensor_tensor(r3[:, :, 1, :], q3[:, :, 0, :], sinb, ALU.mult)), reads=[qs, cs], writes=[qr])
                        P.op("dve", (lambda e: e.tensor_tensor(q3[:, :, 0, :], q3[:, :, 0, :], cosb, ALU.mult)), reads=[qs, cs], writes=[qs])
                        P.op("dve", (lambda e: e.tensor_tensor(q3[:, :, 1, :], q3[:, :, 1, :], cosb, ALU.mult)), reads=[qs, cs], writes=[qs])
                        P.op("dve", (lambda e: e.tensor_tensor(o3[:, :, 0, :], q3[:, :, 0, :], r3[:, :, 0, :], ALU.subtract)), reads=[qs, qr], writes=[qo])
                        P.op("dve", (lambda e: e.tensor_tensor(o3[:, :, 1, :], q3[:, :, 1, :], r3[:, :, 1, :], ALU.add)), reads=[qs, qr], writes=[qo])
                    else:
                        P.op("act", (lambda e, qo=qo, tt=tt, m=m: e.activation(qo[0:m, :], pbs[tt][0:m, :], AF.Copy, scale=rs_tm[0:m, tt:tt + 1])),
                             reads=[pbs[tt], rs_tm], writes=[qo])
                    r0 = lo if ts == 0 else 0
                    tok0 = c0 + ts + r0 - 2
                    nr = m - r0
                    if ch < 8:
                        P.dma("pool", q_o[tok0:tok0 + nr, ch * 512:(ch + 1) * 512], qo[r0:m, :], reads=[qo], writes=[], owner=qo)
                    else:
                        P.dma("pool", kv_o[tok0:tok0 + nr, (ch - 8) * 512:(ch - 7) * 512], qo[r0:m, :], reads=[qo], writes=[], owner=qo)
    P.finish("sp")
    info = P.emit()
    return nc, info


D = 4096
KT = 32
ST = 512
NTOK = 8192
EPS = 1e-5
NH = 16
XQ = "sp"
BCE = "pool"
PG, PCW, PCB, PDTB, PALOG, PDD, PNG = 0, 32, 32 + 40, 32 + 50, 32 + 66, 32 + 82, 32 + 98
NPAR = 32 + 98 + 1024


def build_mamba(n_super=NTOK // ST, debug=False):
    nc = bass.Bass("TRN2", target_bir_lowering=False)
    P = Prog(nc)
    xT = P.dram("xT", [128, KT, NTOK], F32, kind="ExternalInput")
    wx_d = P.dram("wx", [18, 128, KT, 128], BF16, kind="ExternalInput")
    wdt_d = P.dram("wdt", [128, KT * 16], BF16, kind="ExternalInput")
    par_d = P.dram("par", [128, NPAR], F32, kind="ExternalInput")
    cst_d = P.dram("cst", [128, 256], F32, kind="ExternalInput")
    esel_d = P.dram("esel", [16, 2048], F32, kind="ExternalInput")
    idb_d = P.dram("idb", [128, 128], BF16, kind="ExternalInput")
    y_o = P.dram("y_o", [NTOK, 1024], BF16, kind="ExternalOutput")

    par = P.sbuf([128, NPAR], F32, "par")
    cst = P.sbuf([128, 256], F32, "cst")
    esel = P.sbuf([16, 2048], F32, "esel")
    idb = P.sbuf([128, 128], BF16, "idb")
    wdt = P.sbuf([128, KT * 16], BF16, "wdt")
    ones = P.sbuf([128, 128], BF16, "ones")
    onecol = P.sbuf([128, 1], F32, "onecol")
    epsb = P.sbuf([128, 1], F32, "epsb")
    Arep = P.sbuf([128, NH], F32, "Arep")
    uT = P.sbuf([128, KT * ST], BF16, "uT")
    xbcT2 = [P.sbuf([128, 10 * ST], BF16, f"xbcT{i}") for i in range(2)]
    szT2 = [P.sbuf([128, 8 * ST], BF16, f"szT{i}") for i in range(2)]
    rstd = P.sbuf([128, ST], F32, "rstd")
    rs_tm = P.sbuf([128, 4], F32, "rs_tm")
    tails = P.sbuf([128, 30], F32, "tails")
    S = P.sbuf([128, 1024], F32, "S")
    S_bf = P.sbuf([128, 1024], BF16, "S_bf")
    ws = WStream(P, 4, KT * 128, "sp", pf=3)
    xs = [P.sbuf([128, ST], F32, f"xs{i}") for i in range(3)]
    sqb = [P.sbuf([128, ST], BF16, f"sq{i}") for i in range(2)]
    raw = [P.sbuf([128, ST + 3], F32, f"raw{i}") for i in range(2)]
    acc = [P.sbuf([128, ST], F32, f"acc{i}") for i in range(2)]
    Xtm2 = [P.sbuf([128, 1024], BF16, f"Xtm{i}") for i in range(2)]
    Btm2 = [P.sbuf([128, 128], BF16, f"Btm{i}") for i in range(2)]
    Xdt2 = [P.sbuf([128, 1024], BF16, f"Xdt{i}") for i in range(2)]
    Xdte2 = [P.sbuf([128, 1024], BF16, f"Xdte{i}") for i in range(2)]
    arg = [P.sbuf([128, 512], F32, f"arg{i}") for i in range(2)]
    Ldec = [P.sbuf([128, 512], F32, f"Ldec{i}") for i in range(2)]
    MT2 = [P.sbuf([128, 2048], BF16, f"MT{i}") for i in range(2)]
    yd_sb2 = [P.sbuf([128, 1024], F32, f"yd_sb{i}") for i in range(2)]
    yb = P.sbuf([128, 1024], F32, "yb")
    yg = P.sbuf([128, 1024], F32, "yg")
    junk = P.sbuf([128, 1024], BF16, "junk")
    yn = [P.sbuf([128, 1024], BF16, f"yn{i}") for i in range(2)]
    dtv2 = [P.sbuf([128, NH], F32, f"dtv{i}") for i in range(2)]
    e1 = P.sbuf([128, NH], F32, "e1")
    av = P.sbuf([128, NH], F32, "av")
    acs_sb2 = [P.sbuf([128, NH], F32, f"acs_sb{i}") for i in range(2)]
    acsT_sb = P.sbuf([16, 128], F32, "acsT_sb")
    ea2 = [P.sbuf([128, NH], F32, f"ea{i}") for i in range(2)]
    cdb2 = [P.sbuf([128, NH], F32, f"cdb{i}") for i in range(2)]
    dearg = P.sbuf([128, NH], F32, "dearg")
    de2 = [P.sbuf([128, NH], F32, f"de{i}") for i in range(2)]
    ss = P.sbuf([128, 1], F32, "ss")
    rsn = P.sbuf([128, 1], F32, "rsn")

    b0 = P.psum([128, 1024], BF16, "b0")
    b1 = P.psum([128, 1024], BF16, "b1")
    b2 = P.psum([128, 512], F32, "b2")
    b3 = P.psum([128, 512], F32, "b3")
    pA = [P.psum([128, 512], F32, f"pA{i}") for i in range(2)]
    pB = [P.psum([128, 512], F32, f"pB{i}") for i in range(2)]

    T = cst[:, 0:128]
    maskb = cst[:, 128:256]

    P.dma("pool", par[:], par_d[:], reads=[par_d], writes=[par])
    P.dma("pool", cst[:], cst_d[:], reads=[cst_d], writes=[cst])
    P.dma("pool", esel[:], esel_d[:], reads=[esel_d], writes=[esel])
    P.dma("pool", idb[:], idb_d[:], reads=[idb_d], writes=[idb])
    P.dma("pool", wdt[:], wdt_d[:], reads=[wdt_d], writes=[wdt])
    P.op("pool", lambda e: e.memset(ones[:], 1.0), writes=[ones])
    P.op("pool", lambda e: e.memset(onecol[:], 1.0), writes=[onecol])
    P.op("pool", lambda e: e.memset(epsb[:], EPS), writes=[epsb])
    P.op("pool", lambda e: e.memset(tails[:], 0.0), writes=[tails])
    P.op("pool", lambda e: e.memset(S[:], 0.0), writes=[S])
    P.op("pool", lambda e: e.memset(S_bf[:], 0.0), writes=[S_bf])
    P.op("act", lambda e: e.activation(Arep[:], par[:, PALOG:PALOG + NH], AF.Exp), reads=[par], writes=[Arep])
    P.op("dve", lambda e: e.tensor_scalar(Arep[:], Arep[:], -1.0, None, ALU.mult), reads=[Arep], writes=[Arep])

    for st in range(n_super):
        for j in range(18):
            ws.add(wx_d, wx_d[j].rearrange("p a b -> p (a b)"), KT * 128)

    cnt = {"xs": 0, "sq": 0, "t": 0, "g": 0, "yn": 0}
    pending = []
    for st in range(n_super):
        xbcT = xbcT2[st % 2]
        szT = szT2[st % 2]
        t0 = st * ST
        for kt in range(KT):
            x_ = xs[cnt["xs"] % 3]; cnt["xs"] += 1
            P.dma(XQ, x_[:], xT[:, kt, t0:t0 + ST], reads=[xT], writes=[x_])
            sq = sqb[cnt["sq"] % 2]; cnt["sq"] += 1
            P.op("act", (lambda e, x_=x_, sq=sq: e.activation(sq[:], x_[:], AF.Square)), reads=[x_], writes=[sq])
            P.op("pe", (lambda e, sq=sq, kt=kt: e.matmul(b3[:], ones[:], sq[:], start=(kt == 0), stop=(kt == KT - 1))),
                 reads=[ones, sq], writes=[b3])
            P.op("dve", (lambda e, x_=x_, kt=kt: e.tensor_scalar(uT[:, kt * ST:(kt + 1) * ST], x_[:], par[:, PG + kt:PG + kt + 1], None, ALU.mult)),
                 reads=[x_, par], writes=[uT])
        P.op("act", lambda e: e.activation(rstd[:], b3[:], AF.Ln, bias=epsb[:, 0:1], scale=1.0 / D), reads=[b3, epsb], writes=[rstd])
        P.op("act", lambda e: e.activation(rstd[:], rstd[:], AF.Exp, scale=-0.5), reads=[rstd], writes=[rstd])
        for tt in range(4):
            P.op("pe", (lambda e, tt=tt: e.matmul(b2[:, 300 + tt:301 + tt], rstd[0:1, tt * 128:(tt + 1) * 128], onecol[0:1, 0:1],
                                                  start=True, stop=True)), reads=[rstd, onecol], writes=[b2])
        P.op("act", lambda e: e.copy(rs_tm[:, 0:4], b2[:, 300:304]), reads=[b2], writes=[rs_tm])
        for j in range(18):
            pb = pA[cnt["t"] % 2]
            rw = raw[cnt["t"] % 2]
            ac = acc[cnt["t"] % 2]
            cnt["t"] += 1
            w = ws.next()
            for k in range(KT):
                P.op("pe", (lambda e, w=w, k=k, pb=pb: e.matmul(pb[:], w[:, k * 128:(k + 1) * 128], uT[:, k * ST:(k + 1) * ST],
                                                                start=(k == 0), stop=(k == KT - 1))), reads=[w, uT], writes=[pb])
            if j < 8:
                P.op("dve", (lambda e, pb=pb, ac=ac: e.tensor_tensor(ac[:], pb[:], rstd[:], ALU.mult)), reads=[pb, rstd], writes=[ac])
                P.op("act", (lambda e, ac=ac, j=j: e.activation(szT[:, j * ST:(j + 1) * ST], ac[:], AF.Silu)), reads=[ac], writes=[szT])
                continue
            c = j - 8
            P.op("dve", (lambda e, pb=pb, rw=rw: e.tensor_tensor(rw[:, 3:3 + ST], pb[:], rstd[:], ALU.mult)), reads=[pb, rstd], writes=[rw])
            P.op("pool", (lambda e, rw=rw, c=c: e.tensor_copy(rw[:, 0:3], tails[:, 3 * c:3 * c + 3])), reads=[tails], writes=[rw])
            P.op("pool", (lambda e, rw=rw, c=c: e.tensor_copy(tails[:, 3 * c:3 * c + 3], rw[:, ST:ST + 3])), reads=[rw], writes=[tails])
            P.op("dve", (lambda e, rw=rw, ac=ac, c=c: e.tensor_scalar(ac[:], rw[:, 0:ST], par[:, PCW + 4 * c:PCW + 4 * c + 1],
                                                                    par[:, PCB + c:PCB + c + 1], ALU.mult, ALU.add)),
                 reads=[rw, par], writes=[ac])
            for kk in (1, 2, 3):
                P.op("dve", (lambda e, rw=rw, ac=ac, c=c, kk=kk: e.scalar_tensor_tensor(
                    ac[:], rw[:, kk:kk + ST], par[:, PCW + 4 * c + kk:PCW + 4 * c + kk + 1], ac[:], ALU.mult, ALU.add)),
                    reads=[rw, par, ac], writes=[ac])
            P.op("act", (lambda e, ac=ac, c=c: e.activation(xbcT[:, c * ST:(c + 1) * ST], ac[:], AF.Silu)), reads=[ac], writes=[xbcT])
        def stage1(st, tt, par_i, xbcT, szT):
            Xtm, Btm, Xdt, Xdte, MT, yd_sb = Xtm2[par_i], Btm2[par_i], Xdt2[par_i], Xdte2[par_i], MT2[par_i], yd_sb2[par_i]
            dtv, acs_sb, ea, cdb, de = dtv2[par_i], acs_sb2[par_i], ea2[par_i], cdb2[par_i], de2[par_i]
            c0 = tt * 128
            BT = xbcT[:, 8 * ST + c0:8 * ST + c0 + 128]
            CT = xbcT[:, 9 * ST + c0:9 * ST + c0 + 128]
            for k in range(KT):
                P.op("pe", (lambda e, k=k, c0=c0: e.matmul(b2[:, 272:288], uT[:, k * ST + c0:k * ST + c0 + 128], wdt[:, k * 16:(k + 1) * 16],
                                                           start=(k == 0), stop=(k == KT - 1))), reads=[uT, wdt], writes=[b2])
            P.op("dve", (lambda e, tt=tt: e.scalar_tensor_tensor(dtv[:], b2[:, 272:288], rs_tm[:, tt:tt + 1], par[:, PDTB:PDTB + NH], ALU.mult, ALU.add)),
                 reads=[b2, rs_tm, par], writes=[dtv])
            P.op("act", lambda e: e.activation(e1[:], dtv[:], AF.Exp), reads=[dtv], writes=[e1])
            P.op("act", lambda e: e.activation(dtv[:], e1[:], AF.Ln, bias=onecol[:, 0:1]), reads=[e1, onecol], writes=[dtv])
            P.op("dve", lambda e: e.tensor_tensor(av[:], dtv[:], Arep[:], ALU.mult), reads=[dtv, Arep], writes=[av])
            P.op("pe", lambda e: e.matmul(b2[:, 128:144], T, av[:], start=True, stop=True), reads=[cst, av], writes=[b2])
            P.op("pe", lambda e: e.matmul(b2[0:16, 144:272], av[:], T, start=True, stop=True), reads=[cst, av], writes=[b2])
            P.op("act", lambda e: e.copy(acs_sb[:], b2[:, 128:144]), reads=[b2], writes=[acs_sb])
            P.op("act", lambda e: e.copy(acsT_sb[:], b2[0:16, 144:272]), reads=[b2], writes=[acsT_sb])
            P.op("act", lambda e: e.activation(ea[:], acs_sb[:], AF.Exp), reads=[acs_sb], writes=[ea])
            for j in range(8):
                P.op("pe", (lambda e, j=j, c0=c0: e.transpose(b0[:, j * 128:(j + 1) * 128], xbcT[:, j * ST + c0:j * ST + c0 + 128], idb[:])),
                     reads=[xbcT, idb], writes=[b0])
            P.op("act", lambda e: e.copy(Xtm[:], b0[:]), reads=[b0], writes=[Xtm])
            P.op("pe", (lambda e, BT=BT: e.transpose(b0[:, 0:128], BT, idb[:])), reads=[xbcT, idb], writes=[b0])
            P.op("act", lambda e: e.copy(Btm[:], b0[:, 0:128]), reads=[b0], writes=[Btm])
            P.op("pe", (lambda e, BT=BT, CT=CT: e.matmul(b2[:, 0:128], BT, CT, start=True, stop=True)), reads=[xbcT], writes=[b2])
            for g in range(4):
                ar = arg[cnt["g"] % 2]; Ld = Ldec[cnt["g"] % 2]; cnt["g"] += 1
                for hh in range(4):
                    h = 4 * g + hh
                    P.op("pe", (lambda e, h=h, hh=hh: e.matmul(b3[:, hh * 128:(hh + 1) * 128], esel[0:16, h * 128:(h + 1) * 128], acsT_sb[:],
                                                              start=True, stop=True)), reads=[esel, acsT_sb], writes=[b3])
                b3v = b3[:, :].rearrange("p (h l) -> p h l", l=128)[:, :, 127]
                P.op("act", (lambda e, g=g, b3v=b3v: e.activation(cdb[:, 4 * g:4 * g + 4], b3v, AF.Exp)), reads=[b3], writes=[cdb])
                P.op("dve", (lambda e, g=g, b3v=b3v: e.tensor_tensor(dearg[:, 4 * g:4 * g + 4], b3v, acs_sb[:, 4 * g:4 * g + 4], ALU.subtract)),
                     reads=[b3, acs_sb], writes=[dearg])
                for hh in range(4):
                    h = 4 * g + hh
                    P.op("dve", (lambda e, h=h, hh=hh, ar=ar: e.scalar_tensor_tensor(
                        ar[:, hh * 128:(hh + 1) * 128], b3[:, hh * 128:(hh + 1) * 128], acs_sb[:, h:h + 1], maskb, ALU.subtract, ALU.add)),
                        reads=[b3, acs_sb, cst], writes=[ar])
                P.op("act", (lambda e, ar=ar, Ld=Ld: e.activation(Ld[:], ar[:], AF.Exp)), reads=[ar], writes=[Ld])
                for hh in range(4):
                    h = 4 * g + hh
                    P.op("dve", (lambda e, h=h, hh=hh, Ld=Ld: e.tensor_tensor(MT[:, h * 128:(h + 1) * 128], Ld[:, hh * 128:(hh + 1) * 128],
                                                                          b2[:, 0:128], ALU.mult)), reads=[Ld, b2], writes=[MT])
            P.op("act", lambda e: e.activation(de[:], dearg[:], AF.Exp), reads=[dearg], writes=[de])
            X3 = Xtm[:, :].rearrange("p (h d) -> p h d", d=64)
            Xd3 = Xdt[:, :].rearrange("p (h d) -> p h d", d=64)
            Xe3 = Xdte[:, :].rearrange("p (h d) -> p h d", d=64)
            dt_b = dtv[:, 0:NH].unsqueeze(2).broadcast_to([128, NH, 64])
            de_b = de[:, 0:NH].unsqueeze(2).broadcast_to([128, NH, 64])
            P.op(BCE, (lambda e: e.tensor_tensor(Xd3, X3, dt_b, ALU.mult)), reads=[Xtm, dtv], writes=[Xdt])
            P.op(BCE, (lambda e: e.tensor_tensor(Xe3, Xd3, de_b, ALU.mult)), reads=[Xdt, de], writes=[Xdte])
            for h in range(NH):
                P.op("pe", (lambda e, h=h: e.matmul(pA[h // 8][:, (h % 8) * 64:(h % 8 + 1) * 64], MT[:, h * 128:(h + 1) * 128],
                                                    Xdt[:, h * 64:(h + 1) * 64], start=True, stop=True)),
                     reads=[MT, Xdt], writes=[pA[h // 8]])
            for cc in range(2):
                P.op("act", (lambda e, cc=cc: e.copy(yd_sb[:, cc * 512:(cc + 1) * 512], pA[cc][:])), reads=[pA[cc]], writes=[yd_sb])

        def stage2(st, tt, par_i, xbcT, szT):
            Xtm, Btm, Xdt, Xdte, MT, yd_sb = Xtm2[par_i], Btm2[par_i], Xdt2[par_i], Xdte2[par_i], MT2[par_i], yd_sb2[par_i]
            dtv, acs_sb, ea, cdb, de = dtv2[par_i], acs_sb2[par_i], ea2[par_i], cdb2[par_i], de2[par_i]
            c0 = tt * 128
            t0 = st * ST
            CT = xbcT[:, 9 * ST + c0:9 * ST + c0 + 128]
            for cc in range(2):
                P.op("pe", (lambda e, cc=cc, CT=CT: e.matmul(pB[cc][:], CT, S_bf[:, cc * 512:(cc + 1) * 512], start=True, stop=True)),
                     reads=[xbcT, S_bf], writes=[pB[cc]])
            for h in range(NH):
                hs = slice(h * 64, (h + 1) * 64)
                ps_ = slice((h % 8) * 64, (h % 8 + 1) * 64)
                P.op("dve", (lambda e, h=h, hs=hs, ps_=ps_: e.scalar_tensor_tensor(yb[:, hs], pB[h // 8][:, ps_], ea[:, h:h + 1], yd_sb[:, hs],
                                                                               ALU.mult, ALU.add)), reads=[pB[h // 8], ea, yd_sb], writes=[yb])
            for h in range(NH):
                hs = slice(h * 64, (h + 1) * 64)
                P.op("dve", (lambda e, h=h, hs=hs: e.scalar_tensor_tensor(yb[:, hs], Xtm[:, hs], par[:, PDD + h:PDD + h + 1], yb[:, hs],
                                                                      ALU.mult, ALU.add)), reads=[Xtm, par, yb], writes=[yb])
            for cc in range(2):
                P.op("pe", (lambda e, cc=cc: e.matmul(pB[cc][:], Btm[:], Xdte[:, cc * 512:(cc + 1) * 512], start=True, stop=True)),
                     reads=[Btm, Xdte], writes=[pB[cc]])
            for h in range(NH):
                hs = slice(h * 64, (h + 1) * 64)
                ps_ = slice((h % 8) * 64, (h % 8 + 1) * 64)
                P.op("dve", (lambda e, h=h, hs=hs, ps_=ps_: e.scalar_tensor_tensor(S[:, hs], S[:, hs], cdb[:, h:h + 1], pB[h // 8][:, ps_],
                                                                               ALU.mult, ALU.add)), reads=[S, cdb, pB[h // 8]], writes=[S])
            P.op("act", lambda e: e.copy(S_bf[:], S[:]), reads=[S], writes=[S_bf])
            for j in range(8):
                P.op("pe", (lambda e, j=j, c0=c0: e.transpose(b1[:, j * 128:(j + 1) * 128], szT[:, j * ST + c0:j * ST + c0 + 128], idb[:])),
                     reads=[szT, idb], writes=[b1])
            P.op("dve", lambda e: e.tensor_tensor(yg[:], yb[:], b1[:], ALU.mult), reads=[yb, b1], writes=[yg])
            P.op("dve", lambda e: e.scalar_tensor_tensor(junk[:], yg[:], 1.0, yg[:], ALU.mult, ALU.mult, accum_out=ss[:, 0:1]),
                 reads=[yg], writes=[junk, ss])
            P.op("act", lambda e: e.activation(rsn[:], ss[:], AF.Ln, bias=epsb[:, 0:1], scale=1.0 / 1024), reads=[ss, epsb], writes=[rsn])
            P.op("act", lambda e: e.activation(rsn[:], rsn[:], AF.Exp, scale=-0.5), reads=[rsn], writes=[rsn])
            yo_ = yn[cnt["yn"] % 2]; cnt["yn"] += 1
            P.op("dve", (lambda e, yo_=yo_: e.scalar_tensor_tensor(yo_[:], yg[:], rsn[:, 0:1], par[:, PNG:PNG + 1024], ALU.mult, ALU.mult)),
                 reads=[yg, rsn, par], writes=[yo_])
            P.dma("pool", y_o[t0 + c0:t0 + c0 + 128, :], yo_[:], reads=[yo_], writes=[], owner=yo_)

        def capture(fn, *a):
            old = P.ops
            P.ops = []
            fn(*a)
            got = P.ops
            P.ops = old
            return got

        def merge(A, B):
            out = []
            ia = ib = 0
            while ia < len(A) or ib < len(B):
                if ib >= len(B) or (ia < len(A) and ia * len(B) <= ib * len(A)):
                    out.append(A[ia]); ia += 1
                else:
                    out.append(B[ib]); ib += 1
            return out

        for tt in range(4):
            ci = st * 4 + tt
            A = capture(stage1, st, tt, ci % 2, xbcT, szT)
            if pending:
                pst, ptt, ppar, pxb, psz = pending.pop()
                B = capture(stage2, pst, ptt, ppar, pxb, psz)
                P.ops.extend(merge(A, B))
            else:
                P.ops.extend(A)
            pending.append((st, tt, ci % 2, xbcT, szT))
    pst, ptt, ppar, pxb, psz = pending.pop()
    stage2(pst, ptt, ppar, pxb, psz)
    P.finish("sp")
    info = P.emit()
    return nc, info


NQT = 64
SCALE = 128 ** -0.5
NEG = -1.0e30
BIG = 30000.0


def build_attn(n_qt=NQT):
    nc = bass.Bass("TRN2", target_bir_lowering=False)
    P = Prog(nc)
    qT_d = P.dram("qT", [128, NQT, 512], BF16, kind="ExternalInput")
    kT_d = P.dram("kT", [128, 8192], BF16, kind="ExternalInput")
    v_d = P.dram("v", [128, 64, 128], BF16, kind="ExternalInput")
    tb_d = P.dram("tb", [128, 512], BF16, kind="ExternalInput")
    idb_d = P.dram("idb", [128, 128], BF16, kind="ExternalInput")
    en_d = P.dram("en", [32, 32 * 128], BF16, kind="ExternalInput")
    oT_o = P.dram("oT_o", [128, NQT, 512], BF16, kind="ExternalOutput")

    qT = P.sbuf([128, NQT * 512], BF16, "qT")
    kT = P.sbuf([128, 8192], BF16, "kT")
    vv = P.sbuf([128, 64 * 128], BF16, "vv")
    tb = P.sbuf([128, 512], BF16, "tb")
    idb = P.sbuf([128, 128], BF16, "idb")
    en = P.sbuf([32, 32 * 128], BF16, "en")
    ones = P.sbuf([128, 128], BF16, "ones")
    km = P.sbuf([128, 32], F32, "km")
    kmh = P.sbuf([128, 32], BF16, "kmh")
    kmhf = P.sbuf([128, 32], F32, "kmhf")
    kml = P.sbuf([128, 32], BF16, "kml")
    g_sb = P.sbuf([128, 128], F32, "g_sb")
    t8 = P.sbuf([128, 32], F32, "t8")
    selb = P.sbuf([128, 128], BF16, "selb")
    bias2 = [P.sbuf([32, 512], BF16, f"bias2_{i}") for i in range(2)]
    pT = [P.sbuf([128, 512], BF16, f"pT{i}") for i in range(4)]
    rec = P.sbuf([128, 512], F32, "rec")
    ob = [P.sbuf([128, 512], BF16, f"ob{i}") for i in range(2)]
    sTb = [P.psum([128, 512], F32, f"sT{i}") for i in range(3)]
    oTb = [P.psum([128, 512], F32, f"oT{i}") for i in range(2)]
    smb = [P.psum([128, 512], F32, f"sm{i}") for i in range(2)]
    gpb = P.psum([128, 1024], BF16, "gpb")
    gp = sTb[0]

    P.dma("sp", qT[:], qT_d[:].rearrange("p a b -> p (a b)"), reads=[qT_d], writes=[qT])
    P.dma("pool", kT[:], kT_d[:], reads=[kT_d], writes=[kT])
    P.dma("pool", vv[:], v_d[:].rearrange("p a b -> p (a b)"), reads=[v_d], writes=[vv])
    P.dma("pool", tb[:], tb_d[:], reads=[tb_d], writes=[tb])
    P.dma("pool", idb[:], idb_d[:], reads=[idb_d], writes=[idb])
    P.dma("pool", en[:], en_d[:], reads=[en_d], writes=[en])
    P.op("pool", lambda e: e.memset(ones[:], 1.0), writes=[ones])
    P.op("dve", lambda e: e.tensor_reduce(km[:], kT[:, :].rearrange("p (n k) -> p n k", k=256), AX.X, ALU.add), reads=[kT], writes=[km])
    P.op("dve", lambda e: e.tensor_scalar(km[:], km[:], 1.0 / 256, None, ALU.mult), reads=[km], writes=[km])
    P.op("dve", lambda e: e.tensor_copy(kmh[:], km[:]), reads=[km], writes=[kmh])
    P.op("dve", lambda e: e.tensor_copy(kmhf[:], kmh[:]), reads=[kmh], writes=[kmhf])
    P.op("dve", lambda e: e.tensor_tensor(kml[:], km[:], kmhf[:], ALU.subtract), reads=[km, kmhf], writes=[kml])

    LA = 2
    steps = []
    for qt in range(n_qt):
        blk = qt // 2
        half = qt % 2
        tiles = [(2 * n + i, n, False) for n in range(blk) for i in range(2)]
        if half == 0:
            tiles.append((2 * blk, None, True))
        else:
            tiles.append((2 * blk, None, False))
            tiles.append((2 * blk + 1, None, True))
        for ti, (kti, n, diag) in enumerate(tiles):
            steps.append((qt, ti, len(tiles), kti, n, diag))

    def prologue(qt):
        blk = qt // 2
        b2_ = bias2[qt % 2]
        if blk == 0:
            return
        for h in range(4):
            P.op("pe", (lambda e, h=h: e.matmul(gp[:, h * 32:h * 32 + blk], qT[:, qt * 512 + h * 128:qt * 512 + (h + 1) * 128],
                                                kmh[:, 0:blk], start=True, stop=False)), reads=[qT, kmh], writes=[gp])
            P.op("pe", (lambda e, h=h: e.matmul(gp[:, h * 32:h * 32 + blk], qT[:, qt * 512 + h * 128:qt * 512 + (h + 1) * 128],
                                                kml[:, 0:blk], start=False, stop=True)), reads=[qT, kml], writes=[gp])
        P.op("pool", lambda e: e.memset(g_sb[:], NEG), writes=[g_sb])
        g3 = g_sb[:, :].rearrange("p (h n) -> p h n", n=32)[:, :, 0:blk]
        p3 = gp[:, 0:128].rearrange("p (h n) -> p h n", n=32)[:, :, 0:blk]
        P.op("act", (lambda e: e.copy(g3, p3)), reads=[gp], writes=[g_sb])
        for h in range(4):
            P.op("dve", (lambda e, h=h: e.max(t8[:, h * 8:(h + 1) * 8], g_sb[:, h * 32:(h + 1) * 32])), reads=[g_sb], writes=[t8])
        for h in range(4):
            P.op("dve", (lambda e, h=h: e.tensor_scalar(selb[:, h * 32:(h + 1) * 32], g_sb[:, h * 32:(h + 1) * 32],
                                                     t8[:, h * 8 + 2:h * 8 + 3], 1.0, ALU.is_ge, ALU.subtract)), reads=[g_sb, t8], writes=[selb])
        for h in range(4):
            P.op("pe", (lambda e, h=h: e.transpose(gpb[0:32, h * 128:(h + 1) * 128], selb[:, h * 32:(h + 1) * 32], idb[:])),
                 reads=[selb, idb], writes=[gpb])
        P.op("act", (lambda e: e.copy(b2_[:], gpb[0:32, 0:512])), reads=[gpb], writes=[b2_])

    def front(i):
        qt, ti, nt, kti, n, diag = steps[i]
        if ti == 0:
            prologue(qt)
        sT = sTb[1 + i % 2]
        pt = pT[i % 4]
        q_rhs = qT[:, qt * 512:(qt + 1) * 512]
        b2_ = bias2[qt % 2]
        extra = (n is not None) or diag
        P.op("pe", (lambda e: e.matmul(sT[:], kT[:, kti * 128:(kti + 1) * 128], q_rhs, start=True, stop=not extra)),
             reads=[kT, qT], writes=[sT])
        if n is not None:
            P.op("pe", (lambda e: e.matmul(sT[:], en[0:32, n * 128:(n + 1) * 128], b2_[:], start=False, stop=True)),
                 reads=[en, b2_], writes=[sT])
        elif diag:
            P.op("pe", (lambda e: e.matmul(sT[:], idb[:], tb[:], start=False, stop=True)), reads=[idb, tb], writes=[sT])
        P.op("act", (lambda e: e.activation(pt[:], sT[:], AF.Exp, scale=SCALE)), reads=[sT], writes=[pt])

    def back(i):
        qt, ti, nt, kti, n, diag = steps[i]
        pt = pT[i % 4]
        oT = oTb[qt % 2]
        sm = smb[qt % 2]
        first = ti == 0
        last = ti == nt - 1
        P.op("pe", (lambda e: e.matmul(oT[:], vv[:, kti * 128:(kti + 1) * 128], pt[:], start=first, stop=last)), reads=[vv, pt], writes=[oT])
        P.op("pe", (lambda e: e.matmul(sm[:], ones[:], pt[:], start=first, stop=last)), reads=[ones, pt], writes=[sm])
        if last:
            P.op("dve", (lambda e: e.reciprocal(rec[:], sm[:])), reads=[sm], writes=[rec])
            o_ = ob[qt % 2]
            P.op("dve", (lambda e: e.tensor_tensor(o_[:], oT[:], rec[:], ALU.mult)), reads=[oT, rec], writes=[o_])
            P.dma("pool", oT_o[:, qt, :], o_[:], reads=[o_], writes=[], owner=o_)

    for idx in range(len(steps) + LA):
        if idx < len(steps):
            front(idx)
        if idx - LA >= 0:
            back(idx - LA)
    P.finish("sp")
    info = P.emit()
    return nc, info


CCH = 8192


def build_cast(M):
    nc = bass.Bass("TRN2", target_bir_lowering=False)
    P = Prog(nc)
    w_d = P.dram("w", [128, M], F32, kind="ExternalInput")
    wb_d = P.dram("wb", [128, M], BF16, kind="ExternalOutput")
    NB = 3
    fb = [P.sbuf([128, CCH], F32, f"fb{i}") for i in range(NB)]
    bb = [P.sbuf([128, CCH], BF16, f"bb{i}") for i in range(NB)]
    n = M // CCH
    engs = ("dve", "act", "pool")
    for i in range(n):
        f = fb[i % NB]
        b = bb[i % NB]
        P.dma("sp", f[:], w_d[:, i * CCH:(i + 1) * CCH], reads=[w_d], writes=[f])
        en = engs[i % 3]
        if en == "act":
            P.op("act", lambda e: e.copy(b[:], f[:]), reads=[f], writes=[b])
        else:
            P.op(en, lambda e: e.tensor_copy(b[:], f[:]), reads=[f], writes=[b])
        P.dma("pool", wb_d[:, i * CCH:(i + 1) * CCH], b[:], reads=[b], writes=[], owner=b)
    P.finish("sp")
    P.emit()
    return nc


_bf = ml_dtypes.bfloat16
NCORES = 8


def _run(nc, in_maps):
    res = run_bass_kernel_spmd(nc, in_maps, core_ids=list(range(NCORES)))
    return res.results


def _fm(a, kt):
    n = a.shape[0]
    return np.ascontiguousarray(a.T.reshape(kt, 128, n).transpose(1, 0, 2))


def _tok_par(g_ffn, g2, g3, cw, cb):
    par = np.zeros((128, 4 * 32 + 172 * 4), np.float32)
    par[:, 0:32] = g_ffn.reshape(32, 128).T
    par[:, 32:64] = g2.reshape(32, 128).T
    if g3 is not None:
        par[:, 64:96] = g3.reshape(32, 128).T
    par[:, 128:128 + 516] = cw.reshape(3, 172, 128).transpose(2, 1, 0).reshape(128, 516)
    par[:, 128 + 516:] = cb.reshape(172, 128).T
    return par


def _cs_table(base):
    inv = (10000.0 ** (-np.arange(64, dtype=np.float32) / 64)).astype(np.float32)
    tab = np.zeros((128, 9, 128), np.float32)
    for pi in range(3):
        for tt in range(3):
            p = (base + 342 * pi + 128 * tt + np.arange(128) - 2).astype(np.float32)
            a = p[:, None] * inv[None, :]
            tab[:, pi * 3 + tt] = np.concatenate([np.cos(a), np.sin(a)], 1)
    return tab


def kernel(x, norm_a, mamba_w_in, mamba_conv_w, mamba_conv_b, mamba_dt_bias, mamba_a_log,
           mamba_d, mamba_norm, mamba_w_out, kv_norm, w_kv, norm_b, w_q, w_o,
           ffn_norm, ffn_w_up, ffn_conv_w, ffn_conv_b, ffn_w_down, final_norm):
    f32 = np.float32
    x = np.asarray(x, f32)[0]
    w_in = np.asarray(mamba_w_in, f32)[0]

    shapes = []
    for g in range(8):
        shapes.append((f"wx{g}", (18, 128, 32, 128)))
        shapes.append((f"wdt{g}", (128, 512)))
    shapes += [("wpA", (32, 128, 64, 128)), ("wuA", (86, 128, 2, 32, 128)), ("wdA", (32, 128, 86, 128)), ("wqA", (12, 128, 32, 512)),
               ("wpB", (32, 128, 32, 128)), ("wuB", (86, 128, 2, 32, 128)), ("wdB", (32, 128, 86, 128))]
    offs = {}
    tot = 0
    for nm, sh in shapes:
        offs[nm] = (tot, sh)
        tot += int(np.prod(sh))
    per = -(-tot // (NCORES * 128 * CCH)) * CCH
    flat = np.zeros(NCORES * 128 * per, f32)

    def put(nm, arr):
        o, sh = offs[nm]
        flat[o:o + arr.size].reshape(sh)[...] = arr

    for g in range(8):
        wst = np.concatenate([w_in[:, g * 1024:(g + 1) * 1024], w_in[:, 8192 + g * 1024:8192 + (g + 1) * 1024],
                              w_in[:, 16384 + g * 128:16384 + (g + 1) * 128], w_in[:, 17408 + g * 128:17408 + (g + 1) * 128]], 1)
        put(f"wx{g}", wst.reshape(32, 128, 18, 128).transpose(2, 1, 0, 3))
        put(f"wdt{g}", w_in[:, 18432 + g * 16:18432 + (g + 1) * 16].reshape(32, 128, 16).transpose(1, 0, 2).reshape(128, 512))
    put("wpA", np.asarray(mamba_w_out, f32)[0].reshape(64, 128, 32, 128).transpose(2, 1, 0, 3))
    put("wuA", np.asarray(ffn_w_up, f32)[0].reshape(32, 128, 2, 86, 128).transpose(3, 1, 2, 0, 4))
    put("wdA", np.asarray(ffn_w_down, f32)[0].reshape(86, 128, 32, 128).transpose(2, 1, 0, 3))
    wqkv = np.concatenate([np.asarray(w_q, f32)[0], np.asarray(w_kv, f32)], 1)
    put("wqA", wqkv.reshape(32, 128, 12, 512).transpose(2, 1, 0, 3))
    put("wpB", np.asarray(w_o, f32)[0].reshape(32, 128, 32, 128).transpose(2, 1, 0, 3))
    put("wuB", np.asarray(ffn_w_up, f32)[1].reshape(32, 128, 2, 86, 128).transpose(3, 1, 2, 0, 4))
    put("wdB", np.asarray(ffn_w_down, f32)[1].reshape(86, 128, 32, 128).transpose(2, 1, 0, 3))
    del wqkv
    fl = flat.reshape(NCORES, 128, per)
    res = _run(build_cast(per), [{"w": fl[c]} for c in range(NCORES)])
    wbf = np.concatenate([r["wb"].reshape(-1) for r in res])
    del flat, fl, res

    def getw(nm):
        o, sh = offs[nm]
        return wbf[o:o + int(np.prod(sh))].reshape(sh)

    xT_l = _fm(x, 32)
    cst = np.zeros((128, 256), f32)
    kk = np.arange(128)
    cst[:, 0:128] = (kk[:, None] <= kk[None, :]).astype(f32)
    cst[:, 128:256] = np.where(kk[:, None] <= kk[None, :], 0.0, -30000.0)
    esel = np.zeros((16, 2048), f32)
    for h in range(16):
        esel[h, h * 128:(h + 1) * 128] = 1.0
    idb = np.eye(128, dtype=f32).astype(_bf)
    tri = cst[:, 0:128].astype(_bf)
    cwm = np.asarray(mamba_conv_w, f32)[0]
    cbm = np.asarray(mamba_conv_b, f32)[0]
    in_maps = []
    for g in range(8):
        par = np.zeros((128, NPAR), f32)
        par[:, PG:PG + 32] = np.asarray(norm_a, f32)[0].reshape(32, 128).T
        idx = np.concatenate([np.arange(g * 1024, (g + 1) * 1024), 8192 + np.arange(g * 128, (g + 1) * 128),
                              9216 + np.arange(g * 128, (g + 1) * 128)])
        par[:, PCW:PCW + 40] = cwm[:, idx].reshape(4, 10, 128).transpose(2, 1, 0).reshape(128, 40)
        par[:, PCB:PCB + 10] = cbm[idx].reshape(10, 128).T
        par[:, PDTB:PDTB + 16] = np.asarray(mamba_dt_bias, f32)[0][None, g * 16:(g + 1) * 16]
        par[:, PALOG:PALOG + 16] = np.asarray(mamba_a_log, f32)[0][None, g * 16:(g + 1) * 16]
        par[:, PDD:PDD + 16] = np.asarray(mamba_d, f32)[0][None, g * 16:(g + 1) * 16]
        par[:, PNG:PNG + 1024] = np.asarray(mamba_norm, f32)[0][None, g * 1024:(g + 1) * 1024]
        in_maps.append({"xT": xT_l, "wx": getw(f"wx{g}"), "wdt": getw(f"wdt{g}"), "par": par, "cst": cst, "esel": esel, "idb": idb})
    nc1, _ = build_mamba()
    res = _run(nc1, in_maps)
    Y = np.concatenate([r["y_o"] for r in res], 1)
    del in_maps, res, xT_l

    def tok_inputs(act_bf, h_f32, ktin, c):
        a = np.zeros((1026, act_bf.shape[1]), act_bf.dtype)
        hh = np.zeros((1026, 4096), f32)
        lo = 1024 * c - 2
        s0 = max(lo, 0)
        a[s0 - lo:] = act_bf[s0:1024 * c + 1024]
        hh[s0 - lo:] = h_f32[s0:1024 * c + 1024]
        return _fm(a, ktin), _fm(hh, 32)

    parA = _tok_par(np.asarray(ffn_norm, f32)[0], np.asarray(kv_norm, f32), np.asarray(norm_b, f32)[0],
                    np.asarray(ffn_conv_w, f32)[0], np.asarray(ffn_conv_b, f32)[0])
    in_maps = []
    for c in range(8):
        inT, hT = tok_inputs(Y, x, 64, c)
        in_maps.append({"inT": inT, "hT": hT, "wp": getw("wpA"), "wu": getw("wuA"), "wd": getw("wdA"), "wq": getw("wqA"),
                        "par": parA, "cs": _cs_table(1024 * c)})
    nc2, _ = build_tok("A", 64)
    res = _run(nc2, in_maps)
    h1 = np.concatenate([r["hoT"].transpose(2, 1, 0).reshape(1024, 4096) for r in res], 0)
    Q = np.concatenate([r["q_o"] for r in res], 0)
    KV = np.concatenate([r["kv_o"] for r in res], 0)
    del in_maps, res, Y

    tb = np.tile(np.where(kk[:, None] <= kk[None, :], 0.0, -BIG).astype(f32), (1, 4)).astype(_bf)
    en = np.zeros((32, 32 * 128), f32)
    for n_ in range(32):
        en[n_, n_ * 128:(n_ + 1) * 128] = BIG
    en = en.astype(_bf)
    in_maps = []
    for kv in range(8):
        q = Q[:, kv * 512:(kv + 1) * 512]
        qT = np.ascontiguousarray(q.reshape(64, 128, 4, 128).transpose(3, 0, 2, 1).reshape(128, 64, 512))
        kT = np.ascontiguousarray(KV[:, kv * 128:(kv + 1) * 128].T)
        vl = np.ascontiguousarray(KV[:, 1024 + kv * 128:1024 + (kv + 1) * 128].reshape(64, 128, 128).transpose(1, 0, 2))
        in_maps.append({"qT": qT, "kT": kT, "v": vl, "tb": tb, "idb": idb, "en": en})
    nc3, _ = build_attn()
    res = _run(nc3, in_maps)
    OT = np.stack([r["oT_o"].reshape(128, 64, 4, 128) for r in res], 0)
    OT = np.ascontiguousarray(OT.transpose(1, 0, 3, 2, 4).reshape(128, 32, 8192))
    del in_maps, res, Q, KV

    parB = _tok_par(np.asarray(ffn_norm, f32)[1], np.asarray(final_norm, f32), None,
                    np.asarray(ffn_conv_w, f32)[1], np.asarray(ffn_conv_b, f32)[1])
    in_maps = []
    for c in range(8):
        inT = np.zeros((128, 32, 1026), _bf)
        lo = 1024 * c - 2
        s0 = max(lo, 0)
        inT[:, :, s0 - lo:] = OT[:, :, s0:1024 * c + 1024]
        hh = np.zeros((1026, 4096), f32)
        hh[s0 - lo:] = h1[s0:1024 * c + 1024]
        hT = _fm(hh, 32)
        in_maps.append({"inT": inT, "hT": hT, "wp": getw("wpB"), "wu": getw("wuB"), "wd": getw("wdB"), "par": parB})
    nc4, _ = build_tok("B", 32)
    res = _run(nc4, in_maps)
    out = np.concatenate([r["hoT"].transpose(2, 1, 0).reshape(1024, 4096) for r in res], 0)
    return np.ascontiguousarray(out.astype(f32))[None]
```

```python
import numpy as np
import ml_dtypes
import concourse.bass as bass
import concourse.mybir as mybir
from concourse.bass_utils import run_bass_kernel_spmd

F32 = mybir.dt.float32
F32R = mybir.dt.float32r
BF16 = mybir.dt.bfloat16
AF = mybir.ActivationFunctionType
ALU = mybir.AluOpType
AX = mybir.AxisListType


class Buf:
    __slots__ = ("t", "name", "last_w", "readers", "dsem", "dcnt")

    def __init__(self, t, name):
        self.t = t
        self.name = name
        self.last_w = None
        self.readers = []
        self.dsem = None
        self.dcnt = 0

    def __getitem__(self, k):
        return self.t[k]


class _Rec:
    def __init__(self):
        self.call = None

    def __getattr__(self, name):
        def f(*a, **k):
            self.call = (name, a, k)
            return None
        return f


class Op:
    __slots__ = ("call", "eng", "fn", "reads", "writes", "is_dma", "owner", "deps", "need_inc", "semval", "idx", "is_final", "raw")


class Prog:
    ENGS = ("pe", "act", "dve", "pool", "sp")

    def __init__(self, nc, same_engine_sync=True):
        self.nc = nc
        self.eng = {"pe": nc.tensor, "act": nc.scalar, "dve": nc.vector, "pool": nc.gpsimd, "sp": nc.sync}
        self.ops = []
        self.same_engine_sync = same_engine_sync
        self.nbuf = 0

    def sbuf(self, shape, dtype, name=None):
        self.nbuf += 1
        name = "s_" + (name or f"sb{self.nbuf}")
        return Buf(self.nc.alloc_sbuf_tensor(name, list(shape), dtype), name)

    def psum(self, shape, dtype=F32, name=None):
        self.nbuf += 1
        name = "p_" + (name or f"ps{self.nbuf}")
        return Buf(self.nc.alloc_psum_tensor(name, list(shape), dtype), name)

    def dram(self, name, shape, dtype, kind="Internal"):
        return Buf(self.nc.dram_tensor(name, list(shape), dtype, kind=kind), name)

    def op(self, eng, fn, reads=(), writes=()):
        o = Op()
        o.eng = eng
        o.fn = None
        r = _Rec()
        fn(r)
        o.call = r.call
        o.reads = tuple(reads)
        o.writes = tuple(writes)
        o.is_dma = False
        o.owner = None
        o.need_inc = False
        o.is_final = False
        o.idx = len(self.ops)
        self.ops.append(o)
        return o

    def dma(self, queue, out_ap, in_ap, reads=(), writes=(), owner=None, **kw):
        o = self.op(queue, lambda e: e.dma_start(out=out_ap, in_=in_ap, **kw), reads, writes)
        o.is_dma = True
        o.owner = owner if owner is not None else (writes[0] if writes else reads[0])
        return o

    def emit(self):
        nc = self.nc
        ops = self.ops
        for i_, o_ in enumerate(ops):
            o_.idx = i_
        for o in ops:
            deps = set()
            raw = set()
            for b in o.reads:
                if b.last_w is not None:
                    deps.add(b.last_w)
                    raw.add(b.last_w)
            o.raw = raw
            for b in o.writes:
                if b.last_w is not None:
                    deps.add(b.last_w)
                deps.update(b.readers)
            for b in o.reads:
                if (not o.is_dma) and b.readers:
                    q = ops[b.readers[-1]]
                    if (not q.is_dma) and q.eng == o.eng:
                        b.readers[-1] = o.idx
                        continue
                b.readers.append(o.idx)
            for b in o.writes:
                b.last_w = o.idx
                b.readers = []
            deps.discard(o.idx)
            if o.is_final:
                last = {}
                for q in ops[:o.idx]:
                    if q.is_dma:
                        deps.add(q.idx)
                    else:
                        last[q.eng] = q.idx
                deps.update(last.values())
            o.deps = deps
            for d in deps:
                p = ops[d]
                if p.is_dma:
                    continue
                if p.eng != o.eng or (self.same_engine_sync and o.eng != "pe" and d in o.raw):
                    p.need_inc = True
                if p.eng == o.eng and o.is_dma:
                    p.need_inc = True
        esem = {e: nc.alloc_semaphore(f"es_{e}") for e in self.ENGS}
        ecnt = {e: 0 for e in self.ENGS}
        for o in ops:
            if o.is_dma:
                b = o.owner
                if b.dsem is None:
                    b.dsem = nc.alloc_semaphore(f"ds_{b.name}")
                b.dcnt += 16
                o.semval = (b.dsem, b.dcnt)
            elif o.need_inc:
                ecnt[o.eng] += 1
                o.semval = (esem[o.eng], ecnt[o.eng])
            else:
                o.semval = None
        waited = {e: {} for e in self.ENGS}
        nwaits = 0
        for o in ops:
            e = self.eng[o.eng]
            w = waited[o.eng]
            need = {}
            for d in o.deps:
                p = ops[d]
                if p.semval is None:
                    continue
                if (not p.is_dma) and p.eng == o.eng and not (o.is_dma or (self.same_engine_sync and o.eng != "pe" and d in o.raw)):
                    continue
                s, v = p.semval
                key = id(s)
                if w.get(key, 0) >= v:
                    continue
                if key not in need or need[key][1] < v:
                    need[key] = (s, v)
            for key, (s, v) in need.items():
                e.wait_ge(s, v)
                w[key] = v
                nwaits += 1
            ins = None
            if o.call is not None:
                nm, a_, k_ = o.call
                ins = getattr(e, nm)(*a_, **k_)
            if o.semval is not None and ins is not None:
                ins.then_inc(o.semval[0], 16 if o.is_dma else 1)
        self.final = {"esem": esem, "ecnt": ecnt, "nwaits": nwaits}
        return self.final

    def finish(self, eng="sp"):
        o = self.op(eng, lambda e: None)
        o.is_final = True


D = 4096
DT = 32
FF = 11008
FT = 86
TP = 342
EPS = 1e-5
WU = 43 * 128


class WStream:
    def __init__(self, P, nbuf, unit_cols, queue="sp", pf=3):
        self.P = P
        self.bufs = [P.sbuf([128, unit_cols], BF16, f"wp{i}") for i in range(nbuf)]
        self.reqs = []
        self.issued = 0
        self.taken = 0
        self.queue = queue
        self.pf = pf

    def add(self, dbuf, ap, ncols):
        self.reqs.append((dbuf, ap, ncols))

    def _issue(self, i):
        dbuf, ap, ncols = self.reqs[i]
        b = self.bufs[i % len(self.bufs)]
        self.P.dma(self.queue, b[:, 0:ncols], ap, reads=[dbuf], writes=[b])

    def next(self):
        i = self.taken
        lim = min(len(self.reqs), i + self.pf + 1)
        while self.issued < lim:
            self._issue(self.issued)
            self.issued += 1
        self.taken += 1
        return self.bufs[i % len(self.bufs)]


def build_tok(mode, KTIN):
    nc = bass.Bass("TRN2", target_bir_lowering=False)
    P = Prog(nc)
    NTOK = 1026
    inT = P.dram("inT", [128, KTIN, NTOK], BF16, kind="ExternalInput")
    hT = P.dram("hT", [128, DT, NTOK], F32, kind="ExternalInput")
    wp_d = P.dram("wp", [DT, 128, KTIN, 128], BF16, kind="ExternalInput")
    wu_d = P.dram("wu", [FT, 128, 2, DT, 128], BF16, kind="ExternalInput")
    wd_d = P.dram("wd", [DT, 128, FT, 128], BF16, kind="ExternalInput")
    par_d = P.dram("par", [128, 4 * DT + 172 * 4], F32, kind="ExternalInput")
    if mode == "A":
        wq_d = P.dram("wq", [12, 128, DT, 512], BF16, kind="ExternalInput")
        cs_d = P.dram("cs", [128, 9, 128], F32, kind="ExternalInput")
        q_o = P.dram("q_o", [1024, 4096], BF16, kind="ExternalOutput")
        kv_o = P.dram("kv_o", [1024, 2048], BF16, kind="ExternalOutput")
    hoT = P.dram("hoT", [128, DT, 1024], F32, kind="ExternalOutput")
    hmid_d = P.dram("hmid", [128, DT, 1024], F32)
    if mode == "B":
        hout_d = P.dram("hout", [128, DT, 1024], F32)
    else:
        hout_d = hoT
    hmid_v = [Buf(hmid_d.t, f"hmid{i}") for i in range(DT)]
    hout_v = [Buf(hout_d.t, f"hout{i}") for i in range(DT)]

    par = P.sbuf([128, 4 * DT + 172 * 4], F32, "par")
    G_FFN, G_2, G_3, CW, CB = 0, DT, 2 * DT, 4 * DT, 4 * DT + 172 * 3
    big = P.sbuf([128, FT * TP], BF16, "big")
    hn = P.sbuf([128, DT * TP], BF16, "hn")
    ones = P.sbuf([128, 128], BF16, "ones")
    rstd = P.sbuf([128, TP], F32, "rstd")
    tails = P.sbuf([128, 172 * 2], F32, "tails")
    ws = WStream(P, 6, WU, "sp", pf=5)
    stg = [P.sbuf([128, TP], F32, f"stg{i}") for i in range(3)]
    sqb = [P.sbuf([128, TP], BF16, f"sq{i}") for i in range(2)]
    ub = [P.sbuf([128, 2 * (TP + 2)], F32, f"ub{i}") for i in range(2)]
    cv = [P.sbuf([128, 2 * TP], F32, f"cv{i}") for i in range(2)]
    ps = [P.psum([128, 512], F32, f"psb{i}") for i in range(8)]
    PS_MM = (ps[0], ps[1])
    PS_UP = ((ps[2], ps[3]), (ps[4], ps[5]))
    PS_SS = ps[6]
    PS_TM = ps[7]
    if mode == "A":
        cs = P.sbuf([128, 9 * 128], F32, "cs")
        rs_tm = P.sbuf([128, 4], F32, "rs_tm")
        qst = cv
        qrt = ub
        qob = [P.sbuf([128, 512], BF16, f"qob{i}") for i in range(2)]
        onecol = P.sbuf([128, 1], F32, "onecol")

    P.dma("pool", par[:], par_d[:], reads=[par_d], writes=[par])
    P.op("pool", lambda e: e.memset(ones[:], 1.0), writes=[ones])
    P.op("pool", lambda e: e.memset(tails[:], 0.0), writes=[tails])
    for sg_ in stg:
        P.op("pool", (lambda e, sg_=sg_: e.memset(sg_[:], 0.0)), writes=[sg_])
    if mode == "A":
        P.dma("pool", cs[:], cs_d[:].rearrange("p a b -> p (a b)"), reads=[cs_d], writes=[cs])
        P.op("pool", lambda e: e.memset(onecol[:], 1.0), writes=[onecol])

    passes = [(0, TP, False), (TP, TP, False), (2 * TP, TP, False)]
    for (c0, n, halo) in passes:
        for dt in range(DT):
            for h0 in range(0, KTIN, 32):
                ws.add(wp_d, wp_d[dt, :, h0:h0 + 32, :].rearrange("p a b -> p (a b)"), 32 * 128)
        for j in range(FT):
            for half in range(2):
                ws.add(wu_d, wu_d[j, :, half].rearrange("p b c -> p (b c)"), DT * 128)
        if halo:
            continue
        for dt in range(DT):
            for h0 in range(0, FT, 43):
                ws.add(wd_d, wd_d[dt, :, h0:h0 + 43, :].rearrange("p a b -> p (a b)"), 43 * 128)
        if mode == "A":
            for ch in range(12):
                for k0 in range(0, DT, 8):
                    ws.add(wq_d, wq_d[ch, :, k0:k0 + 8, :].rearrange("p a b -> p (a b)"), 8 * 512)

    cnt = {"mm": 0, "up": 0, "stg": 0, "sq": 0, "q": 0}

    def rstd_from(ssq_ps, out_ap, n, eng_cols):
        P.op("act", lambda e: e.activation(out_ap, ssq_ps[:, 0:n], AF.Sqrt, bias=epsb[:, 0:1], scale=1.0 / D),
             reads=[ssq_ps, epsb], writes=[rstd])
        P.op("dve", lambda e: e.reciprocal(out_ap, out_ap), reads=[rstd], writes=[rstd])

    epsb = P.sbuf([128, 1], F32, "epsb")
    P.op("pool", lambda e: e.memset(epsb[:], EPS), writes=[epsb])

    for pi, (c0, n, halo) in enumerate(passes):
        o0 = c0 - 2
        lo = 2 if c0 == 0 else 0
        P.dma("sp", big[:, 0:KTIN * TP].rearrange("p (k t) -> p k t", t=TP)[:, :, 0:n], inT[:, :, c0:c0 + n],
              reads=[inT], writes=[big])
        for dt in range(DT):
            pb = PS_MM[cnt["mm"] % 2]; cnt["mm"] += 1
            for h0 in range(0, KTIN, 32):
                w = ws.next()
                for k in range(32):
                    kk = h0 + k
                    P.op("pe", (lambda e, w=w, k=k, kk=kk, pb=pb: e.matmul(
                        pb[:, 0:n], w[:, k * 128:(k + 1) * 128], big[:, kk * TP:kk * TP + n],
                        start=(kk == 0), stop=(kk == KTIN - 1))), reads=[w, big], writes=[pb])
            sg = stg[cnt["stg"] % 3]; cnt["stg"] += 1
            P.dma("pool", sg[:, 0:n], hT[:, dt, c0:c0 + n], reads=[hT], writes=[sg])
            P.op("dve", (lambda e, sg=sg, pb=pb: e.tensor_tensor(sg[:, 0:n], pb[:, 0:n], sg[:, 0:n], ALU.add)),
                 reads=[pb, sg], writes=[sg])
            if not halo:
                P.dma("pool", hmid_d[:, dt, o0 + lo:o0 + n], sg[:, lo:n], reads=[sg], writes=[hmid_v[dt]], owner=sg)
            sq = sqb[cnt["sq"] % 2]; cnt["sq"] += 1
            P.op("act", (lambda e, sg=sg, sq=sq: e.activation(sq[:, 0:n], sg[:, 0:n], AF.Square)), reads=[sg], writes=[sq])
            P.op("pe", (lambda e, sq=sq, dt=dt: e.matmul(PS_SS[:, 0:n], ones[:], sq[:, 0:n], start=(dt == 0), stop=(dt == DT - 1))),
                 reads=[ones, sq], writes=[PS_SS])
            P.op("dve", (lambda e, sg=sg, dt=dt: e.tensor_scalar(hn[:, dt * TP:dt * TP + n], sg[:, 0:n],
                                                              par[:, G_FFN + dt:G_FFN + dt + 1], None, ALU.mult)),
                 reads=[sg, par], writes=[hn])
        rstd_from(PS_SS, rstd[:, 0:n], n, 128)
        for j in range(FT):
            pg, pv = PS_UP[cnt["up"] % 2]
            u = ub[cnt["up"] % 2]
            c = cv[cnt["up"] % 2]
            cnt["up"] += 1
            for half, pb in ((0, pg), (1, pv)):
                w = ws.next()
                for k in range(DT):
                    P.op("pe", (lambda e, w=w, k=k, half=half, pb=pb: e.matmul(
                        pb[:, 0:n], w[:, k * 128:(k + 1) * 128], hn[:, k * TP:k * TP + n],
                        start=(k == 0), stop=(k == DT - 1))), reads=[w, hn], writes=[pb])
            ug = u[:, 0:TP + 2]
            uv = u[:, TP + 2:2 * TP + 4]
            tg = j
            tv = FT + j
            P.op("dve", (lambda e, pg=pg, ug=ug: e.tensor_tensor(ug[:, 2:2 + n], pg[:, 0:n], rstd[:, 0:n], ALU.mult)),
                 reads=[pg, rstd], writes=[u])
            P.op("dve", (lambda e, pv=pv, uv=uv: e.tensor_tensor(uv[:, 2:2 + n], pv[:, 0:n], rstd[:, 0:n], ALU.mult)),
                 reads=[pv, rstd], writes=[u])
            if halo:
                P.op("pool", (lambda e, ug=ug, tg=tg: e.tensor_copy(tails[:, 2 * tg:2 * tg + 2], ug[:, 2:4])), reads=[u], writes=[tails])
                P.op("pool", (lambda e, uv=uv, tv=tv: e.tensor_copy(tails[:, 2 * tv:2 * tv + 2], uv[:, 2:4])), reads=[u], writes=[tails])
                continue
            P.op("pool", (lambda e, ug=ug, tg=tg: e.tensor_copy(ug[:, 0:2], tails[:, 2 * tg:2 * tg + 2])), reads=[tails], writes=[u])
            P.op("pool", (lambda e, uv=uv, tv=tv: e.tensor_copy(uv[:, 0:2], tails[:, 2 * tv:2 * tv + 2])), reads=[tails], writes=[u])
            P.op("pool", (lambda e, ug=ug, tg=tg: e.tensor_copy(tails[:, 2 * tg:2 * tg + 2], ug[:, n:n + 2])), reads=[u], writes=[tails])
            P.op("pool", (lambda e, uv=uv, tv=tv: e.tensor_copy(tails[:, 2 * tv:2 * tv + 2], uv[:, n:n + 2])), reads=[u], writes=[tails])
            cg = c[:, 0:TP]
            cvv = c[:, TP:2 * TP]
            for (uu, cc, t) in ((ug, cg, tg), (uv, cvv, tv)):
                P.op("dve", (lambda e, uu=uu, cc=cc, t=t: e.tensor_scalar(
                    cc[:, 0:n], uu[:, 0:n], par[:, CW + 3 * t:CW + 3 * t + 1], par[:, CB + t:CB + t + 1], ALU.mult, ALU.add)),
                    reads=[u, par], writes=[c])
                for kk in (1, 2):
                    P.op("dve", (lambda e, uu=uu, cc=cc, t=t, kk=kk: e.scalar_tensor_tensor(
                        cc[:, 0:n], uu[:, kk:kk + n], par[:, CW + 3 * t + kk:CW + 3 * t + kk + 1], cc[:, 0:n], ALU.mult, ALU.add)),
                        reads=[u, par, c], writes=[c])
            P.op("act", (lambda e, cg=cg: e.activation(cg[:, 0:n], cg[:, 0:n], AF.Silu)), reads=[c], writes=[c])
            P.op("dve", (lambda e, cg=cg, cvv=cvv, j=j: e.tensor_tensor(big[:, j * TP:j * TP + n], cg[:, 0:n], cvv[:, 0:n], ALU.mult)),
                 reads=[c], writes=[big])
        if halo:
            continue
        for dt in range(DT):
            pb = PS_MM[cnt["mm"] % 2]; cnt["mm"] += 1
            for h0 in range(0, FT, 43):
                w = ws.next()
                for k in range(43):
                    kk = h0 + k
                    P.op("pe", (lambda e, w=w, k=k, kk=kk, pb=pb: e.matmul(
                        pb[:, 0:n], w[:, k * 128:(k + 1) * 128], big[:, kk * TP:kk * TP + n],
                        start=(kk == 0), stop=(kk == FT - 1))), reads=[w, big], writes=[pb])
            sg = stg[cnt["stg"] % 3]; cnt["stg"] += 1
            P.dma("pool", sg[:, lo:n], hmid_d[:, dt, o0 + lo:o0 + n], reads=[hmid_v[dt]], writes=[sg])
            P.op("dve", (lambda e, sg=sg, pb=pb: e.tensor_tensor(sg[:, 0:n], pb[:, 0:n], sg[:, 0:n], ALU.add)),
                 reads=[pb, sg], writes=[sg])
            P.dma("pool", hout_d[:, dt, o0 + lo:o0 + n], sg[:, lo:n], reads=[sg], writes=[hout_v[dt]], owner=sg)
            sq = sqb[cnt["sq"] % 2]; cnt["sq"] += 1
            P.op("act", (lambda e, sg=sg, sq=sq: e.activation(sq[:, 0:n], sg[:, 0:n], AF.Square)), reads=[sg], writes=[sq])
            P.op("pe", (lambda e, sq=sq, dt=dt: e.matmul(PS_SS[:, 0:n], ones[:], sq[:, 0:n], start=(dt == 0), stop=(dt == DT - 1))),
                 reads=[ones, sq], writes=[PS_SS])
            if mode == "A":
                P.op("dve", (lambda e, sg=sg, dt=dt: e.tensor_scalar(hn[:, dt * TP:dt * TP + n], sg[:, 0:n],
                                                                  par[:, G_2 + dt:G_2 + dt + 1], None, ALU.mult)),
                     reads=[sg, par], writes=[hn])
        rstd_from(PS_SS, rstd[:, 0:n], n, 128)
        if mode == "B":
            for dt in range(DT):
                sg = stg[cnt["stg"] % 3]; cnt["stg"] += 1
                P.dma("pool", sg[:, lo:n], hout_d[:, dt, o0 + lo:o0 + n], reads=[hout_v[dt]], writes=[sg])
                P.op("dve", (lambda e, sg=sg, dt=dt: e.scalar_tensor_tensor(sg[:, 0:n], sg[:, 0:n], par[:, G_2 + dt:G_2 + dt + 1],
                                                                         rstd[:, 0:n], ALU.mult, ALU.mult)),
                     reads=[sg, par, rstd], writes=[sg])
                P.dma("pool", hoT[:, dt, o0 + lo:o0 + n], sg[:, lo:n], reads=[sg], writes=[], owner=sg)
        else:
            ttiles = [(ts, min(128, n - ts)) for ts in range(0, n, 128)]
            for tt, (ts, m) in enumerate(ttiles):
                P.op("pe", (lambda e, tt=tt, ts=ts, m=m: e.matmul(PS_TM[0:m, tt:tt + 1], rstd[0:1, ts:ts + m], onecol[0:1, 0:1],
                                                                  start=True, stop=True)), reads=[rstd, onecol], writes=[PS_TM])
            P.op("act", lambda e: e.copy(rs_tm[:, 0:4], PS_TM[:, 0:4]), reads=[PS_TM], writes=[rs_tm])
            for dt in range(DT):
                sg = stg[cnt["stg"] % 3]; cnt["stg"] += 1
                P.dma("pool", sg[:, lo:n], hout_d[:, dt, o0 + lo:o0 + n], reads=[hout_v[dt]], writes=[sg])
                P.op("dve", (lambda e, sg=sg, dt=dt: e.tensor_scalar(big[:, dt * TP:dt * TP + n], sg[:, 0:n],
                                                                  par[:, G_3 + dt:G_3 + dt + 1], None, ALU.mult)),
                     reads=[sg, par], writes=[big])
            for ch in range(12):
                src = big if ch < 8 else hn
                pbs = [ps[0], ps[1], ps[2], ps[3]]
                for piece in range(4):
                    w = ws.next()
                    for tt, (ts, m) in enumerate(ttiles):
                        for k in range(8):
                            kk = piece * 8 + k
                            P.op("pe", (lambda e, w=w, k=k, kk=kk, tt=tt, ts=ts, m=m, src=src: e.matmul(
                                pbs[tt][0:m, 0:512], src[:, kk * TP + ts:kk * TP + ts + m], w[:, k * 512:(k + 1) * 512],
                                start=(kk == 0), stop=(kk == DT - 1))), reads=[w, src], writes=[pbs[tt]])
                for tt, (ts, m) in enumerate(ttiles):
                    tile_idx = pi * 3 + tt
                    qs = qst[cnt["q"] % 2]; qr = qrt[cnt["q"] % 2]; qo = qob[cnt["q"] % 2]; cnt["q"] += 1
                    rope = ch < 10
                    if rope:
                        P.op("act", (lambda e, qs=qs, tt=tt, m=m: e.activation(qs[0:m, 0:512], pbs[tt][0:m, :], AF.Copy, scale=rs_tm[0:m, tt:tt + 1])),
                             reads=[pbs[tt], rs_tm], writes=[qs])
                        q3 = qs[0:m, 0:512].rearrange("p (h two d) -> p h two d", two=2, d=64)
                        r3 = qr[0:m, 0:512].rearrange("p (h two d) -> p h two d", two=2, d=64)
                        o3 = qo[0:m, :].rearrange("p (h two d) -> p h two d", two=2, d=64)
                        cosb = cs[0:m, tile_idx * 128:tile_idx * 128 + 64].unsqueeze(1).broadcast_to([m, 4, 64])
                        sinb = cs[0:m, tile_idx * 128 + 64:tile_idx * 128 + 128].unsqueeze(1).broadcast_to([m, 4, 64])
                        P.op("pool", (lambda e: e.tensor_tensor(r3[:, :, 0, :], q3[:, :, 1, :], sinb, ALU.mult)), reads=[qs, cs], writes=[qr])
                        P.op("pool", (lambda e: e.tensor_tensor(r3[:, :, 1, :], q3[:, :, 0, :], sinb, ALU.mult)), reads=[qs, cs], writes=[qr])
                        P.op("dve", (lambda e: e.tensor_tensor(q3[:, :, 0, :], q3[:, :, 0, :], cosb, ALU.mult)), reads=[qs, cs], writes=[qs])
                        P.op("dve", (lambda e: e.tensor_tensor(q3[:, :, 1, :], q3[:, :, 1, :], cosb, ALU.mult)), reads=[qs, cs], writes=[qs])
                        P.op("dve", (lambda e: e.tensor_tensor(o3[:, :, 0, :], q3[:, :, 0, :], r3[:, :, 0, :], ALU.subtract)), reads=[qs, qr], writes=[qo])
                        P.op("dve", (lambda e: e.tensor_tensor(o3[:, :, 1, :], q3[:, :, 1, :], r3[:, :, 1, :], ALU.add)), reads=[qs, qr], writes=[qo])
                    else:
                        P.op("act", (lambda e, qo=qo, tt=tt, m=m: e.activation(qo[0:m, :], pbs[tt][0:m, :], AF.Copy, scale=rs_tm[0:m, tt:tt + 1])),
                             reads=[pbs[tt], rs_tm], writes=[qo])
                    r0 = lo if ts == 0 else 0
                    tok0 = c0 + ts + r0 - 2
                    nr = m - r0
                    if ch < 8:
                        P.dma("pool", q_o[tok0:tok0 + nr, ch * 512:(ch + 1) * 512], qo[r0:m, :], reads=[qo], writes=[], owner=qo)
                    else:
                        P.dma("pool", kv_o[tok0:tok0 + nr, (ch - 8) * 512:(ch - 7) * 512], qo[r0:m, :], reads=[qo], writes=[], owner=qo)
    P.finish("sp")
    info = P.emit()
    return nc, info


D = 4096
KT = 32
ST = 512
NTOK = 8192
EPS = 1e-5
NH = 16
XQ = "sp"
BCE = "pool"
NXS = 8
PG, PCW, PCB, PDTB, PALOG, PDD, PNG = 0, 32, 32 + 40, 32 + 50, 32 + 66, 32 + 82, 32 + 98
NPAR = 32 + 98 + 1024


def build_mamba(n_super=NTOK // ST, debug=False):
    nc = bass.Bass("TRN2", target_bir_lowering=False)
    P = Prog(nc)
    xT = P.dram("xT", [128, KT, NTOK], F32, kind="ExternalInput")
    wx_d = P.dram("wx", [18, 128, KT, 128], BF16, kind="ExternalInput")
    wdt_d = P.dram("wdt", [128, KT * 16], BF16, kind="ExternalInput")
    par_d = P.dram("par", [128, NPAR], F32, kind="ExternalInput")
    cst_d = P.dram("cst", [128, 256], F32, kind="ExternalInput")
    esel_d = P.dram("esel", [16, 2048], F32, kind="ExternalInput")
    idb_d = P.dram("idb", [128, 128], BF16, kind="ExternalInput")
    y_o = P.dram("y_o", [NTOK, 1024], BF16, kind="ExternalOutput")

    par = P.sbuf([128, NPAR], F32, "par")
    cst = P.sbuf([128, 256], F32, "cst")
    esel = P.sbuf([16, 2048], F32, "esel")
    idb = P.sbuf([128, 128], BF16, "idb")
    wdt = P.sbuf([128, KT * 16], BF16, "wdt")
    ones = P.sbuf([128, 128], BF16, "ones")
    onecol = P.sbuf([128, 1], F32, "onecol")
    epsb = P.sbuf([128, 1], F32, "epsb")
    Arep = P.sbuf([128, NH], F32, "Arep")
    uT = P.sbuf([128, KT * ST], BF16, "uT")
    xbcT2 = [P.sbuf([128, 10 * ST], BF16, f"xbcT{i}") for i in range(2)]
    szT2 = [P.sbuf([128, 8 * ST], BF16, f"szT{i}") for i in range(2)]
    rstd = P.sbuf([128, ST], F32, "rstd")
    rs_tm = P.sbuf([128, 4], F32, "rs_tm")
    tails = P.sbuf([128, 30], F32, "tails")
    S = P.sbuf([128, 1024], F32, "S")
    S_bf = P.sbuf([128, 1024], BF16, "S_bf")
    ws = WStream(P, 4, KT * 128, "sp", pf=3)
    xs = [P.sbuf([128, ST], F32, f"xs{i}") for i in range(NXS)]
    sqb = [P.sbuf([128, ST], BF16, f"sq{i}") for i in range(2)]
    raw = [P.sbuf([128, ST + 3], F32, f"raw{i}") for i in range(2)]
    acc = [P.sbuf([128, ST], F32, f"acc{i}") for i in range(2)]
    Xtm2 = [P.sbuf([128, 1024], BF16, f"Xtm{i}") for i in range(2)]
    Btm2 = [P.sbuf([128, 128], BF16, f"Btm{i}") for i in range(2)]
    Xdt2 = [P.sbuf([128, 1024], BF16, f"Xdt{i}") for i in range(2)]
    Xdte2 = [P.sbuf([128, 1024], BF16, f"Xdte{i}") for i in range(2)]
    arg = [P.sbuf([128, 512], F32, f"arg{i}") for i in range(2)]
    Ldec = [P.sbuf([128, 512], F32, f"Ldec{i}") for i in range(2)]
    MT2 = [P.sbuf([128, 2048], BF16, f"MT{i}") for i in range(2)]
    yd_sb2 = [P.sbuf([128, 1024], F32, f"yd_sb{i}") for i in range(2)]
    yb = P.sbuf([128, 1024], F32, "yb")
    yg = P.sbuf([128, 1024], F32, "yg")
    junk = P.sbuf([128, 1024], BF16, "junk")
    yn = [P.sbuf([128, 1024], BF16, f"yn{i}") for i in range(2)]
    dtv2 = [P.sbuf([128, NH], F32, f"dtv{i}") for i in range(2)]
    e1 = P.sbuf([128, NH], F32, "e1")
    av = P.sbuf([128, NH], F32, "av")
    acs_sb2 = [P.sbuf([128, NH], F32, f"acs_sb{i}") for i in range(2)]
    acsT_sb = P.sbuf([16, 128], F32, "acsT_sb")
    ea2 = [P.sbuf([128, NH], F32, f"ea{i}") for i in range(2)]
    cdb2 = [P.sbuf([128, NH], F32, f"cdb{i}") for i in range(2)]
    dearg = P.sbuf([128, NH], F32, "dearg")
    de2 = [P.sbuf([128, NH], F32, f"de{i}") for i in range(2)]
    ss = P.sbuf([128, 1], F32, "ss")
    rsn = P.sbuf([128, 1], F32, "rsn")

    b0 = P.psum([128, 1024], BF16, "b0")
    b1 = P.psum([128, 1024], BF16, "b1")
    b2 = P.psum([128, 512], F32, "b2")
    b3 = P.psum([128, 512], F32, "b3")
    pA = [P.psum([128, 512], F32, f"pA{i}") for i in range(2)]
    pB = [P.psum([128, 512], F32, f"pB{i}") for i in range(2)]

    T = cst[:, 0:128]
    maskb = cst[:, 128:256]

    P.dma("pool", par[:], par_d[:], reads=[par_d], writes=[par])
    P.dma("pool", cst[:], cst_d[:], reads=[cst_d], writes=[cst])
    P.dma("pool", esel[:], esel_d[:], reads=[esel_d], writes=[esel])
    P.dma("pool", idb[:], idb_d[:], reads=[idb_d], writes=[idb])
    P.dma("pool", wdt[:], wdt_d[:], reads=[wdt_d], writes=[wdt])
    P.op("pool", lambda e: e.memset(ones[:], 1.0), writes=[ones])
    P.op("pool", lambda e: e.memset(onecol[:], 1.0), writes=[onecol])
    P.op("pool", lambda e: e.memset(epsb[:], EPS), writes=[epsb])
    P.op("pool", lambda e: e.memset(tails[:], 0.0), writes=[tails])
    P.op("pool", lambda e: e.memset(S[:], 0.0), writes=[S])
    P.op("pool", lambda e: e.memset(S_bf[:], 0.0), writes=[S_bf])
    P.op("act", lambda e: e.activation(Arep[:], par[:, PALOG:PALOG + NH], AF.Exp), reads=[par], writes=[Arep])
    P.op("dve", lambda e: e.tensor_scalar(Arep[:], Arep[:], -1.0, None, ALU.mult), reads=[Arep], writes=[Arep])

    for st in range(n_super):
        for j in range(18):
            ws.add(wx_d, wx_d[j].rearrange("p a b -> p (a b)"), KT * 128)

    cnt = {"xs": 0, "sq": 0, "t": 0, "g": 0, "yn": 0}
    pending = []
    for st in range(n_super):
        xbcT = xbcT2[st % 2]
        szT = szT2[st % 2]
        t0 = st * ST
        for kt in range(KT):
            x_ = xs[cnt["xs"] % NXS]; cnt["xs"] += 1
            P.dma(XQ, x_[:], xT[:, kt, t0:t0 + ST], reads=[xT], writes=[x_])
            sq = sqb[cnt["sq"] % 2]; cnt["sq"] += 1
            P.op("act", (lambda e, x_=x_, sq=sq: e.activation(sq[:], x_[:], AF.Square)), reads=[x_], writes=[sq])
            P.op("pe", (lambda e, sq=sq, kt=kt: e.matmul(b3[:], ones[:], sq[:], start=(kt == 0), stop=(kt == KT - 1))),
                 reads=[ones, sq], writes=[b3])
            P.op("dve", (lambda e, x_=x_, kt=kt: e.tensor_scalar(uT[:, kt * ST:(kt + 1) * ST], x_[:], par[:, PG + kt:PG + kt + 1], None, ALU.mult)),
                 reads=[x_, par], writes=[uT])
        P.op("act", lambda e: e.activation(rstd[:], b3[:], AF.Ln, bias=epsb[:, 0:1], scale=1.0 / D), reads=[b3, epsb], writes=[rstd])
        P.op("act", lambda e: e.activation(rstd[:], rstd[:], AF.Exp, scale=-0.5), reads=[rstd], writes=[rstd])
        for tt in range(4):
            P.op("pe", (lambda e, tt=tt: e.matmul(b2[:, 300 + tt:301 + tt], rstd[0:1, tt * 128:(tt + 1) * 128], onecol[0:1, 0:1],
                                                  start=True, stop=True)), reads=[rstd, onecol], writes=[b2])
        P.op("act", lambda e: e.copy(rs_tm[:, 0:4], b2[:, 300:304]), reads=[b2], writes=[rs_tm])
        for j in range(18):
            pb = pA[cnt["t"] % 2]
            rw = raw[cnt["t"] % 2]
            ac = acc[cnt["t"] % 2]
            cnt["t"] += 1
            w = ws.next()
            for k in range(KT):
                P.op("pe", (lambda e, w=w, k=k, pb=pb: e.matmul(pb[:], w[:, k * 128:(k + 1) * 128], uT[:, k * ST:(k + 1) * ST],
                                                                start=(k == 0), stop=(k == KT - 1))), reads=[w, uT], writes=[pb])
            if j < 8:
                P.op("dve", (lambda e, pb=pb, ac=ac: e.tensor_tensor(ac[:], pb[:], rstd[:], ALU.mult)), reads=[pb, rstd], writes=[ac])
                P.op("act", (lambda e, ac=ac, j=j: e.activation(szT[:, j * ST:(j + 1) * ST], ac[:], AF.Silu)), reads=[ac], writes=[szT])
                continue
            c = j - 8
            P.op("dve", (lambda e, pb=pb, rw=rw: e.tensor_tensor(rw[:, 3:3 + ST], pb[:], rstd[:], ALU.mult)), reads=[pb, rstd], writes=[rw])
            P.op("pool", (lambda e, rw=rw, c=c: e.tensor_copy(rw[:, 0:3], tails[:, 3 * c:3 * c + 3])), reads=[tails], writes=[rw])
            P.op("pool", (lambda e, rw=rw, c=c: e.tensor_copy(tails[:, 3 * c:3 * c + 3], rw[:, ST:ST + 3])), reads=[rw], writes=[tails])
            P.op("dve", (lambda e, rw=rw, ac=ac, c=c: e.tensor_scalar(ac[:], rw[:, 0:ST], par[:, PCW + 4 * c:PCW + 4 * c + 1],
                                                                    par[:, PCB + c:PCB + c + 1], ALU.mult, ALU.add)),
                 reads=[rw, par], writes=[ac])
            for kk in (1, 2, 3):
                P.op("dve", (lambda e, rw=rw, ac=ac, c=c, kk=kk: e.scalar_tensor_tensor(
                    ac[:], rw[:, kk:kk + ST], par[:, PCW + 4 * c + kk:PCW + 4 * c + kk + 1], ac[:], ALU.mult, ALU.add)),
                    reads=[rw, par, ac], writes=[ac])
            P.op("act", (lambda e, ac=ac, c=c: e.activation(xbcT[:, c * ST:(c + 1) * ST], ac[:], AF.Silu)), reads=[ac], writes=[xbcT])
        def stage1(st, tt, par_i, xbcT, szT):
            Xtm, Btm, Xdt, Xdte, MT, yd_sb = Xtm2[par_i], Btm2[par_i], Xdt2[par_i], Xdte2[par_i], MT2[par_i], yd_sb2[par_i]
            dtv, acs_sb, ea, cdb, de = dtv2[par_i], acs_sb2[par_i], ea2[par_i], cdb2[par_i], de2[par_i]
            c0 = tt * 128
            BT = xbcT[:, 8 * ST + c0:8 * ST + c0 + 128]
            CT = xbcT[:, 9 * ST + c0:9 * ST + c0 + 128]
            for k in range(KT):
                P.op("pe", (lambda e, k=k, c0=c0: e.matmul(b2[:, 272:288], uT[:, k * ST + c0:k * ST + c0 + 128], wdt[:, k * 16:(k + 1) * 16],
                                                           start=(k == 0), stop=(k == KT - 1))), reads=[uT, wdt], writes=[b2])
            P.op("dve", (lambda e, tt=tt: e.scalar_tensor_tensor(dtv[:], b2[:, 272:288], rs_tm[:, tt:tt + 1], par[:, PDTB:PDTB + NH], ALU.mult, ALU.add)),
                 reads=[b2, rs_tm, par], writes=[dtv])
            P.op("act", lambda e: e.activation(e1[:], dtv[:], AF.Exp), reads=[dtv], writes=[e1])
            P.op("act", lambda e: e.activation(dtv[:], e1[:], AF.Ln, bias=onecol[:, 0:1]), reads=[e1, onecol], writes=[dtv])
            P.op("dve", lambda e: e.tensor_tensor(av[:], dtv[:], Arep[:], ALU.mult), reads=[dtv, Arep], writes=[av])
            P.op("pe", lambda e: e.matmul(b2[:, 128:144], T, av[:], start=True, stop=True), reads=[cst, av], writes=[b2])
            P.op("pe", lambda e: e.matmul(b2[0:16, 144:272], av[:], T, start=True, stop=True), reads=[cst, av], writes=[b2])
            P.op("act", lambda e: e.copy(acs_sb[:], b2[:, 128:144]), reads=[b2], writes=[acs_sb])
            P.op("act", lambda e: e.copy(acsT_sb[:], b2[0:16, 144:272]), reads=[b2], writes=[acsT_sb])
            P.op("act", lambda e: e.activation(ea[:], acs_sb[:], AF.Exp), reads=[acs_sb], writes=[ea])
            for j in range(8):
                P.op("pe", (lambda e, j=j, c0=c0: e.transpose(b0[:, j * 128:(j + 1) * 128], xbcT[:, j * ST + c0:j * ST + c0 + 128], idb[:])),
                     reads=[xbcT, idb], writes=[b0])
            P.op("act", lambda e: e.copy(Xtm[:], b0[:]), reads=[b0], writes=[Xtm])
            P.op("pe", (lambda e, BT=BT: e.transpose(b0[:, 0:128], BT, idb[:])), reads=[xbcT, idb], writes=[b0])
            P.op("act", lambda e: e.copy(Btm[:], b0[:, 0:128]), reads=[b0], writes=[Btm])
            P.op("pe", (lambda e, BT=BT, CT=CT: e.matmul(b2[:, 0:128], BT, CT, start=True, stop=True)), reads=[xbcT], writes=[b2])
            for g in range(4):
                ar = arg[cnt["g"] % 2]; Ld = Ldec[cnt["g"] % 2]; cnt["g"] += 1
                for hh in range(4):
                    h = 4 * g + hh
                    P.op("pe", (lambda e, h=h, hh=hh: e.matmul(b3[:, hh * 128:(hh + 1) * 128], esel[0:16, h * 128:(h + 1) * 128], acsT_sb[:],
                                                              start=True, stop=True)), reads=[esel, acsT_sb], writes=[b3])
                b3v = b3[:, :].rearrange("p (h l) -> p h l", l=128)[:, :, 127]
                P.op("act", (lambda e, g=g, b3v=b3v: e.activation(cdb[:, 4 * g:4 * g + 4], b3v, AF.Exp)), reads=[b3], writes=[cdb])
                P.op("dve", (lambda e, g=g, b3v=b3v: e.tensor_tensor(dearg[:, 4 * g:4 * g + 4], b3v, acs_sb[:, 4 * g:4 * g + 4], ALU.subtract)),
                     reads=[b3, acs_sb], writes=[dearg])
                for hh in range(4):
                    h = 4 * g + hh
                    P.op("dve", (lambda e, h=h, hh=hh, ar=ar: e.scalar_tensor_tensor(
                        ar[:, hh * 128:(hh + 1) * 128], b3[:, hh * 128:(hh + 1) * 128], acs_sb[:, h:h + 1], maskb, ALU.subtract, ALU.add)),
                        reads=[b3, acs_sb, cst], writes=[ar])
                P.op("act", (lambda e, ar=ar, Ld=Ld: e.activation(Ld[:], ar[:], AF.Exp)), reads=[ar], writes=[Ld])
                for hh in range(4):
                    h = 4 * g + hh
                    P.op("dve", (lambda e, h=h, hh=hh, Ld=Ld: e.tensor_tensor(MT[:, h * 128:(h + 1) * 128], Ld[:, hh * 128:(hh + 1) * 128],
                                                                          b2[:, 0:128], ALU.mult)), reads=[Ld, b2], writes=[MT])
            P.op("act", lambda e: e.activation(de[:], dearg[:], AF.Exp), reads=[dearg], writes=[de])
            X3 = Xtm[:, :].rearrange("p (h d) -> p h d", d=64)
            Xd3 = Xdt[:, :].rearrange("p (h d) -> p h d", d=64)
            Xe3 = Xdte[:, :].rearrange("p (h d) -> p h d", d=64)
            dt_b = dtv[:, 0:NH].unsqueeze(2).broadcast_to([128, NH, 64])
            de_b = de[:, 0:NH].unsqueeze(2).broadcast_to([128, NH, 64])
            P.op(BCE, (lambda e: e.tensor_tensor(Xd3, X3, dt_b, ALU.mult)), reads=[Xtm, dtv], writes=[Xdt])
            P.op(BCE, (lambda e: e.tensor_tensor(Xe3, Xd3, de_b, ALU.mult)), reads=[Xdt, de], writes=[Xdte])
            for h in range(NH):
                P.op("pe", (lambda e, h=h: e.matmul(pA[h // 8][:, (h % 8) * 64:(h % 8 + 1) * 64], MT[:, h * 128:(h + 1) * 128],
                                                    Xdt[:, h * 64:(h + 1) * 64], start=True, stop=True)),
                     reads=[MT, Xdt], writes=[pA[h // 8]])
            for cc in range(2):
                P.op("act", (lambda e, cc=cc: e.copy(yd_sb[:, cc * 512:(cc + 1) * 512], pA[cc][:])), reads=[pA[cc]], writes=[yd_sb])

        def stage2(st, tt, par_i, xbcT, szT):
            Xtm, Btm, Xdt, Xdte, MT, yd_sb = Xtm2[par_i], Btm2[par_i], Xdt2[par_i], Xdte2[par_i], MT2[par_i], yd_sb2[par_i]
            dtv, acs_sb, ea, cdb, de = dtv2[par_i], acs_sb2[par_i], ea2[par_i], cdb2[par_i], de2[par_i]
            c0 = tt * 128
            t0 = st * ST
            CT = xbcT[:, 9 * ST + c0:9 * ST + c0 + 128]
            for cc in range(2):
                P.op("pe", (lambda e, cc=cc, CT=CT: e.matmul(pB[cc][:], CT, S_bf[:, cc * 512:(cc + 1) * 512], start=True, stop=True)),
                     reads=[xbcT, S_bf], writes=[pB[cc]])
            for h in range(NH):
                hs = slice(h * 64, (h + 1) * 64)
                ps_ = slice((h % 8) * 64, (h % 8 + 1) * 64)
                P.op("dve", (lambda e, h=h, hs=hs, ps_=ps_: e.scalar_tensor_tensor(yb[:, hs], pB[h // 8][:, ps_], ea[:, h:h + 1], yd_sb[:, hs],
                                                                               ALU.mult, ALU.add)), reads=[pB[h // 8], ea, yd_sb], writes=[yb])
            for h in range(NH):
                hs = slice(h * 64, (h + 1) * 64)
                P.op("dve", (lambda e, h=h, hs=hs: e.scalar_tensor_tensor(yb[:, hs], Xtm[:, hs], par[:, PDD + h:PDD + h + 1], yb[:, hs],
                                                                      ALU.mult, ALU.add)), reads=[Xtm, par, yb], writes=[yb])
            for cc in range(2):
                P.op("pe", (lambda e, cc=cc: e.matmul(pB[cc][:], Btm[:], Xdte[:, cc * 512:(cc + 1) * 512], start=True, stop=True)),
                     reads=[Btm, Xdte], writes=[pB[cc]])
            for h in range(NH):
                hs = slice(h * 64, (h + 1) * 64)
                ps_ = slice((h % 8) * 64, (h % 8 + 1) * 64)
                P.op("dve", (lambda e, h=h, hs=hs, ps_=ps_: e.scalar_tensor_tensor(S[:, hs], S[:, hs], cdb[:, h:h + 1], pB[h // 8][:, ps_],
                                                                               ALU.mult, ALU.add)), reads=[S, cdb, pB[h // 8]], writes=[S])
            P.op("act", lambda e: e.copy(S_bf[:], S[:]), reads=[S], writes=[S_bf])
            for j in range(8):
                P.op("pe", (lambda e, j=j, c0=c0: e.transpose(b1[:, j * 128:(j + 1) * 128], szT[:, j * ST + c0:j * ST + c0 + 128], idb[:])),
                     reads=[szT, idb], writes=[b1])
            P.op("dve", lambda e: e.tensor_tensor(yg[:], yb[:], b1[:], ALU.mult), reads=[yb, b1], writes=[yg])
            P.op("dve", lambda e: e.scalar_tensor_tensor(junk[:], yg[:], 1.0, yg[:], ALU.mult, ALU.mult, accum_out=ss[:, 0:1]),
                 reads=[yg], writes=[junk, ss])
            P.op("act", lambda e: e.activation(rsn[:], ss[:], AF.Ln, bias=epsb[:, 0:1], scale=1.0 / 1024), reads=[ss, epsb], writes=[rsn])
            P.op("act", lambda e: e.activation(rsn[:], rsn[:], AF.Exp, scale=-0.5), reads=[rsn], writes=[rsn])
            yo_ = yn[cnt["yn"] % 2]; cnt["yn"] += 1
            P.op("dve", (lambda e, yo_=yo_: e.scalar_tensor_tensor(yo_[:], yg[:], rsn[:, 0:1], par[:, PNG:PNG + 1024], ALU.mult, ALU.mult)),
                 reads=[yg, rsn, par], writes=[yo_])
            P.dma("pool", y_o[t0 + c0:t0 + c0 + 128, :], yo_[:], reads=[yo_], writes=[], owner=yo_)

        def capture(fn, *a):
            old = P.ops
            P.ops = []
            fn(*a)
            got = P.ops
            P.ops = old
            return got

        def merge(A, B):
            out = []
            ia = ib = 0
            while ia < len(A) or ib < len(B):
                if ib >= len(B) or (ia < len(A) and ia * len(B) <= ib * len(A)):
                    out.append(A[ia]); ia += 1
                else:
                    out.append(B[ib]); ib += 1
            return out

        for tt in range(4):
            ci = st * 4 + tt
            A = capture(stage1, st, tt, ci % 2, xbcT, szT)
            if pending:
                pst, ptt, ppar, pxb, psz = pending.pop()
                B = capture(stage2, pst, ptt, ppar, pxb, psz)
                P.ops.extend(merge(A, B))
            else:
                P.ops.extend(A)
            pending.append((st, tt, ci % 2, xbcT, szT))
    pst, ptt, ppar, pxb, psz = pending.pop()
    stage2(pst, ptt, ppar, pxb, psz)
    P.finish("sp")
    info = P.emit()
    return nc, info


NQT = 64
SCALE = 128 ** -0.5
NEG = -1.0e30
BIG = 30000.0


def build_attn(n_qt=NQT):
    nc = bass.Bass("TRN2", target_bir_lowering=False)
    P = Prog(nc)
    qT_d = P.dram("qT", [128, NQT, 512], BF16, kind="ExternalInput")
    kT_d = P.dram("kT", [128, 8192], BF16, kind="ExternalInput")
    v_d = P.dram("v", [128, 64, 128], BF16, kind="ExternalInput")
    tb_d = P.dram("tb", [128, 512], BF16, kind="ExternalInput")
    idb_d = P.dram("idb", [128, 128], BF16, kind="ExternalInput")
    en_d = P.dram("en", [32, 32 * 128], BF16, kind="ExternalInput")
    oT_o = P.dram("oT_o", [128, NQT, 512], BF16, kind="ExternalOutput")

    qT = P.sbuf([128, NQT * 512], BF16, "qT")
    kT = P.sbuf([128, 8192], BF16, "kT")
    vv = P.sbuf([128, 64 * 128], BF16, "vv")
    tb = P.sbuf([128, 512], BF16, "tb")
    idb = P.sbuf([128, 128], BF16, "idb")
    en = P.sbuf([32, 32 * 128], BF16, "en")
    ones = P.sbuf([128, 128], BF16, "ones")
    km = P.sbuf([128, 32], F32, "km")
    kmh = P.sbuf([128, 32], BF16, "kmh")
    kmhf = P.sbuf([128, 32], F32, "kmhf")
    kml = P.sbuf([128, 32], BF16, "kml")
    g_sb = P.sbuf([128, 128], F32, "g_sb")
    t8 = P.sbuf([128, 32], F32, "t8")
    selb = P.sbuf([128, 128], BF16, "selb")
    bias2 = [P.sbuf([32, 512], BF16, f"bias2_{i}") for i in range(2)]
    pT = [P.sbuf([128, 512], BF16, f"pT{i}") for i in range(4)]
    rec = P.sbuf([128, 512], F32, "rec")
    ob = [P.sbuf([128, 512], BF16, f"ob{i}") for i in range(2)]
    sTb = [P.psum([128, 512], F32, f"sT{i}") for i in range(3)]
    oTb = [P.psum([128, 512], F32, f"oT{i}") for i in range(2)]
    smb = [P.psum([128, 512], F32, f"sm{i}") for i in range(2)]
    gpb = P.psum([128, 1024], BF16, "gpb")
    gp = sTb[0]

    P.dma("sp", qT[:], qT_d[:].rearrange("p a b -> p (a b)"), reads=[qT_d], writes=[qT])
    P.dma("pool", kT[:], kT_d[:], reads=[kT_d], writes=[kT])
    P.dma("pool", vv[:], v_d[:].rearrange("p a b -> p (a b)"), reads=[v_d], writes=[vv])
    P.dma("pool", tb[:], tb_d[:], reads=[tb_d], writes=[tb])
    P.dma("pool", idb[:], idb_d[:], reads=[idb_d], writes=[idb])
    P.dma("pool", en[:], en_d[:], reads=[en_d], writes=[en])
    P.op("pool", lambda e: e.memset(ones[:], 1.0), writes=[ones])
    P.op("dve", lambda e: e.tensor_reduce(km[:], kT[:, :].rearrange("p (n k) -> p n k", k=256), AX.X, ALU.add), reads=[kT], writes=[km])
    P.op("dve", lambda e: e.tensor_scalar(km[:], km[:], 1.0 / 256, None, ALU.mult), reads=[km], writes=[km])
    P.op("dve", lambda e: e.tensor_copy(kmh[:], km[:]), reads=[km], writes=[kmh])
    P.op("dve", lambda e: e.tensor_copy(kmhf[:], kmh[:]), reads=[kmh], writes=[kmhf])
    P.op("dve", lambda e: e.tensor_tensor(kml[:], km[:], kmhf[:], ALU.subtract), reads=[km, kmhf], writes=[kml])

    LA = 2
    steps = []
    for qt in range(n_qt):
        blk = qt // 2
        half = qt % 2
        tiles = [(2 * n + i, n, False) for n in range(blk) for i in range(2)]
        if half == 0:
            tiles.append((2 * blk, None, True))
        else:
            tiles.append((2 * blk, None, False))
            tiles.append((2 * blk + 1, None, True))
        for ti, (kti, n, diag) in enumerate(tiles):
            steps.append((qt, ti, len(tiles), kti, n, diag))

    def prologue(qt):
        blk = qt // 2
        b2_ = bias2[qt % 2]
        if blk == 0:
            return
        for h in range(4):
            P.op("pe", (lambda e, h=h: e.matmul(gp[:, h * 32:h * 32 + blk], qT[:, qt * 512 + h * 128:qt * 512 + (h + 1) * 128],
                                                kmh[:, 0:blk], start=True, stop=False)), reads=[qT, kmh], writes=[gp])
            P.op("pe", (lambda e, h=h: e.matmul(gp[:, h * 32:h * 32 + blk], qT[:, qt * 512 + h * 128:qt * 512 + (h + 1) * 128],
                                                kml[:, 0:blk], start=False, stop=True)), reads=[qT, kml], writes=[gp])
        P.op("pool", lambda e: e.memset(g_sb[:], NEG), writes=[g_sb])
        g3 = g_sb[:, :].rearrange("p (h n) -> p h n", n=32)[:, :, 0:blk]
        p3 = gp[:, 0:128].rearrange("p (h n) -> p h n", n=32)[:, :, 0:blk]
        P.op("act", (lambda e: e.copy(g3, p3)), reads=[gp], writes=[g_sb])
        for h in range(4):
            P.op("dve", (lambda e, h=h: e.max(t8[:, h * 8:(h + 1) * 8], g_sb[:, h * 32:(h + 1) * 32])), reads=[g_sb], writes=[t8])
        for h in range(4):
            P.op("dve", (lambda e, h=h: e.tensor_scalar(selb[:, h * 32:(h + 1) * 32], g_sb[:, h * 32:(h + 1) * 32],
                                                     t8[:, h * 8 + 2:h * 8 + 3], 1.0, ALU.is_ge, ALU.subtract)), reads=[g_sb, t8], writes=[selb])
        for h in range(4):
            P.op("pe", (lambda e, h=h: e.transpose(gpb[0:32, h * 128:(h + 1) * 128], selb[:, h * 32:(h + 1) * 32], idb[:])),
                 reads=[selb, idb], writes=[gpb])
        P.op("act", (lambda e: e.copy(b2_[:], gpb[0:32, 0:512])), reads=[gpb], writes=[b2_])

    def front(i):
        qt, ti, nt, kti, n, diag = steps[i]
        if ti == 0:
            prologue(qt)
        sT = sTb[1 + i % 2]
        pt = pT[i % 4]
        q_rhs = qT[:, qt * 512:(qt + 1) * 512]
        b2_ = bias2[qt % 2]
        extra = (n is not None) or diag
        P.op("pe", (lambda e: e.matmul(sT[:], kT[:, kti * 128:(kti + 1) * 128], q_rhs, start=True, stop=not extra)),
             reads=[kT, qT], writes=[sT])
        if n is not None:
            P.op("pe", (lambda e: e.matmul(sT[:], en[0:32, n * 128:(n + 1) * 128], b2_[:], start=False, stop=True)),
                 reads=[en, b2_], writes=[sT])
        elif diag:
            P.op("pe", (lambda e: e.matmul(sT[:], idb[:], tb[:], start=False, stop=True)), reads=[idb, tb], writes=[sT])
        P.op("act", (lambda e: e.activation(pt[:], sT[:], AF.Exp, scale=SCALE)), reads=[sT], writes=[pt])

    def back(i):
        qt, ti, nt, kti, n, diag = steps[i]
        pt = pT[i % 4]
        oT = oTb[qt % 2]
        sm = smb[qt % 2]
        first = ti == 0
        last = ti == nt - 1
        P.op("pe", (lambda e: e.matmul(oT[:], vv[:, kti * 128:(kti + 1) * 128], pt[:], start=first, stop=last)), reads=[vv, pt], writes=[oT])
        P.op("pe", (lambda e: e.matmul(sm[:], ones[:], pt[:], start=first, stop=last)), reads=[ones, pt], writes=[sm])
        if last:
            P.op("dve", (lambda e: e.reciprocal(rec[:], sm[:])), reads=[sm], writes=[rec])
            o_ = ob[qt % 2]
            P.op("dve", (lambda e: e.tensor_tensor(o_[:], oT[:], rec[:], ALU.mult)), reads=[oT, rec], writes=[o_])
            P.dma("pool", oT_o[:, qt, :], o_[:], reads=[o_], writes=[], owner=o_)

    for idx in range(len(steps) + LA):
        if idx < len(steps):
            front(idx)
        if idx - LA >= 0:
            back(idx - LA)
    P.finish("sp")
    info = P.emit()
    return nc, info


CCH = 8192


def build_cast(M):
    nc = bass.Bass("TRN2", target_bir_lowering=False)
    P = Prog(nc)
    w_d = P.dram("w", [128, M], F32, kind="ExternalInput")
    wb_d = P.dram("wb", [128, M], BF16, kind="ExternalOutput")
    NB = 3
    fb = [P.sbuf([128, CCH], F32, f"fb{i}") for i in range(NB)]
    bb = [P.sbuf([128, CCH], BF16, f"bb{i}") for i in range(NB)]
    n = M // CCH
    engs = ("dve", "act")
    for i in range(n):
        f = fb[i % NB]
        b = bb[i % NB]
        P.dma("sp", f[:], w_d[:, i * CCH:(i + 1) * CCH], reads=[w_d], writes=[f])
        en = engs[i % 2]
        if en == "act":
            P.op("act", lambda e: e.copy(b[:], f[:]), reads=[f], writes=[b])
        else:
            P.op(en, lambda e: e.tensor_copy(b[:], f[:]), reads=[f], writes=[b])
        P.dma("pool", wb_d[:, i * CCH:(i + 1) * CCH], b[:], reads=[b], writes=[], owner=b)
    P.finish("sp")
    P.emit()
    return nc


_bf = ml_dtypes.bfloat16
NCORES = 8


def _run(nc, in_maps):
    res = run_bass_kernel_spmd(nc, in_maps, core_ids=list(range(NCORES)))
    return res.results


def _fm(a, kt):
    n = a.shape[0]
    return np.ascontiguousarray(a.T.reshape(kt, 128, n).transpose(1, 0, 2))


def _tok_par(g_ffn, g2, g3, cw, cb):
    par = np.zeros((128, 4 * 32 + 172 * 4), np.float32)
    par[:, 0:32] = g_ffn.reshape(32, 128).T
    par[:, 32:64] = g2.reshape(32, 128).T
    if g3 is not None:
        par[:, 64:96] = g3.reshape(32, 128).T
    par[:, 128:128 + 516] = cw.reshape(3, 172, 128).transpose(2, 1, 0).reshape(128, 516)
    par[:, 128 + 516:] = cb.reshape(172, 128).T
    return par


def _cs_table(base):
    inv = (10000.0 ** (-np.arange(64, dtype=np.float32) / 64)).astype(np.float32)
    tab = np.zeros((128, 9, 128), np.float32)
    for pi in range(3):
        for tt in range(3):
            p = (base + 342 * pi + 128 * tt + np.arange(128) - 2).astype(np.float32)
            a = p[:, None] * inv[None, :]
            tab[:, pi * 3 + tt] = np.concatenate([np.cos(a), np.sin(a)], 1)
    return tab


def kernel(x, norm_a, mamba_w_in, mamba_conv_w, mamba_conv_b, mamba_dt_bias, mamba_a_log,
           mamba_d, mamba_norm, mamba_w_out, kv_norm, w_kv, norm_b, w_q, w_o,
           ffn_norm, ffn_w_up, ffn_conv_w, ffn_conv_b, ffn_w_down, final_norm):
    f32 = np.float32
    x = np.asarray(x, f32)[0]
    w_in = np.asarray(mamba_w_in, f32)[0]

    shapes = []
    for g in range(8):
        shapes.append((f"wx{g}", (18, 128, 32, 128)))
        shapes.append((f"wdt{g}", (128, 512)))
    shapes += [("wpA", (32, 128, 64, 128)), ("wuA", (86, 128, 2, 32, 128)), ("wdA", (32, 128, 86, 128)), ("wqA", (12, 128, 32, 512)),
               ("wpB", (32, 128, 32, 128)), ("wuB", (86, 128, 2, 32, 128)), ("wdB", (32, 128, 86, 128))]
    offs = {}
    tot = 0
    for nm, sh in shapes:
        offs[nm] = (tot, sh)
        tot += int(np.prod(sh))
    per = -(-tot // (NCORES * 128 * CCH)) * CCH
    flat = np.zeros(NCORES * 128 * per, f32)

    def put(nm, arr):
        o, sh = offs[nm]
        flat[o:o + arr.size].reshape(sh)[...] = arr

    for g in range(8):
        wst = np.concatenate([w_in[:, g * 1024:(g + 1) * 1024], w_in[:, 8192 + g * 1024:8192 + (g + 1) * 1024],
                              w_in[:, 16384 + g * 128:16384 + (g + 1) * 128], w_in[:, 17408 + g * 128:17408 + (g + 1) * 128]], 1)
        put(f"wx{g}", wst.reshape(32, 128, 18, 128).transpose(2, 1, 0, 3))
        put(f"wdt{g}", w_in[:, 18432 + g * 16:18432 + (g + 1) * 16].reshape(32, 128, 16).transpose(1, 0, 2).reshape(128, 512))
    put("wpA", np.asarray(mamba_w_out, f32)[0].reshape(64, 128, 32, 128).transpose(2, 1, 0, 3))
    put("wuA", np.asarray(ffn_w_up, f32)[0].reshape(32, 128, 2, 86, 128).transpose(3, 1, 2, 0, 4))
    put("wdA", np.asarray(ffn_w_down, f32)[0].reshape(86, 128, 32, 128).transpose(2, 1, 0, 3))
    wqkv = np.concatenate([np.asarray(w_q, f32)[0], np.asarray(w_kv, f32)], 1)
    put("wqA", wqkv.reshape(32, 128, 12, 512).transpose(2, 1, 0, 3))
    put("wpB", np.asarray(w_o, f32)[0].reshape(32, 128, 32, 128).transpose(2, 1, 0, 3))
    put("wuB", np.asarray(ffn_w_up, f32)[1].reshape(32, 128, 2, 86, 128).transpose(3, 1, 2, 0, 4))
    put("wdB", np.asarray(ffn_w_down, f32)[1].reshape(86, 128, 32, 128).transpose(2, 1, 0, 3))
    del wqkv
    fl = flat.reshape(NCORES, 128, per)
    res = _run(build_cast(per), [{"w": fl[c]} for c in range(NCORES)])
    wbf = np.concatenate([r["wb"].reshape(-1) for r in res])
    del flat, fl, res

    def getw(nm):
        o, sh = offs[nm]
        return wbf[o:o + int(np.prod(sh))].reshape(sh)

    xT_l = _fm(x, 32)
    cst = np.zeros((128, 256), f32)
    kk = np.arange(128)
    cst[:, 0:128] = (kk[:, None] <= kk[None, :]).astype(f32)
    cst[:, 128:256] = np.where(kk[:, None] <= kk[None, :], 0.0, -30000.0)
    esel = np.zeros((16, 2048), f32)
    for h in range(16):
        esel[h, h * 128:(h + 1) * 128] = 1.0
    idb = np.eye(128, dtype=f32).astype(_bf)
    tri = cst[:, 0:128].astype(_bf)
    cwm = np.asarray(mamba_conv_w, f32)[0]
    cbm = np.asarray(mamba_conv_b, f32)[0]
    in_maps = []
    for g in range(8):
        par = np.zeros((128, NPAR), f32)
        par[:, PG:PG + 32] = np.asarray(norm_a, f32)[0].reshape(32, 128).T
        idx = np.concatenate([np.arange(g * 1024, (g + 1) * 1024), 8192 + np.arange(g * 128, (g + 1) * 128),
                              9216 + np.arange(g * 128, (g + 1) * 128)])
        par[:, PCW:PCW + 40] = cwm[:, idx].reshape(4, 10, 128).transpose(2, 1, 0).reshape(128, 40)
        par[:, PCB:PCB + 10] = cbm[idx].reshape(10, 128).T
        par[:, PDTB:PDTB + 16] = np.asarray(mamba_dt_bias, f32)[0][None, g * 16:(g + 1) * 16]
        par[:, PALOG:PALOG + 16] = np.asarray(mamba_a_log, f32)[0][None, g * 16:(g + 1) * 16]
        par[:, PDD:PDD + 16] = np.asarray(mamba_d, f32)[0][None, g * 16:(g + 1) * 16]
        par[:, PNG:PNG + 1024] = np.asarray(mamba_norm, f32)[0][None, g * 1024:(g + 1) * 1024]
        in_maps.append({"xT": xT_l, "wx": getw(f"wx{g}"), "wdt": getw(f"wdt{g}"), "par": par, "cst": cst, "esel": esel, "idb": idb})
    nc1, _ = build_mamba()
    res = _run(nc1, in_maps)
    Y = np.concatenate([r["y_o"] for r in res], 1)
    del in_maps, res, xT_l

    def tok_inputs(act_bf, h_f32, ktin, c):
        a = np.zeros((1026, act_bf.shape[1]), act_bf.dtype)
        hh = np.zeros((1026, 4096), f32)
        lo = 1024 * c - 2
        s0 = max(lo, 0)
        a[s0 - lo:] = act_bf[s0:1024 * c + 1024]
        hh[s0 - lo:] = h_f32[s0:1024 * c + 1024]
        return _fm(a, ktin), _fm(hh, 32)

    parA = _tok_par(np.asarray(ffn_norm, f32)[0], np.asarray(kv_norm, f32), np.asarray(norm_b, f32)[0],
                    np.asarray(ffn_conv_w, f32)[0], np.asarray(ffn_conv_b, f32)[0])
    in_maps = []
    for c in range(8):
        inT, hT = tok_inputs(Y, x, 64, c)
        in_maps.append({"inT": inT, "hT": hT, "wp": getw("wpA"), "wu": getw("wuA"), "wd": getw("wdA"), "wq": getw("wqA"),
                        "par": parA, "cs": _cs_table(1024 * c)})
    nc2, _ = build_tok("A", 64)
    res = _run(nc2, in_maps)
    h1 = np.concatenate([r["hoT"].transpose(2, 1, 0).reshape(1024, 4096) for r in res], 0)
    Q = np.concatenate([r["q_o"] for r in res], 0)
    KV = np.concatenate([r["kv_o"] for r in res], 0)
    del in_maps, res, Y

    tb = np.tile(np.where(kk[:, None] <= kk[None, :], 0.0, -BIG).astype(f32), (1, 4)).astype(_bf)
    en = np.zeros((32, 32 * 128), f32)
    for n_ in range(32):
        en[n_, n_ * 128:(n_ + 1) * 128] = BIG
    en = en.astype(_bf)
    in_maps = []
    for kv in range(8):
        q = Q[:, kv * 512:(kv + 1) * 512]
        qT = np.ascontiguousarray(q.reshape(64, 128, 4, 128).transpose(3, 0, 2, 1).reshape(128, 64, 512))
        kT = np.ascontiguousarray(KV[:, kv * 128:(kv + 1) * 128].T)
        vl = np.ascontiguousarray(KV[:, 1024 + kv * 128:1024 + (kv + 1) * 128].reshape(64, 128, 128).transpose(1, 0, 2))
        in_maps.append({"qT": qT, "kT": kT, "v": vl, "tb": tb, "idb": idb, "en": en})
    nc3, _ = build_attn()
    res = _run(nc3, in_maps)
    OT = np.stack([r["oT_o"].reshape(128, 64, 4, 128) for r in res], 0)
    OT = np.ascontiguousarray(OT.transpose(1, 0, 3, 2, 4).reshape(128, 32, 8192))
    del in_maps, res, Q, KV

    parB = _tok_par(np.asarray(ffn_norm, f32)[1], np.asarray(final_norm, f32), None,
                    np.asarray(ffn_conv_w, f32)[1], np.asarray(ffn_conv_b, f32)[1])
    in_maps = []
    for c in range(8):
        inT = np.zeros((128, 32, 1026), _bf)
        lo = 1024 * c - 2
        s0 = max(lo, 0)
        inT[:, :, s0 - lo:] = OT[:, :, s0:1024 * c + 1024]
        hh = np.zeros((1026, 4096), f32)
        hh[s0 - lo:] = h1[s0:1024 * c + 1024]
        hT = _fm(hh, 32)
        in_maps.append({"inT": inT, "hT": hT, "wp": getw("wpB"), "wu": getw("wuB"), "wd": getw("wdB"), "par": parB})
    nc4, _ = build_tok("B", 32)
    res = _run(nc4, in_maps)
    out = np.concatenate([r["hoT"].transpose(2, 1, 0).reshape(1024, 4096) for r in res], 0)
    return np.ascontiguousarray(out.astype(f32))[None]
```

```python
import numpy as np
import ml_dtypes
import concourse.bass as bass
import concourse.mybir as mybir
from concourse.bass_utils import run_bass_kernel_spmd

F32 = mybir.dt.float32
F32R = mybir.dt.float32r
BF16 = mybir.dt.bfloat16
AF = mybir.ActivationFunctionType
ALU = mybir.AluOpType
AX = mybir.AxisListType


class Buf:
    __slots__ = ("t", "name", "last_w", "readers", "dsem", "dcnt")

    def __init__(self, t, name):
        self.t = t
        self.name = name
        self.last_w = None
        self.readers = []
        self.dsem = None
        self.dcnt = 0

    def __getitem__(self, k):
        return self.t[k]


class _Rec:
    def __init__(self):
        self.call = None

    def __getattr__(self, name):
        def f(*a, **k):
            self.call = (name, a, k)
            return None
        return f


class Op:
    __slots__ = ("call", "eng", "fn", "reads", "writes", "is_dma", "owner", "deps", "need_inc", "semval", "idx", "is_final", "raw")


class Prog:
    ENGS = ("pe", "act", "dve", "pool", "sp")

    def __init__(self, nc, same_engine_sync=True):
        self.nc = nc
        self.eng = {"pe": nc.tensor, "act": nc.scalar, "dve": nc.vector, "pool": nc.gpsimd, "sp": nc.sync}
        self.ops = []
        self.same_engine_sync = same_engine_sync
        self.nbuf = 0

    def sbuf(self, shape, dtype, name=None):
        self.nbuf += 1
        name = "s_" + (name or f"sb{self.nbuf}")
        return Buf(self.nc.alloc_sbuf_tensor(name, list(shape), dtype), name)

    def psum(self, shape, dtype=F32, name=None):
        self.nbuf += 1
        name = "p_" + (name or f"ps{self.nbuf}")
        return Buf(self.nc.alloc_psum_tensor(name, list(shape), dtype), name)

    def dram(self, name, shape, dtype, kind="Internal"):
        return Buf(self.nc.dram_tensor(name, list(shape), dtype, kind=kind), name)

    def op(self, eng, fn, reads=(), writes=()):
        o = Op()
        o.eng = eng
        o.fn = None
        r = _Rec()
        fn(r)
        o.call = r.call
        o.reads = tuple(reads)
        o.writes = tuple(writes)
        o.is_dma = False
        o.owner = None
        o.need_inc = False
        o.is_final = False
        o.idx = len(self.ops)
        self.ops.append(o)
        return o

    def dma(self, queue, out_ap, in_ap, reads=(), writes=(), owner=None, **kw):
        o = self.op(queue, lambda e: e.dma_start(out=out_ap, in_=in_ap, **kw), reads, writes)
        o.is_dma = True
        o.owner = owner if owner is not None else (writes[0] if writes else reads[0])
        return o

    def emit(self):
        nc = self.nc
        ops = self.ops
        for i_, o_ in enumerate(ops):
            o_.idx = i_
        for o in ops:
            deps = set()
            raw = set()
            for b in o.reads:
                if b.last_w is not None:
                    deps.add(b.last_w)
                    raw.add(b.last_w)
            o.raw = raw
            for b in o.writes:
                if b.last_w is not None:
                    deps.add(b.last_w)
                deps.update(b.readers)
            for b in o.reads:
                if (not o.is_dma) and b.readers:
                    q = ops[b.readers[-1]]
                    if (not q.is_dma) and q.eng == o.eng:
                        b.readers[-1] = o.idx
                        continue
                b.readers.append(o.idx)
            for b in o.writes:
                b.last_w = o.idx
                b.readers = []
            deps.discard(o.idx)
            if o.is_final:
                last = {}
                for q in ops[:o.idx]:
                    if q.is_dma:
                        deps.add(q.idx)
                    else:
                        last[q.eng] = q.idx
                deps.update(last.values())
            o.deps = deps
            for d in deps:
                p = ops[d]
                if p.is_dma:
                    continue
                if p.eng != o.eng or (self.same_engine_sync and o.eng != "pe" and d in o.raw):
                    p.need_inc = True
                if p.eng == o.eng and o.is_dma:
                    p.need_inc = True
        esem = {e: nc.alloc_semaphore(f"es_{e}") for e in self.ENGS}
        ecnt = {e: 0 for e in self.ENGS}
        for o in ops:
            if o.is_dma:
                b = o.owner
                if b.dsem is None:
                    b.dsem = nc.alloc_semaphore(f"ds_{b.name}")
                b.dcnt += 16
                o.semval = (b.dsem, b.dcnt)
            elif o.need_inc:
                ecnt[o.eng] += 1
                o.semval = (esem[o.eng], ecnt[o.eng])
            else:
                o.semval = None
        waited = {e: {} for e in self.ENGS}
        nwaits = 0
        for o in ops:
            e = self.eng[o.eng]
            w = waited[o.eng]
            need = {}
            for d in o.deps:
                p = ops[d]
                if p.semval is None:
                    continue
                if (not p.is_dma) and p.eng == o.eng and not (o.is_dma or (self.same_engine_sync and o.eng != "pe" and d in o.raw)):
                    continue
                s, v = p.semval
                key = id(s)
                if w.get(key, 0) >= v:
                    continue
                if key not in need or need[key][1] < v:
                    need[key] = (s, v)
            for key, (s, v) in need.items():
                e.wait_ge(s, v)
                w[key] = v
                nwaits += 1
            ins = None
            if o.call is not None:
                nm, a_, k_ = o.call
                ins = getattr(e, nm)(*a_, **k_)
            if o.semval is not None and ins is not None:
                ins.then_inc(o.semval[0], 16 if o.is_dma else 1)
        self.final = {"esem": esem, "ecnt": ecnt, "nwaits": nwaits}
        return self.final

    def finish(self, eng="sp"):
        o = self.op(eng, lambda e: None)
        o.is_final = True


D = 4096
DT = 32
FF = 11008
FT = 86
TP = 342
EPS = 1e-5
WU = 43 * 128


class WStream:
    def __init__(self, P, nbuf, unit_cols, queue="sp", pf=3):
        self.P = P
        self.bufs = [P.sbuf([128, unit_cols], BF16, f"wp{i}") for i in range(nbuf)]
        self.reqs = []
        self.issued = 0
        self.taken = 0
        self.queue = queue
        self.pf = pf

    def add(self, dbuf, ap, ncols):
        self.reqs.append((dbuf, ap, ncols))

    def _issue(self, i):
        dbuf, ap, ncols = self.reqs[i]
        b = self.bufs[i % len(self.bufs)]
        self.P.dma(self.queue, b[:, 0:ncols], ap, reads=[dbuf], writes=[b])

    def next(self):
        i = self.taken
        lim = min(len(self.reqs), i + self.pf + 1)
        while self.issued < lim:
            self._issue(self.issued)
            self.issued += 1
        self.taken += 1
        return self.bufs[i % len(self.bufs)]


def build_tok(mode, KTIN):
    nc = bass.Bass("TRN2", target_bir_lowering=False)
    P = Prog(nc)
    NTOK = 1026
    inT = P.dram("inT", [128, KTIN, NTOK], BF16, kind="ExternalInput")
    hT = P.dram("hT", [128, DT, NTOK], F32, kind="ExternalInput")
    wp_d = P.dram("wp", [DT, 128, KTIN, 128], BF16, kind="ExternalInput")
    wu_d = P.dram("wu", [FT, 128, 2, DT, 128], BF16, kind="ExternalInput")
    wd_d = P.dram("wd", [DT, 128, FT, 128], BF16, kind="ExternalInput")
    par_d = P.dram("par", [128, 4 * DT + 172 * 4], F32, kind="ExternalInput")
    if mode == "A":
        wq_d = P.dram("wq", [12, 128, DT, 512], BF16, kind="ExternalInput")
        cs_d = P.dram("cs", [128, 9, 128], F32, kind="ExternalInput")
        q_o = P.dram("q_o", [1024, 4096], BF16, kind="ExternalOutput")
        kv_o = P.dram("kv_o", [1024, 2048], BF16, kind="ExternalOutput")
    hoT = P.dram("hoT", [128, DT, 1024], F32, kind="ExternalOutput")
    hmid_d = P.dram("hmid", [128, DT, 1024], F32)
    if mode == "B":
        hout_d = P.dram("hout", [128, DT, 1024], F32)
    else:
        hout_d = hoT
    hmid_v = [Buf(hmid_d.t, f"hmid{i}") for i in range(DT)]
    hout_v = [Buf(hout_d.t, f"hout{i}") for i in range(DT)]

    par = P.sbuf([128, 4 * DT + 172 * 4], F32, "par")
    G_FFN, G_2, G_3, CW, CB = 0, DT, 2 * DT, 4 * DT, 4 * DT + 172 * 3
    big = P.sbuf([128, FT * TP], BF16, "big")
    hn = P.sbuf([128, DT * TP], BF16, "hn")
    ones = P.sbuf([128, 128], BF16, "ones")
    rstd = P.sbuf([128, TP], F32, "rstd")
    tails = P.sbuf([128, 172 * 2], F32, "tails")
    ws = WStream(P, 6, WU, "sp", pf=5)
    stg = [P.sbuf([128, TP], F32, f"stg{i}") for i in range(3)]
    sqb = [P.sbuf([128, TP], BF16, f"sq{i}") for i in range(2)]
    ub = [P.sbuf([128, 2 * (TP + 2)], F32, f"ub{i}") for i in range(2)]
    cv = [P.sbuf([128, 2 * TP], F32, f"cv{i}") for i in range(2)]
    ps = [P.psum([128, 512], F32, f"psb{i}") for i in range(8)]
    PS_MM = (ps[0], ps[1])
    PS_UP = ((ps[2], ps[3]), (ps[4], ps[5]))
    PS_SS = ps[6]
    PS_TM = ps[7]
    if mode == "A":
        cs = P.sbuf([128, 9 * 128], F32, "cs")
        rs_tm = P.sbuf([128, 4], F32, "rs_tm")
        qst = cv
        qrt = ub
        qob = [P.sbuf([128, 512], BF16, f"qob{i}") for i in range(2)]
        onecol = P.sbuf([128, 1], F32, "onecol")

    P.dma("pool", par[:], par_d[:], reads=[par_d], writes=[par])
    P.op("pool", lambda e: e.memset(ones[:], 1.0), writes=[ones])
    P.op("pool", lambda e: e.memset(tails[:], 0.0), writes=[tails])
    for sg_ in stg:
        P.op("pool", (lambda e, sg_=sg_: e.memset(sg_[:], 0.0)), writes=[sg_])
    if mode == "A":
        P.dma("pool", cs[:], cs_d[:].rearrange("p a b -> p (a b)"), reads=[cs_d], writes=[cs])
        P.op("pool", lambda e: e.memset(onecol[:], 1.0), writes=[onecol])

    passes = [(0, TP, False), (TP, TP, False), (2 * TP, TP, False)]
    for (c0, n, halo) in passes:
        for dt in range(DT):
            for h0 in range(0, KTIN, 32):
                ws.add(wp_d, wp_d[dt, :, h0:h0 + 32, :].rearrange("p a b -> p (a b)"), 32 * 128)
        for j in range(FT):
            for half in range(2):
                ws.add(wu_d, wu_d[j, :, half].rearrange("p b c -> p (b c)"), DT * 128)
        if halo:
            continue
        for dt in range(DT):
            for h0 in range(0, FT, 43):
                ws.add(wd_d, wd_d[dt, :, h0:h0 + 43, :].rearrange("p a b -> p (a b)"), 43 * 128)
        if mode == "A":
            for ch in range(12):
                for k0 in range(0, DT, 8):
                    ws.add(wq_d, wq_d[ch, :, k0:k0 + 8, :].rearrange("p a b -> p (a b)"), 8 * 512)

    cnt = {"mm": 0, "up": 0, "stg": 0, "sq": 0, "q": 0}

    def rstd_from(ssq_ps, out_ap, n, eng_cols):
        P.op("act", lambda e: e.activation(out_ap, ssq_ps[:, 0:n], AF.Sqrt, bias=epsb[:, 0:1], scale=1.0 / D),
             reads=[ssq_ps, epsb], writes=[rstd])
        P.op("dve", lambda e: e.reciprocal(out_ap, out_ap), reads=[rstd], writes=[rstd])

    epsb = P.sbuf([128, 1], F32, "epsb")
    P.op("pool", lambda e: e.memset(epsb[:], EPS), writes=[epsb])

    for pi, (c0, n, halo) in enumerate(passes):
        o0 = c0 - 2
        lo = 2 if c0 == 0 else 0
        P.dma("sp", big[:, 0:KTIN * TP].rearrange("p (k t) -> p k t", t=TP)[:, :, 0:n], inT[:, :, c0:c0 + n],
              reads=[inT], writes=[big])
        for dt in range(DT):
            pb = PS_MM[cnt["mm"] % 2]; cnt["mm"] += 1
            for h0 in range(0, KTIN, 32):
                w = ws.next()
                for k in range(32):
                    kk = h0 + k
                    P.op("pe", (lambda e, w=w, k=k, kk=kk, pb=pb: e.matmul(
                        pb[:, 0:n], w[:, k * 128:(k + 1) * 128], big[:, kk * TP:kk * TP + n],
                        start=(kk == 0), stop=(kk == KTIN - 1))), reads=[w, big], writes=[pb])
            sg = stg[cnt["stg"] % 3]; cnt["stg"] += 1
            P.dma("pool", sg[:, 0:n], hT[:, dt, c0:c0 + n], reads=[hT], writes=[sg])
            P.op("dve", (lambda e, sg=sg, pb=pb: e.tensor_tensor(sg[:, 0:n], pb[:, 0:n], sg[:, 0:n], ALU.add)),
                 reads=[pb, sg], writes=[sg])
            if not halo:
                P.dma("pool", hmid_d[:, dt, o0 + lo:o0 + n], sg[:, lo:n], reads=[sg], writes=[hmid_v[dt]], owner=sg)
            sq = sqb[cnt["sq"] % 2]; cnt["sq"] += 1
            P.op("act", (lambda e, sg=sg, sq=sq: e.activation(sq[:, 0:n], sg[:, 0:n], AF.Square)), reads=[sg], writes=[sq])
            P.op("pe", (lambda e, sq=sq, dt=dt: e.matmul(PS_SS[:, 0:n], ones[:], sq[:, 0:n], start=(dt == 0), stop=(dt == DT - 1))),
                 reads=[ones, sq], writes=[PS_SS])
            P.op("dve", (lambda e, sg=sg, dt=dt: e.tensor_scalar(hn[:, dt * TP:dt * TP + n], sg[:, 0:n],
                                                              par[:, G_FFN + dt:G_FFN + dt + 1], None, ALU.mult)),
                 reads=[sg, par], writes=[hn])
        rstd_from(PS_SS, rstd[:, 0:n], n, 128)
        for j in range(FT):
            pg, pv = PS_UP[cnt["up"] % 2]
            u = ub[cnt["up"] % 2]
            c = cv[cnt["up"] % 2]
            cnt["up"] += 1
            for half, pb in ((0, pg), (1, pv)):
                w = ws.next()
                for k in range(DT):
                    P.op("pe", (lambda e, w=w, k=k, half=half, pb=pb: e.matmul(
                        pb[:, 0:n], w[:, k * 128:(k + 1) * 128], hn[:, k * TP:k * TP + n],
                        start=(k == 0), stop=(k == DT - 1))), reads=[w, hn], writes=[pb])
            ug = u[:, 0:TP + 2]
            uv = u[:, TP + 2:2 * TP + 4]
            tg = j
            tv = FT + j
            P.op("dve", (lambda e, pg=pg, ug=ug: e.tensor_tensor(ug[:, 2:2 + n], pg[:, 0:n], rstd[:, 0:n], ALU.mult)),
                 reads=[pg, rstd], writes=[u])
            P.op("dve", (lambda e, pv=pv, uv=uv: e.tensor_tensor(uv[:, 2:2 + n], pv[:, 0:n], rstd[:, 0:n], ALU.mult)),
                 reads=[pv, rstd], writes=[u])
            if halo:
                P.op("pool", (lambda e, ug=ug, tg=tg: e.tensor_copy(tails[:, 2 * tg:2 * tg + 2], ug[:, 2:4])), reads=[u], writes=[tails])
                P.op("pool", (lambda e, uv=uv, tv=tv: e.tensor_copy(tails[:, 2 * tv:2 * tv + 2], uv[:, 2:4])), reads=[u], writes=[tails])
                continue
            P.op("pool", (lambda e, ug=ug, tg=tg: e.tensor_copy(ug[:, 0:2], tails[:, 2 * tg:2 * tg + 2])), reads=[tails], writes=[u])
            P.op("pool", (lambda e, uv=uv, tv=tv: e.tensor_copy(uv[:, 0:2], tails[:, 2 * tv:2 * tv + 2])), reads=[tails], writes=[u])
            P.op("pool", (lambda e, ug=ug, tg=tg: e.tensor_copy(tails[:, 2 * tg:2 * tg + 2], ug[:, n:n + 2])), reads=[u], writes=[tails])
            P.op("pool", (lambda e, uv=uv, tv=tv: e.tensor_copy(tails[:, 2 * tv:2 * tv + 2], uv[:, n:n + 2])), reads=[u], writes=[tails])
            cg = c[:, 0:TP]
            cvv = c[:, TP:2 * TP]
            for (uu, cc, t) in ((ug, cg, tg), (uv, cvv, tv)):
                P.op("dve", (lambda e, uu=uu, cc=cc, t=t: e.tensor_scalar(
                    cc[:, 0:n], uu[:, 0:n], par[:, CW + 3 * t:CW + 3 * t + 1], par[:, CB + t:CB + t + 1], ALU.mult, ALU.add)),
                    reads=[u, par], writes=[c])
                for kk in (1, 2):
                    P.op("dve", (lambda e, uu=uu, cc=cc, t=t, kk=kk: e.scalar_tensor_tensor(
                        cc[:, 0:n], uu[:, kk:kk + n], par[:, CW + 3 * t + kk:CW + 3 * t + kk + 1], cc[:, 0:n], ALU.mult, ALU.add)),
                        reads=[u, par, c], writes=[c])
            P.op("act", (lambda e, cg=cg: e.activation(cg[:, 0:n], cg[:, 0:n], AF.Silu)), reads=[c], writes=[c])
            P.op("dve", (lambda e, cg=cg, cvv=cvv, j=j: e.tensor_tensor(big[:, j * TP:j * TP + n], cg[:, 0:n], cvv[:, 0:n], ALU.mult)),
                 reads=[c], writes=[big])
        if halo:
            continue
        for dt in range(DT):
            pb = PS_MM[cnt["mm"] % 2]; cnt["mm"] += 1
            for h0 in range(0, FT, 43):
                w = ws.next()
                for k in range(43):
                    kk = h0 + k
                    P.op("pe", (lambda e, w=w, k=k, kk=kk, pb=pb: e.matmul(
                        pb[:, 0:n], w[:, k * 128:(k + 1) * 128], big[:, kk * TP:kk * TP + n],
                        start=(kk == 0), stop=(kk == FT - 1))), reads=[w, big], writes=[pb])
            sg = stg[cnt["stg"] % 3]; cnt["stg"] += 1
            P.dma("pool", sg[:, lo:n], hmid_d[:, dt, o0 + lo:o0 + n], reads=[hmid_v[dt]], writes=[sg])
            P.op("dve", (lambda e, sg=sg, pb=pb: e.tensor_tensor(sg[:, 0:n], pb[:, 0:n], sg[:, 0:n], ALU.add)),
                 reads=[pb, sg], writes=[sg])
            P.dma("pool", hout_d[:, dt, o0 + lo:o0 + n], sg[:, lo:n], reads=[sg], writes=[hout_v[dt]], owner=sg)
            sq = sqb[cnt["sq"] % 2]; cnt["sq"] += 1
            P.op("act", (lambda e, sg=sg, sq=sq: e.activation(sq[:, 0:n], sg[:, 0:n], AF.Square)), reads=[sg], writes=[sq])
            P.op("pe", (lambda e, sq=sq, dt=dt: e.matmul(PS_SS[:, 0:n], ones[:], sq[:, 0:n], start=(dt == 0), stop=(dt == DT - 1))),
                 reads=[ones, sq], writes=[PS_SS])
            if mode == "A":
                P.op("dve", (lambda e, sg=sg, dt=dt: e.tensor_scalar(hn[:, dt * TP:dt * TP + n], sg[:, 0:n],
                                                                  par[:, G_2 + dt:G_2 + dt + 1], None, ALU.mult)),
                     reads=[sg, par], writes=[hn])
        rstd_from(PS_SS, rstd[:, 0:n], n, 128)
        if mode == "B":
            for dt in range(DT):
                sg = stg[cnt["stg"] % 3]; cnt["stg"] += 1
                P.dma("pool", sg[:, lo:n], hout_d[:, dt, o0 + lo:o0 + n], reads=[hout_v[dt]], writes=[sg])
                P.op("dve", (lambda e, sg=sg, dt=dt: e.scalar_tensor_tensor(sg[:, 0:n], sg[:, 0:n], par[:, G_2 + dt:G_2 + dt + 1],
                                                                         rstd[:, 0:n], ALU.mult, ALU.mult)),
                     reads=[sg, par, rstd], writes=[sg])
                P.dma("pool", hoT[:, dt, o0 + lo:o0 + n], sg[:, lo:n], reads=[sg], writes=[], owner=sg)
        else:
            ttiles = [(ts, min(128, n - ts)) for ts in range(0, n, 128)]
            for tt, (ts, m) in enumerate(ttiles):
                P.op("pe", (lambda e, tt=tt, ts=ts, m=m: e.matmul(PS_TM[0:m, tt:tt + 1], rstd[0:1, ts:ts + m], onecol[0:1, 0:1],
                                                                  start=True, stop=True)), reads=[rstd, onecol], writes=[PS_TM])
            P.op("act", lambda e: e.copy(rs_tm[:, 0:4], PS_TM[:, 0:4]), reads=[PS_TM], writes=[rs_tm])
            for dt in range(DT):
                sg = stg[cnt["stg"] % 3]; cnt["stg"] += 1
                P.dma("pool", sg[:, lo:n], hout_d[:, dt, o0 + lo:o0 + n], reads=[hout_v[dt]], writes=[sg])
                P.op("dve", (lambda e, sg=sg, dt=dt: e.tensor_scalar(big[:, dt * TP:dt * TP + n], sg[:, 0:n],
                                                                  par[:, G_3 + dt:G_3 + dt + 1], None, ALU.mult)),
                     reads=[sg, par], writes=[big])
            for ch in range(12):
                src = big if ch < 8 else hn
                pbs = [ps[0], ps[1], ps[2], ps[3]]
                for piece in range(4):
                    w = ws.next()
                    for tt, (ts, m) in enumerate(ttiles):
                        for k in range(8):
                            kk = piece * 8 + k
                            P.op("pe", (lambda e, w=w, k=k, kk=kk, tt=tt, ts=ts, m=m, src=src: e.matmul(
                                pbs[tt][0:m, 0:512], src[:, kk * TP + ts:kk * TP + ts + m], w[:, k * 512:(k + 1) * 512],
                                start=(kk == 0), stop=(kk == DT - 1))), reads=[w, src], writes=[pbs[tt]])
                for tt, (ts, m) in enumerate(ttiles):
                    tile_idx = pi * 3 + tt
                    qs = qst[cnt["q"] % 2]; qr = qrt[cnt["q"] % 2]; qo = qob[cnt["q"] % 2]; cnt["q"] += 1
                    rope = ch < 10
                    if rope:
                        P.op("act", (lambda e, qs=qs, tt=tt, m=m: e.activation(qs[0:m, 0:512], pbs[tt][0:m, :], AF.Copy, scale=rs_tm[0:m, tt:tt + 1])),
                             reads=[pbs[tt], rs_tm], writes=[qs])
                        q3 = qs[0:m, 0:512].rearrange("p (h two d) -> p h two d", two=2, d=64)
                        r3 = qr[0:m, 0:512].rearrange("p (h two d) -> p h two d", two=2, d=64)
                        o3 = qo[0:m, :].rearrange("p (h two d) -> p h two d", two=2, d=64)
                        cosb = cs[0:m, tile_idx * 128:tile_idx * 128 + 64].unsqueeze(1).broadcast_to([m, 4, 64])
                        sinb = cs[0:m, tile_idx * 128 + 64:tile_idx * 128 + 128].unsqueeze(1).broadcast_to([m, 4, 64])
                        P.op("pool", (lambda e: e.tensor_tensor(r3[:, :, 0, :], q3[:, :, 1, :], sinb, ALU.mult)), reads=[qs, cs], writes=[qr])
                        P.op("pool", (lambda e: e.tensor_tensor(r3[:, :, 1, :], q3[:, :, 0, :], sinb, ALU.mult)), reads=[qs, cs], writes=[qr])
                        P.op("dve", (lambda e: e.tensor_tensor(q3[:, :, 0, :], q3[:, :, 0, :], cosb, ALU.mult)), reads=[qs, cs], writes=[qs])
                        P.op("dve", (lambda e: e.tensor_tensor(q3[:, :, 1, :], q3[:, :, 1, :], cosb, ALU.mult)), reads=[qs, cs], writes=[qs])
                        P.op("dve", (lambda e: e.tensor_tensor(o3[:, :, 0, :], q3[:, :, 0, :], r3[:, :, 0, :], ALU.subtract)), reads=[qs, qr], writes=[qo])
                        P.op("dve", (lambda e: e.tensor_tensor(o3[:, :, 1, :], q3[:, :, 1, :], r3[:, :, 1, :], ALU.add)), reads=[qs, qr], writes=[qo])
                    else:
                        P.op("act", (lambda e, qo=qo, tt=tt, m=m: e.activation(qo[0:m, :], pbs[tt][0:m, :], AF.Copy, scale=rs_tm[0:m, tt:tt + 1])),
                             reads=[pbs[tt], rs_tm], writes=[qo])
                    r0 = lo if ts == 0 else 0
                    tok0 = c0 + ts + r0 - 2
                    nr = m - r0
                    if ch < 8:
                        P.dma("pool", q_o[tok0:tok0 + nr, ch * 512:(ch + 1) * 512], qo[r0:m, :], reads=[qo], writes=[], owner=qo)
                    else:
                        P.dma("pool", kv_o[tok0:tok0 + nr, (ch - 8) * 512:(ch - 7) * 512], qo[r0:m, :], reads=[qo], writes=[], owner=qo)
    P.finish("sp")
    info = P.emit()
    return nc, info


D = 4096
KT = 32
ST = 512
NTOK = 8192
EPS = 1e-5
NH = 16
XQ = "sp"
BCE = "pool"
NXS = 8
PG, PCW, PCB, PDTB, PALOG, PDD, PNG = 0, 32, 32 + 40, 32 + 50, 32 + 66, 32 + 82, 32 + 98
NPAR = 32 + 98 + 1024


def build_mamba(n_super=NTOK // ST, debug=False):
    nc = bass.Bass("TRN2", target_bir_lowering=False)
    P = Prog(nc)
    xT = P.dram("xT", [128, KT, NTOK], F32, kind="ExternalInput")
    wx_d = P.dram("wx", [18, 128, KT, 128], BF16, kind="ExternalInput")
    wdt_d = P.dram("wdt", [128, KT * 16], BF16, kind="ExternalInput")
    par_d = P.dram("par", [128, NPAR], F32, kind="ExternalInput")
    cst_d = P.dram("cst", [128, 256], F32, kind="ExternalInput")
    esel_d = P.dram("esel", [16, 2048], F32, kind="ExternalInput")
    idb_d = P.dram("idb", [128, 128], BF16, kind="ExternalInput")
    y_o = P.dram("y_o", [NTOK, 1024], BF16, kind="ExternalOutput")

    par = P.sbuf([128, NPAR], F32, "par")
    cst = P.sbuf([128, 256], F32, "cst")
    esel = P.sbuf([16, 2048], F32, "esel")
    idb = P.sbuf([128, 128], BF16, "idb")
    wdt = P.sbuf([128, KT * 16], BF16, "wdt")
    ones = P.sbuf([128, 128], BF16, "ones")
    onecol = P.sbuf([128, 1], F32, "onecol")
    epsb = P.sbuf([128, 1], F32, "epsb")
    Arep = P.sbuf([128, NH], F32, "Arep")
    uT = P.sbuf([128, KT * ST], BF16, "uT")
    xbcT2 = [P.sbuf([128, 10 * ST], BF16, f"xbcT{i}") for i in range(2)]
    szT2 = [P.sbuf([128, 8 * ST], BF16, f"szT{i}") for i in range(2)]
    rstd = P.sbuf([128, ST], F32, "rstd")
    rs_tm = P.sbuf([128, 4], F32, "rs_tm")
    tails = P.sbuf([128, 30], F32, "tails")
    S = P.sbuf([128, 1024], F32, "S")
    S_bf = P.sbuf([128, 1024], BF16, "S_bf")
    ws = WStream(P, 4, KT * 128, "sp", pf=3)
    xs = [P.sbuf([128, ST], F32, f"xs{i}") for i in range(NXS)]
    sqb = [P.sbuf([128, ST], BF16, f"sq{i}") for i in range(2)]
    raw = [P.sbuf([128, ST + 3], F32, f"raw{i}") for i in range(2)]
    acc = [P.sbuf([128, ST], F32, f"acc{i}") for i in range(2)]
    Xtm2 = [P.sbuf([128, 1024], BF16, f"Xtm{i}") for i in range(2)]
    Btm2 = [P.sbuf([128, 128], BF16, f"Btm{i}") for i in range(2)]
    Xdt2 = [P.sbuf([128, 1024], BF16, f"Xdt{i}") for i in range(2)]
    Xdte2 = [P.sbuf([128, 1024], BF16, f"Xdte{i}") for i in range(2)]
    arg = [P.sbuf([128, 512], F32, f"arg{i}") for i in range(2)]
    Ldec = [P.sbuf([128, 512], F32, f"Ldec{i}") for i in range(2)]
    MT2 = [P.sbuf([128, 2048], BF16, f"MT{i}") for i in range(2)]
    yd_sb2 = [P.sbuf([128, 1024], F32, f"yd_sb{i}") for i in range(2)]
    yb = P.sbuf([128, 1024], F32, "yb")
    yg = P.sbuf([128, 1024], F32, "yg")
    junk = P.sbuf([128, 1024], BF16, "junk")
    yn = [P.sbuf([128, 1024], BF16, f"yn{i}") for i in range(2)]
    dtv2 = [P.sbuf([128, NH], F32, f"dtv{i}") for i in range(2)]
    e1 = P.sbuf([128, NH], F32, "e1")
    av = P.sbuf([128, NH], F32, "av")
    acs_sb2 = [P.sbuf([128, NH], F32, f"acs_sb{i}") for i in range(2)]
    acsT_sb = P.sbuf([16, 128], F32, "acsT_sb")
    ea2 = [P.sbuf([128, NH], F32, f"ea{i}") for i in range(2)]
    cdb2 = [P.sbuf([128, NH], F32, f"cdb{i}") for i in range(2)]
    dearg = P.sbuf([128, NH], F32, "dearg")
    de2 = [P.sbuf([128, NH], F32, f"de{i}") for i in range(2)]
    ss = P.sbuf([128, 1], F32, "ss")
    rsn = P.sbuf([128, 1], F32, "rsn")

    b0 = P.psum([128, 1024], BF16, "b0")
    b1 = P.psum([128, 1024], BF16, "b1")
    b2 = P.psum([128, 512], F32, "b2")
    b3 = P.psum([128, 512], F32, "b3")
    pA = [P.psum([128, 512], F32, f"pA{i}") for i in range(2)]
    pB = [P.psum([128, 512], F32, f"pB{i}") for i in range(2)]

    T = cst[:, 0:128]
    maskb = cst[:, 128:256]

    P.dma("pool", par[:], par_d[:], reads=[par_d], writes=[par])
    P.dma("pool", cst[:], cst_d[:], reads=[cst_d], writes=[cst])
    P.dma("pool", esel[:], esel_d[:], reads=[esel_d], writes=[esel])
    P.dma("pool", idb[:], idb_d[:], reads=[idb_d], writes=[idb])
    P.dma("pool", wdt[:], wdt_d[:], reads=[wdt_d], writes=[wdt])
    P.op("pool", lambda e: e.memset(ones[:], 1.0), writes=[ones])
    P.op("pool", lambda e: e.memset(onecol[:], 1.0), writes=[onecol])
    P.op("pool", lambda e: e.memset(epsb[:], EPS), writes=[epsb])
    P.op("pool", lambda e: e.memset(tails[:], 0.0), writes=[tails])
    P.op("pool", lambda e: e.memset(S[:], 0.0), writes=[S])
    P.op("pool", lambda e: e.memset(S_bf[:], 0.0), writes=[S_bf])
    P.op("act", lambda e: e.activation(Arep[:], par[:, PALOG:PALOG + NH], AF.Exp), reads=[par], writes=[Arep])
    P.op("dve", lambda e: e.tensor_scalar(Arep[:], Arep[:], -1.0, None, ALU.mult), reads=[Arep], writes=[Arep])

    for st in range(n_super):
        for j in range(18):
            ws.add(wx_d, wx_d[j].rearrange("p a b -> p (a b)"), KT * 128)

    cnt = {"xs": 0, "sq": 0, "t": 0, "g": 0, "yn": 0}
    pending = []
    for st in range(n_super):
        xbcT = xbcT2[st % 2]
        szT = szT2[st % 2]
        t0 = st * ST
        for kt in range(KT):
            x_ = xs[cnt["xs"] % NXS]; cnt["xs"] += 1
            P.dma(XQ, x_[:], xT[:, kt, t0:t0 + ST], reads=[xT], writes=[x_])
            sq = sqb[cnt["sq"] % 2]; cnt["sq"] += 1
            P.op("act", (lambda e, x_=x_, sq=sq: e.activation(sq[:], x_[:], AF.Square)), reads=[x_], writes=[sq])
            P.op("pe", (lambda e, sq=sq, kt=kt: e.matmul(b3[:], ones[:], sq[:], start=(kt == 0), stop=(kt == KT - 1))),
                 reads=[ones, sq], writes=[b3])
            P.op("dve", (lambda e, x_=x_, kt=kt: e.tensor_scalar(uT[:, kt * ST:(kt + 1) * ST], x_[:], par[:, PG + kt:PG + kt + 1], None, ALU.mult)),
                 reads=[x_, par], writes=[uT])
        P.op("act", lambda e: e.activation(rstd[:], b3[:], AF.Ln, bias=epsb[:, 0:1], scale=1.0 / D), reads=[b3, epsb], writes=[rstd])
        P.op("act", lambda e: e.activation(rstd[:], rstd[:], AF.Exp, scale=-0.5), reads=[rstd], writes=[rstd])
        for tt in range(4):
            P.op("pe", (lambda e, tt=tt: e.matmul(b2[:, 300 + tt:301 + tt], rstd[0:1, tt * 128:(tt + 1) * 128], onecol[0:1, 0:1],
                                                  start=True, stop=True)), reads=[rstd, onecol], writes=[b2])
        P.op("act", lambda e: e.copy(rs_tm[:, 0:4], b2[:, 300:304]), reads=[b2], writes=[rs_tm])
        for j in range(18):
            pb = pA[cnt["t"] % 2]
            rw = raw[cnt["t"] % 2]
            ac = acc[cnt["t"] % 2]
            cnt["t"] += 1
            w = ws.next()
            for k in range(KT):
                P.op("pe", (lambda e, w=w, k=k, pb=pb: e.matmul(pb[:], w[:, k * 128:(k + 1) * 128], uT[:, k * ST:(k + 1) * ST],
                                                                start=(k == 0), stop=(k == KT - 1))), reads=[w, uT], writes=[pb])
            if j < 8:
                P.op("dve", (lambda e, pb=pb, ac=ac: e.tensor_tensor(ac[:], pb[:], rstd[:], ALU.mult)), reads=[pb, rstd], writes=[ac])
                P.op("act", (lambda e, ac=ac, j=j: e.activation(szT[:, j * ST:(j + 1) * ST], ac[:], AF.Silu)), reads=[ac], writes=[szT])
                continue
            c = j - 8
            P.op("dve", (lambda e, pb=pb, rw=rw: e.tensor_tensor(rw[:, 3:3 + ST], pb[:], rstd[:], ALU.mult)), reads=[pb, rstd], writes=[rw])
            P.op("pool", (lambda e, rw=rw, c=c: e.tensor_copy(rw[:, 0:3], tails[:, 3 * c:3 * c + 3])), reads=[tails], writes=[rw])
            P.op("pool", (lambda e, rw=rw, c=c: e.tensor_copy(tails[:, 3 * c:3 * c + 3], rw[:, ST:ST + 3])), reads=[rw], writes=[tails])
            P.op("dve", (lambda e, rw=rw, ac=ac, c=c: e.tensor_scalar(ac[:], rw[:, 0:ST], par[:, PCW + 4 * c:PCW + 4 * c + 1],
                                                                    par[:, PCB + c:PCB + c + 1], ALU.mult, ALU.add)),
                 reads=[rw, par], writes=[ac])
            for kk in (1, 2, 3):
                P.op("dve", (lambda e, rw=rw, ac=ac, c=c, kk=kk: e.scalar_tensor_tensor(
                    ac[:], rw[:, kk:kk + ST], par[:, PCW + 4 * c + kk:PCW + 4 * c + kk + 1], ac[:], ALU.mult, ALU.add)),
                    reads=[rw, par, ac], writes=[ac])
            P.op("act", (lambda e, ac=ac, c=c: e.activation(xbcT[:, c * ST:(c + 1) * ST], ac[:], AF.Silu)), reads=[ac], writes=[xbcT])
        def stage1(st, tt, par_i, xbcT, szT):
            Xtm, Btm, Xdt, Xdte, MT, yd_sb = Xtm2[par_i], Btm2[par_i], Xdt2[par_i], Xdte2[par_i], MT2[par_i], yd_sb2[par_i]
            dtv, acs_sb, ea, cdb, de = dtv2[par_i], acs_sb2[par_i], ea2[par_i], cdb2[par_i], de2[par_i]
            c0 = tt * 128
            BT = xbcT[:, 8 * ST + c0:8 * ST + c0 + 128]
            CT = xbcT[:, 9 * ST + c0:9 * ST + c0 + 128]
            for k in range(KT):
                P.op("pe", (lambda e, k=k, c0=c0: e.matmul(b2[:, 272:288], uT[:, k * ST + c0:k * ST + c0 + 128], wdt[:, k * 16:(k + 1) * 16],
                                                           start=(k == 0), stop=(k == KT - 1))), reads=[uT, wdt], writes=[b2])
            P.op("dve", (lambda e, tt=tt: e.scalar_tensor_tensor(dtv[:], b2[:, 272:288], rs_tm[:, tt:tt + 1], par[:, PDTB:PDTB + NH], ALU.mult, ALU.add)),
                 reads=[b2, rs_tm, par], writes=[dtv])
            P.op("act", lambda e: e.activation(e1[:], dtv[:], AF.Exp), reads=[dtv], writes=[e1])
            P.op("act", lambda e: e.activation(dtv[:], e1[:], AF.Ln, bias=onecol[:, 0:1]), reads=[e1, onecol], writes=[dtv])
            P.op("dve", lambda e: e.tensor_tensor(av[:], dtv[:], Arep[:], ALU.mult), reads=[dtv, Arep], writes=[av])
            P.op("pe", lambda e: e.matmul(b2[:, 128:144], T, av[:], start=True, stop=True), reads=[cst, av], writes=[b2])
            P.op("pe", lambda e: e.matmul(b2[0:16, 144:272], av[:], T, start=True, stop=True), reads=[cst, av], writes=[b2])
            P.op("act", lambda e: e.copy(acs_sb[:], b2[:, 128:144]), reads=[b2], writes=[acs_sb])
            P.op("act", lambda e: e.copy(acsT_sb[:], b2[0:16, 144:272]), reads=[b2], writes=[acsT_sb])
            P.op("act", lambda e: e.activation(ea[:], acs_sb[:], AF.Exp), reads=[acs_sb], writes=[ea])
            for j in range(8):
                P.op("pe", (lambda e, j=j, c0=c0: e.transpose(b0[:, j * 128:(j + 1) * 128], xbcT[:, j * ST + c0:j * ST + c0 + 128], idb[:])),
                     reads=[xbcT, idb], writes=[b0])
            P.op("act", lambda e: e.copy(Xtm[:], b0[:]), reads=[b0], writes=[Xtm])
            P.op("pe", (lambda e, BT=BT: e.transpose(b0[:, 0:128], BT, idb[:])), reads=[xbcT, idb], writes=[b0])
            P.op("act", lambda e: e.copy(Btm[:], b0[:, 0:128]), reads=[b0], writes=[Btm])
            P.op("pe", (lambda e, BT=BT, CT=CT: e.matmul(b2[:, 0:128], BT, CT, start=True, stop=True)), reads=[xbcT], writes=[b2])
            for g in range(4):
                ar = arg[cnt["g"] % 2]; Ld = Ldec[cnt["g"] % 2]; cnt["g"] += 1
                for hh in range(4):
                    h = 4 * g + hh
                    P.op("pe", (lambda e, h=h, hh=hh: e.matmul(b3[:, hh * 128:(hh + 1) * 128], esel[0:16, h * 128:(h + 1) * 128], acsT_sb[:],
                                                              start=True, stop=True)), reads=[esel, acsT_sb], writes=[b3])
                b3v = b3[:, :].rearrange("p (h l) -> p h l", l=128)[:, :, 127]
                P.op("act", (lambda e, g=g, b3v=b3v: e.activation(cdb[:, 4 * g:4 * g + 4], b3v, AF.Exp)), reads=[b3], writes=[cdb])
                P.op("dve", (lambda e, g=g, b3v=b3v: e.tensor_tensor(dearg[:, 4 * g:4 * g + 4], b3v, acs_sb[:, 4 * g:4 * g + 4], ALU.subtract)),
                     reads=[b3, acs_sb], writes=[dearg])
                for hh in range(4):
                    h = 4 * g + hh
                    P.op("dve", (lambda e, h=h, hh=hh, ar=ar: e.scalar_tensor_tensor(
                        ar[:, hh * 128:(hh + 1) * 128], b3[:, hh * 128:(hh + 1) * 128], acs_sb[:, h:h + 1], maskb, ALU.subtract, ALU.add)),
                        reads=[b3, acs_sb, cst], writes=[ar])
                P.op("act", (lambda e, ar=ar, Ld=Ld: e.activation(Ld[:], ar[:], AF.Exp)), reads=[ar], writes=[Ld])
                for hh in range(4):
                    h = 4 * g + hh
                    P.op("dve", (lambda e, h=h, hh=hh, Ld=Ld: e.tensor_tensor(MT[:, h * 128:(h + 1) * 128], Ld[:, hh * 128:(hh + 1) * 128],
                                                                          b2[:, 0:128], ALU.mult)), reads=[Ld, b2], writes=[MT])
            P.op("act", lambda e: e.activation(de[:], dearg[:], AF.Exp), reads=[dearg], writes=[de])
            X3 = Xtm[:, :].rearrange("p (h d) -> p h d", d=64)
            Xd3 = Xdt[:, :].rearrange("p (h d) -> p h d", d=64)
            Xe3 = Xdte[:, :].rearrange("p (h d) -> p h d", d=64)
            dt_b = dtv[:, 0:NH].unsqueeze(2).broadcast_to([128, NH, 64])
            de_b = de[:, 0:NH].unsqueeze(2).broadcast_to([128, NH, 64])
            P.op(BCE, (lambda e: e.tensor_tensor(Xd3, X3, dt_b, ALU.mult)), reads=[Xtm, dtv], writes=[Xdt])
            P.op(BCE, (lambda e: e.tensor_tensor(Xe3, Xd3, de_b, ALU.mult)), reads=[Xdt, de], writes=[Xdte])
            for h in range(NH):
                P.op("pe", (lambda e, h=h: e.matmul(pA[h // 8][:, (h % 8) * 64:(h % 8 + 1) * 64], MT[:, h * 128:(h + 1) * 128],
                                                    Xdt[:, h * 64:(h + 1) * 64], start=True, stop=True)),
                     reads=[MT, Xdt], writes=[pA[h // 8]])
            for cc in range(2):
                P.op("act", (lambda e, cc=cc: e.copy(yd_sb[:, cc * 512:(cc + 1) * 512], pA[cc][:])), reads=[pA[cc]], writes=[yd_sb])

        def stage2(st, tt, par_i, xbcT, szT):
            Xtm, Btm, Xdt, Xdte, MT, yd_sb = Xtm2[par_i], Btm2[par_i], Xdt2[par_i], Xdte2[par_i], MT2[par_i], yd_sb2[par_i]
            dtv, acs_sb, ea, cdb, de = dtv2[par_i], acs_sb2[par_i], ea2[par_i], cdb2[par_i], de2[par_i]
            c0 = tt * 128
            t0 = st * ST
            CT = xbcT[:, 9 * ST + c0:9 * ST + c0 + 128]
            for cc in range(2):
                P.op("pe", (lambda e, cc=cc, CT=CT: e.matmul(pB[cc][:], CT, S_bf[:, cc * 512:(cc + 1) * 512], start=True, stop=True)),
                     reads=[xbcT, S_bf], writes=[pB[cc]])
            for h in range(NH):
                hs = slice(h * 64, (h + 1) * 64)
                ps_ = slice((h % 8) * 64, (h % 8 + 1) * 64)
                P.op("dve", (lambda e, h=h, hs=hs, ps_=ps_: e.scalar_tensor_tensor(yb[:, hs], pB[h // 8][:, ps_], ea[:, h:h + 1], yd_sb[:, hs],
                                                                               ALU.mult, ALU.add)), reads=[pB[h // 8], ea, yd_sb], writes=[yb])
            for h in range(NH):
                hs = slice(h * 64, (h + 1) * 64)
                P.op("dve", (lambda e, h=h, hs=hs: e.scalar_tensor_tensor(yb[:, hs], Xtm[:, hs], par[:, PDD + h:PDD + h + 1], yb[:, hs],
                                                                      ALU.mult, ALU.add)), reads=[Xtm, par, yb], writes=[yb])
            for cc in range(2):
                P.op("pe", (lambda e, cc=cc: e.matmul(pB[cc][:], Btm[:], Xdte[:, cc * 512:(cc + 1) * 512], start=True, stop=True)),
                     reads=[Btm, Xdte], writes=[pB[cc]])
            for h in range(NH):
                hs = slice(h * 64, (h + 1) * 64)
                ps_ = slice((h % 8) * 64, (h % 8 + 1) * 64)
                P.op("dve", (lambda e, h=h, hs=hs, ps_=ps_: e.scalar_tensor_tensor(S[:, hs], S[:, hs], cdb[:, h:h + 1], pB[h // 8][:, ps_],
                                                                               ALU.mult, ALU.add)), reads=[S, cdb, pB[h // 8]], writes=[S])
            P.op("act", lambda e: e.copy(S_bf[:], S[:]), reads=[S], writes=[S_bf])
            for j in range(8):
                P.op("pe", (lambda e, j=j, c0=c0: e.transpose(b1[:, j * 128:(j + 1) * 128], szT[:, j * ST + c0:j * ST + c0 + 128], idb[:])),
                     reads=[szT, idb], writes=[b1])
            P.op("dve", lambda e: e.tensor_tensor(yg[:], yb[:], b1[:], ALU.mult), reads=[yb, b1], writes=[yg])
            P.op("dve", lambda e: e.scalar_tensor_tensor(junk[:], yg[:], 1.0, yg[:], ALU.mult, ALU.mult, accum_out=ss[:, 0:1]),
                 reads=[yg], writes=[junk, ss])
            P.op("act", lambda e: e.activation(rsn[:], ss[:], AF.Ln, bias=epsb[:, 0:1], scale=1.0 / 1024), reads=[ss, epsb], writes=[rsn])
            P.op("act", lambda e: e.activation(rsn[:], rsn[:], AF.Exp, scale=-0.5), reads=[rsn], writes=[rsn])
            yo_ = yn[cnt["yn"] % 2]; cnt["yn"] += 1
            P.op("dve", (lambda e, yo_=yo_: e.scalar_tensor_tensor(yo_[:], yg[:], rsn[:, 0:1], par[:, PNG:PNG + 1024], ALU.mult, ALU.mult)),
                 reads=[yg, rsn, par], writes=[yo_])
            P.dma("pool", y_o[t0 + c0:t0 + c0 + 128, :], yo_[:], reads=[yo_], writes=[], owner=yo_)

        def capture(fn, *a):
            old = P.ops
            P.ops = []
            fn(*a)
            got = P.ops
            P.ops = old
            return got

        def merge(A, B):
            out = []
            ia = ib = 0
            while ia < len(A) or ib < len(B):
                if ib >= len(B) or (ia < len(A) and ia * len(B) <= ib * len(A)):
                    out.append(A[ia]); ia += 1
                else:
                    out.append(B[ib]); ib += 1
            return out

        for tt in range(4):
            ci = st * 4 + tt
            A = capture(stage1, st, tt, ci % 2, xbcT, szT)
            if pending:
                pst, ptt, ppar, pxb, psz = pending.pop()
                B = capture(stage2, pst, ptt, ppar, pxb, psz)
                P.ops.extend(merge(A, B))
            else:
                P.ops.extend(A)
            pending.append((st, tt, ci % 2, xbcT, szT))
    pst, ptt, ppar, pxb, psz = pending.pop()
    stage2(pst, ptt, ppar, pxb, psz)
    P.finish("sp")
    info = P.emit()
    return nc, info


NQT = 64
SCALE = 128 ** -0.5
NEG = -1.0e30
BIG = 30000.0


def build_attn(n_qt=NQT):
    nc = bass.Bass("TRN2", target_bir_lowering=False)
    P = Prog(nc)
    qT_d = P.dram("qT", [128, NQT, 512], BF16, kind="ExternalInput")
    kT_d = P.dram("kT", [128, 8192], BF16, kind="ExternalInput")
    v_d = P.dram("v", [128, 64, 128], BF16, kind="ExternalInput")
    tb_d = P.dram("tb", [128, 512], BF16, kind="ExternalInput")
    idb_d = P.dram("idb", [128, 128], BF16, kind="ExternalInput")
    en_d = P.dram("en", [32, 32 * 128], BF16, kind="ExternalInput")
    oT_o = P.dram("oT_o", [128, NQT, 512], BF16, kind="ExternalOutput")

    qT = P.sbuf([128, NQT * 512], BF16, "qT")
    kT = P.sbuf([128, 8192], BF16, "kT")
    vv = P.sbuf([128, 64 * 128], BF16, "vv")
    tb = P.sbuf([128, 512], BF16, "tb")
    idb = P.sbuf([128, 128], BF16, "idb")
    en = P.sbuf([32, 32 * 128], BF16, "en")
    ones = P.sbuf([128, 128], BF16, "ones")
    km = P.sbuf([128, 32], F32, "km")
    kmh = P.sbuf([128, 32], BF16, "kmh")
    kmhf = P.sbuf([128, 32], F32, "kmhf")
    kml = P.sbuf([128, 32], BF16, "kml")
    g_sb = P.sbuf([128, 128], F32, "g_sb")
    t8 = P.sbuf([128, 32], F32, "t8")
    selb = P.sbuf([128, 128], BF16, "selb")
    bias2 = [P.sbuf([32, 512], BF16, f"bias2_{i}") for i in range(2)]
    pT = [P.sbuf([128, 512], BF16, f"pT{i}") for i in range(4)]
    rec = P.sbuf([128, 512], F32, "rec")
    ob = [P.sbuf([128, 512], BF16, f"ob{i}") for i in range(2)]
    sTb = [P.psum([128, 512], F32, f"sT{i}") for i in range(3)]
    oTb = [P.psum([128, 512], F32, f"oT{i}") for i in range(2)]
    smb = [P.psum([128, 512], F32, f"sm{i}") for i in range(2)]
    gpb = P.psum([128, 1024], BF16, "gpb")
    gp = sTb[0]

    P.dma("sp", qT[:], qT_d[:].rearrange("p a b -> p (a b)"), reads=[qT_d], writes=[qT])
    P.dma("pool", kT[:], kT_d[:], reads=[kT_d], writes=[kT])
    P.dma("pool", vv[:], v_d[:].rearrange("p a b -> p (a b)"), reads=[v_d], writes=[vv])
    P.dma("pool", tb[:], tb_d[:], reads=[tb_d], writes=[tb])
    P.dma("pool", idb[:], idb_d[:], reads=[idb_d], writes=[idb])
    P.dma("pool", en[:], en_d[:], reads=[en_d], writes=[en])
    P.op("pool", lambda e: e.memset(ones[:], 1.0), writes=[ones])
    P.op("dve", lambda e: e.tensor_reduce(km[:], kT[:, :].rearrange("p (n k) -> p n k", k=256), AX.X, ALU.add), reads=[kT], writes=[km])
    P.op("dve", lambda e: e.tensor_scalar(km[:], km[:], 1.0 / 256, None, ALU.mult), reads=[km], writes=[km])
    P.op("dve", lambda e: e.tensor_copy(kmh[:], km[:]), reads=[km], writes=[kmh])
    P.op("dve", lambda e: e.tensor_copy(kmhf[:], kmh[:]), reads=[kmh], writes=[kmhf])
    P.op("dve", lambda e: e.tensor_tensor(kml[:], km[:], kmhf[:], ALU.subtract), reads=[km, kmhf], writes=[kml])

    LA = 2
    steps = []
    for qt in range(n_qt):
        blk = qt // 2
        half = qt % 2
        tiles = [(2 * n + i, n, False) for n in range(blk) for i in range(2)]
        if half == 0:
            tiles.append((2 * blk, None, True))
        else:
            tiles.append((2 * blk, None, False))
            tiles.append((2 * blk + 1, None, True))
        for ti, (kti, n, diag) in enumerate(tiles):
            steps.append((qt, ti, len(tiles), kti, n, diag))

    def prologue(qt):
        blk = qt // 2
        b2_ = bias2[qt % 2]
        if blk == 0:
            return
        for h in range(4):
            P.op("pe", (lambda e, h=h: e.matmul(gp[:, h * 32:h * 32 + blk], qT[:, qt * 512 + h * 128:qt * 512 + (h + 1) * 128],
                                                kmh[:, 0:blk], start=True, stop=False)), reads=[qT, kmh], writes=[gp])
            P.op("pe", (lambda e, h=h: e.matmul(gp[:, h * 32:h * 32 + blk], qT[:, qt * 512 + h * 128:qt * 512 + (h + 1) * 128],
                                                kml[:, 0:blk], start=False, stop=True)), reads=[qT, kml], writes=[gp])
        P.op("pool", lambda e: e.memset(g_sb[:], NEG), writes=[g_sb])
        g3 = g_sb[:, :].rearrange("p (h n) -> p h n", n=32)[:, :, 0:blk]
        p3 = gp[:, 0:128].rearrange("p (h n) -> p h n", n=32)[:, :, 0:blk]
        P.op("act", (lambda e: e.copy(g3, p3)), reads=[gp], writes=[g_sb])
        for h in range(4):
            P.op("dve", (lambda e, h=h: e.max(t8[:, h * 8:(h + 1) * 8], g_sb[:, h * 32:(h + 1) * 32])), reads=[g_sb], writes=[t8])
        for h in range(4):
            P.op("dve", (lambda e, h=h: e.tensor_scalar(selb[:, h * 32:(h + 1) * 32], g_sb[:, h * 32:(h + 1) * 32],
                                                     t8[:, h * 8 + 2:h * 8 + 3], 1.0, ALU.is_ge, ALU.subtract)), reads=[g_sb, t8], writes=[selb])
        for h in range(4):
            P.op("pe", (lambda e, h=h: e.transpose(gpb[0:32, h * 128:(h + 1) * 128], selb[:, h * 32:(h + 1) * 32], idb[:])),
                 reads=[selb, idb], writes=[gpb])
        P.op("act", (lambda e: e.copy(b2_[:], gpb[0:32, 0:512])), reads=[gpb], writes=[b2_])

    def front(i):
        qt, ti, nt, kti, n, diag = steps[i]
        if ti == 0:
            prologue(qt)
        sT = sTb[1 + i % 2]
        pt = pT[i % 4]
        q_rhs = qT[:, qt * 512:(qt + 1) * 512]
        b2_ = bias2[qt % 2]
        extra = (n is not None) or diag
        P.op("pe", (lambda e: e.matmul(sT[:], kT[:, kti * 128:(kti + 1) * 128], q_rhs, start=True, stop=not extra)),
             reads=[kT, qT], writes=[sT])
        if n is not None:
            P.op("pe", (lambda e: e.matmul(sT[:], en[0:32, n * 128:(n + 1) * 128], b2_[:], start=False, stop=True)),
                 reads=[en, b2_], writes=[sT])
        elif diag:
            P.op("pe", (lambda e: e.matmul(sT[:], idb[:], tb[:], start=False, stop=True)), reads=[idb, tb], writes=[sT])
        P.op("act", (lambda e: e.activation(pt[:], sT[:], AF.Exp, scale=SCALE)), reads=[sT], writes=[pt])

    def back(i):
        qt, ti, nt, kti, n, diag = steps[i]
        pt = pT[i % 4]
        oT = oTb[qt % 2]
        sm = smb[qt % 2]
        first = ti == 0
        last = ti == nt - 1
        P.op("pe", (lambda e: e.matmul(oT[:], vv[:, kti * 128:(kti + 1) * 128], pt[:], start=first, stop=last)), reads=[vv, pt], writes=[oT])
        P.op("pe", (lambda e: e.matmul(sm[:], ones[:], pt[:], start=first, stop=last)), reads=[ones, pt], writes=[sm])
        if last:
            P.op("dve", (lambda e: e.reciprocal(rec[:], sm[:])), reads=[sm], writes=[rec])
            o_ = ob[qt % 2]
            P.op("dve", (lambda e: e.tensor_tensor(o_[:], oT[:], rec[:], ALU.mult)), reads=[oT, rec], writes=[o_])
            P.dma("pool", oT_o[:, qt, :], o_[:], reads=[o_], writes=[], owner=o_)

    for idx in range(len(steps) + LA):
        if idx < len(steps):
            front(idx)
        if idx - LA >= 0:
            back(idx - LA)
    P.finish("sp")
    info = P.emit()
    return nc, info


CCH = 8192


def build_cast(M):
    nc = bass.Bass("TRN2", target_bir_lowering=False)
    P = Prog(nc)
    w_d = P.dram("w", [128, M], F32, kind="ExternalInput")
    wb_d = P.dram("wb", [128, M], BF16, kind="ExternalOutput")
    NB = 3
    fb = [P.sbuf([128, CCH], F32, f"fb{i}") for i in range(NB)]
    bb = [P.sbuf([128, CCH], BF16, f"bb{i}") for i in range(NB)]
    n = M // CCH
    engs = ("dve", "act", "pool")
    for i in range(n):
        f = fb[i % NB]
        b = bb[i % NB]
        P.dma("sp", f[:], w_d[:, i * CCH:(i + 1) * CCH], reads=[w_d], writes=[f])
        en = engs[i % 3]
        if en == "act":
            P.op("act", lambda e: e.copy(b[:], f[:]), reads=[f], writes=[b])
        else:
            P.op(en, lambda e: e.tensor_copy(b[:], f[:]), reads=[f], writes=[b])
        P.dma("pool", wb_d[:, i * CCH:(i + 1) * CCH], b[:], reads=[b], writes=[], owner=b)
    P.finish("sp")
    P.emit()
    return nc


_bf = ml_dtypes.bfloat16
NCORES = 8


def _run(nc, in_maps):
    res = run_bass_kernel_spmd(nc, in_maps, core_ids=list(range(NCORES)))
    return res.results


def _fm(a, kt):
    n = a.shape[0]
    return np.ascontiguousarray(a.T.reshape(kt, 128, n).transpose(1, 0, 2))


def _tok_par(g_ffn, g2, g3, cw, cb):
    par = np.zeros((128, 4 * 32 + 172 * 4), np.float32)
    par[:, 0:32] = g_ffn.reshape(32, 128).T
    par[:, 32:64] = g2.reshape(32, 128).T
    if g3 is not None:
        par[:, 64:96] = g3.reshape(32, 128).T
    par[:, 128:128 + 516] = cw.reshape(3, 172, 128).transpose(2, 1, 0).reshape(128, 516)
    par[:, 128 + 516:] = cb.reshape(172, 128).T
    return par


def _cs_table(base):
    inv = (10000.0 ** (-np.arange(64, dtype=np.float32) / 64)).astype(np.float32)
    tab = np.zeros((128, 9, 128), np.float32)
    for pi in range(3):
        for tt in range(3):
            p = (base + 342 * pi + 128 * tt + np.arange(128) - 2).astype(np.float32)
            a = p[:, None] * inv[None, :]
            tab[:, pi * 3 + tt] = np.concatenate([np.cos(a), np.sin(a)], 1)
    return tab


def kernel(x, norm_a, mamba_w_in, mamba_conv_w, mamba_conv_b, mamba_dt_bias, mamba_a_log,
           mamba_d, mamba_norm, mamba_w_out, kv_norm, w_kv, norm_b, w_q, w_o,
           ffn_norm, ffn_w_up, ffn_conv_w, ffn_conv_b, ffn_w_down, final_norm):
    f32 = np.float32
    x = np.asarray(x, f32)[0]
    w_in = np.asarray(mamba_w_in, f32)[0]

    shapes = []
    for g in range(8):
        shapes.append((f"wx{g}", (18, 128, 32, 128)))
        shapes.append((f"wdt{g}", (128, 512)))
    shapes += [("wpA", (32, 128, 64, 128)), ("wuA", (86, 128, 2, 32, 128)), ("wdA", (32, 128, 86, 128)), ("wqA", (12, 128, 32, 512)),
               ("wpB", (32, 128, 32, 128)), ("wuB", (86, 128, 2, 32, 128)), ("wdB", (32, 128, 86, 128))]
    offs = {}
    tot = 0
    for nm, sh in shapes:
        offs[nm] = (tot, sh)
        tot += int(np.prod(sh))
    per = -(-tot // (NCORES * 128 * CCH)) * CCH
    flat = np.zeros(NCORES * 128 * per, f32)

    def put(nm, arr):
        o, sh = offs[nm]
        flat[o:o + arr.size].reshape(sh)[...] = arr

    for g in range(8):
        wst = np.concatenate([w_in[:, g * 1024:(g + 1) * 1024], w_in[:, 8192 + g * 1024:8192 + (g + 1) * 1024],
                              w_in[:, 16384 + g * 128:16384 + (g + 1) * 128], w_in[:, 17408 + g * 128:17408 + (g + 1) * 128]], 1)
        put(f"wx{g}", wst.reshape(32, 128, 18, 128).transpose(2, 1, 0, 3))
        put(f"wdt{g}", w_in[:, 18432 + g * 16:18432 + (g + 1) * 16].reshape(32, 128, 16).transpose(1, 0, 2).reshape(128, 512))
    put("wpA", np.asarray(mamba_w_out, f32)[0].reshape(64, 128, 32, 128).transpose(2, 1, 0, 3))
    put("wuA", np.asarray(ffn_w_up, f32)[0].reshape(32, 128, 2, 86, 128).transpose(3, 1, 2, 0, 4))
    put("wdA", np.asarray(ffn_w_down, f32)[0].reshape(86, 128, 32, 128).transpose(2, 1, 0, 3))
    wqkv = np.concatenate([np.asarray(w_q, f32)[0], np.asarray(w_kv, f32)], 1)
    put("wqA", wqkv.reshape(32, 128, 12, 512).transpose(2, 1, 0, 3))
    put("wpB", np.asarray(w_o, f32)[0].reshape(32, 128, 32, 128).transpose(2, 1, 0, 3))
    put("wuB", np.asarray(ffn_w_up, f32)[1].reshape(32, 128, 2, 86, 128).transpose(3, 1, 2, 0, 4))
    put("wdB", np.asarray(ffn_w_down, f32)[1].reshape(86, 128, 32, 128).transpose(2, 1, 0, 3))
    del wqkv
    fl = flat.reshape(NCORES, 128, per)
    res = _run(build_cast(per), [{"w": fl[c]} for c in range(NCORES)])
    wbf = np.concatenate([r["wb"].reshape(-1) for r in res])
    del flat, fl, res

    def getw(nm):
        o, sh = offs[nm]
        return wbf[o:o + int(np.prod(sh))].reshape(sh)

    xT_l = _fm(x, 32)
    cst = np.zeros((128, 256), f32)
    kk = np.arange(128)
    cst[:, 0:128] = (kk[:, None] <= kk[None, :]).astype(f32)
    cst[:, 128:256] = np.where(kk[:, None] <= kk[None, :], 0.0, -30000.0)
    esel = np.zeros((16, 2048), f32)
    for h in range(16):
        esel[h, h * 128:(h + 1) * 128] = 1.0
    idb = np.eye(128, dtype=f32).astype(_bf)
    tri = cst[:, 0:128].astype(_bf)
    cwm = np.asarray(mamba_conv_w, f32)[0]
    cbm = np.asarray(mamba_conv_b, f32)[0]
    in_maps = []
    for g in range(8):
        par = np.zeros((128, NPAR), f32)
        par[:, PG:PG + 32] = np.asarray(norm_a, f32)[0].reshape(32, 128).T
        idx = np.concatenate([np.arange(g * 1024, (g + 1) * 1024), 8192 + np.arange(g * 128, (g + 1) * 128),
                              9216 + np.arange(g * 128, (g + 1) * 128)])
        par[:, PCW:PCW + 40] = cwm[:, idx].reshape(4, 10, 128).transpose(2, 1, 0).reshape(128, 40)
        par[:, PCB:PCB + 10] = cbm[idx].reshape(10, 128).T
        par[:, PDTB:PDTB + 16] = np.asarray(mamba_dt_bias, f32)[0][None, g * 16:(g + 1) * 16]
        par[:, PALOG:PALOG + 16] = np.asarray(mamba_a_log, f32)[0][None, g * 16:(g + 1) * 16]
        par[:, PDD:PDD + 16] = np.asarray(mamba_d, f32)[0][None, g * 16:(g + 1) * 16]
        par[:, PNG:PNG + 1024] = np.asarray(mamba_norm, f32)[0][None, g * 1024:(g + 1) * 1024]
        in_maps.append({"xT": xT_l, "wx": getw(f"wx{g}"), "wdt": getw(f"wdt{g}"), "par": par, "cst": cst, "esel": esel, "idb": idb})
    nc1, _ = build_mamba()
    res = _run(nc1, in_maps)
    Y = np.concatenate([r["y_o"] for r in res], 1)
    del in_maps, res, xT_l

    def tok_inputs(act_bf, h_f32, ktin, c):
        a = np.zeros((1026, act_bf.shape[1]), act_bf.dtype)
        hh = np.zeros((1026, 4096), f32)
        lo = 1024 * c - 2
        s0 = max(lo, 0)
        a[s0 - lo:] = act_bf[s0:1024 * c + 1024]
        hh[s0 - lo:] = h_f32[s0:1024 * c + 1024]
        return _fm(a, ktin), _fm(hh, 32)

    parA = _tok_par(np.asarray(ffn_norm, f32)[0], np.asarray(kv_norm, f32), np.asarray(norm_b, f32)[0],
                    np.asarray(ffn_conv_w, f32)[0], np.asarray(ffn_conv_b, f32)[0])
    in_maps = []
    for c in range(8):
        inT, hT = tok_inputs(Y, x, 64, c)
        in_maps.append({"inT": inT, "hT": hT, "wp": getw("wpA"), "wu": getw("wuA"), "wd": getw("wdA"), "wq": getw("wqA"),
                        "par": parA, "cs": _cs_table(1024 * c)})
    nc2, _ = build_tok("A", 64)
    res = _run(nc2, in_maps)
    h1 = np.concatenate([r["hoT"].transpose(2, 1, 0).reshape(1024, 4096) for r in res], 0)
    Q = np.concatenate([r["q_o"] for r in res], 0)
    KV = np.concatenate([r["kv_o"] for r in res], 0)
    del in_maps, res, Y

    tb = np.tile(np.where(kk[:, None] <= kk[None, :], 0.0, -BIG).astype(f32), (1, 4)).astype(_bf)
    en = np.zeros((32, 32 * 128), f32)
    for n_ in range(32):
        en[n_, n_ * 128:(n_ + 1) * 128] = BIG
    en = en.astype(_bf)
    in_maps = []
    for kv in range(8):
        q = Q[:, kv * 512:(kv + 1) * 512]
        qT = np.ascontiguousarray(q.reshape(64, 128, 4, 128).transpose(3, 0, 2, 1).reshape(128, 64, 512))
        kT = np.ascontiguousarray(KV[:, kv * 128:(kv + 1) * 128].T)
        vl = np.ascontiguousarray(KV[:, 1024 + kv * 128:1024 + (kv + 1) * 128].reshape(64, 128, 128).transpose(1, 0, 2))
        in_maps.append({"qT": qT, "kT": kT, "v": vl, "tb": tb, "idb": idb, "en": en})
    nc3, _ = build_attn()
    res = _run(nc3, in_maps)
    OT = np.stack([r["oT_o"].reshape(128, 64, 4, 128) for r in res], 0)
    OT = np.ascontiguousarray(OT.transpose(1, 0, 3, 2, 4).reshape(128, 32, 8192))
    del in_maps, res, Q, KV

    parB = _tok_par(np.asarray(ffn_norm, f32)[1], np.asarray(final_norm, f32), None,
                    np.asarray(ffn_conv_w, f32)[1], np.asarray(ffn_conv_b, f32)[1])
    in_maps = []
    for c in range(8):
        inT = np.zeros((128, 32, 1026), _bf)
        lo = 1024 * c - 2
        s0 = max(lo, 0)
        inT[:, :, s0 - lo:] = OT[:, :, s0:1024 * c + 1024]
        hh = np.zeros((1026, 4096), f32)
        hh[s0 - lo:] = h1[s0:1024 * c + 1024]
        hT = _fm(hh, 32)
        in_maps.append({"inT": inT, "hT": hT, "wp": getw("wpB"), "wu": getw("wuB"), "wd": getw("wdB"), "par": parB})
    nc4, _ = build_tok("B", 32)
    res = _run(nc4, in_maps)
    out = np.concatenate([r["hoT"].transpose(2, 1, 0).reshape(1024, 4096) for r in res], 0)
    return np.ascontiguousarray(out.astype(f32))[None]
```
